# Optimizing a Trainium2 kernel written in Bass

```python
import jax, jax.numpy as jnp
from jax import lax
import numpy as np

D_MODEL = 2048
BATCH = 4
SEQ = 2048
DEPTH = 1
DEC_BATCH = 32
DEC_SEQ = 1
PAST_LEN = 8192
PAGE_SIZE = 128

POOL_WIDTH = D_MODEL // 2
POOL_WINDOWS = (2, 4, 8, 16)
N_POOL_GROUPS = len(POOL_WINDOWS)
POOL_GROUP = POOL_WIDTH // N_POOL_GROUPS
POOL_STATE = max(POOL_WINDOWS) - 1
HEAD_DIM = 128
ATTN_WIDTH = D_MODEL - POOL_WIDTH
N_HEADS = ATTN_WIDTH // HEAD_DIM
DILATED = ((128, 1), (512, 4), (2048, 16))
MAX_WINDOW = max(w for w, _ in DILATED)
ROPE_DIM = HEAD_DIM // 4
ROPE_THETA = 500000.0
IN_WIDTH = POOL_WIDTH + 3 * ATTN_WIDTH
N_MEM = 256
MEM_HEADS = 4
MEM_HEAD_DIM = 128
MEM_WIDTH = MEM_HEADS * MEM_HEAD_DIM
D_FF = 4 * D_MODEL
EPS = 1e-6
BLOCK = 128
NEG_INF = -1e30

kernel_name = 'pool_dilated_hybrid_step'


def rmsnorm(x, g):
    xf = x.astype(jnp.float32)
    y = xf * lax.rsqrt(jnp.mean(xf * xf, axis=-1, keepdims=True) + EPS) * g.astype(jnp.float32)
    return y.astype(x.dtype)


def rope(x, pos):
    half = ROPE_DIM // 2
    inv = jnp.power(jnp.float32(ROPE_THETA), -jnp.arange(half, dtype=jnp.float32) * 2.0 / ROPE_DIM)
    ang = pos.astype(jnp.float32)[:, None] * inv[None, :]
    cos = jnp.cos(ang)[None, :, None, :]
    sin = jnp.sin(ang)[None, :, None, :]
    xf = x.astype(jnp.float32)
    x1, x2, rest = xf[..., :half], xf[..., half:ROPE_DIM], xf[..., ROPE_DIM:]
    out = jnp.concatenate([x1 * cos - x2 * sin, x2 * cos + x1 * sin, rest], axis=-1)
    return out.astype(x.dtype)


def pool_mix(u, prev, pos, w_pool, pool_scale):
    n, t, c = u.shape
    if prev is None:
        prev = jnp.zeros((n, POOL_STATE, c), u.dtype)
    ext = jnp.concatenate([prev, u], axis=1)
    cs = jnp.concatenate([jnp.zeros((n, 1, c), jnp.float32),
                          jnp.cumsum(ext.astype(jnp.float32), axis=1)], axis=1)
    p1 = POOL_STATE + 1
    outs = []
    for g, w in enumerate(POOL_WINDOWS):
        sl = slice(g * POOL_GROUP, (g + 1) * POOL_GROUP)
        wsum = cs[:, p1:p1 + t, sl] - cs[:, p1 - w:p1 - w + t, sl]
        cnt = jnp.minimum(w, pos + 1).astype(jnp.float32)[None, :, None]
        pooled = (wsum / cnt - u[..., sl].astype(jnp.float32)).astype(u.dtype)
        outs.append(jnp.einsum('ntc,ce->nte', pooled, w_pool[g]))
    out = jnp.concatenate(outs, axis=-1) * pool_scale
    return out, ext[:, -POOL_STATE:]


def band_attention(q, k, v, band):
    n, L, h, dh = q.shape
    nb = -(-L // BLOCK)
    pad = nb * BLOCK - L
    qb = jnp.pad(q, ((0, 0), (0, pad), (0, 0), (0, 0))).reshape(n, nb, BLOCK, h, dh)
    kb = jnp.pad(k, ((0, 0), (BLOCK, pad), (0, 0), (0, 0))).reshape(n, nb + 1, BLOCK, h, dh)
    vb = jnp.pad(v, ((0, 0), (BLOCK, pad), (0, 0), (0, 0))).reshape(n, nb + 1, BLOCK, h, dh)
    k2 = jnp.concatenate([kb[:, :-1], kb[:, 1:]], axis=2)
    v2 = jnp.concatenate([vb[:, :-1], vb[:, 1:]], axis=2)
    s = jnp.einsum('nbqhd,nbkhd->nbhqk', qb, k2, preferred_element_type=jnp.float32) * (dh ** -0.5)
    qi = jnp.arange(BLOCK)[:, None]
    kj = jnp.arange(2 * BLOCK)[None, :] - BLOCK
    dist = qi - kj
    key_pos = jnp.arange(nb)[:, None, None] * BLOCK + kj[None]
    mask = (dist >= 0)[None] & (dist <= band)[None] & (key_pos >= 0)
    s = jnp.where(mask[None, :, None], s, NEG_INF)
    m = jnp.max(s, axis=-1, keepdims=True)
    p = jnp.exp(s - m)
    den = jnp.sum(p, axis=-1, keepdims=True)
    o = jnp.einsum('nbhqk,nbkhd->nbqhd', p / den, v2.astype(jnp.float32))
    lse = (m + jnp.log(den))[..., 0].transpose(0, 1, 3, 2)
    return o.reshape(n, nb * BLOCK, h, dh)[:, :L], lse.reshape(n, nb * BLOCK, h)[:, :L]


def combine_branches(outs, lses):
    alpha = jax.nn.softmax(jnp.stack(lses, axis=0), axis=0)
    return jnp.einsum('gnth,gnthd->nthd', alpha, jnp.stack(outs, axis=0))


def dilated_attention_prompt(q, k, v):
    n, t, h, dh = q.shape
    outs, lses = [], []
    for w, d in DILATED:
        L = t // d
        def to_sub(a):
            return a.reshape(n, L, d, h, dh).transpose(0, 2, 1, 3, 4).reshape(n * d, L, h, dh)
        o, lse = band_attention(to_sub(q), to_sub(k), to_sub(v), w // d)
        outs.append(o.reshape(n, d, L, h, dh).transpose(0, 2, 1, 3, 4).reshape(n, t, h, dh))
        lses.append(lse.reshape(n, d, L, h).transpose(0, 2, 1, 3).reshape(n, t, h))
    return combine_branches(outs, lses)


def dilated_attention_sample(q, k, v, k_past, v_past):
    n, t, h, dh = q.shape
    win = k_past.shape[1]
    k_all = jnp.concatenate([k_past, k], axis=1)
    v_all = jnp.concatenate([v_past, v], axis=1)
    ti = jnp.arange(t)
    outs, lses = [], []
    for w, d in DILATED:
        offs = jnp.arange(w // d + 1) * d
        idx = win + ti[:, None] - offs[None, :]
        valid = idx >= 0
        idx = jnp.maximum(idx, 0)
        kg = k_all[:, idx]
        vg = v_all[:, idx]
        s = jnp.einsum('nthd,ntkhd->nthk', q, kg, preferred_element_type=jnp.float32) * (dh ** -0.5)
        s = jnp.where(valid[None, :, None, :], s, NEG_INF)
        m = jnp.max(s, axis=-1, keepdims=True)
        p = jnp.exp(s - m)
        den = jnp.sum(p, axis=-1, keepdims=True)
        outs.append(jnp.einsum('nthk,ntkhd->nthd', p / den, vg.astype(jnp.float32)))
        lses.append((m + jnp.log(den))[..., 0])
    return combine_branches(outs, lses)


def memory_kv(mem, g, w_kv):
    n, m, _ = mem.shape
    kv = jnp.einsum('nmd,de->nme', rmsnorm(mem, g), w_kv)
    k = kv[..., :MEM_WIDTH].reshape(n, m, MEM_HEADS, MEM_HEAD_DIM)
    v = kv[..., MEM_WIDTH:].reshape(n, m, MEM_HEADS, MEM_HEAD_DIM)
    return k, v


def memory_attention(h, mem_k, mem_v, w_xq, w_xo):
    n, t, _ = h.shape
    q = jnp.einsum('ntd,de->nte', h, w_xq).reshape(n, t, MEM_HEADS, MEM_HEAD_DIM)
    s = jnp.einsum('nthd,nmhd->nhtm', q, mem_k, preferred_element_type=jnp.float32) * (MEM_HEAD_DIM ** -0.5)
    p = jax.nn.softmax(s, axis=-1).astype(mem_v.dtype)
    o = jnp.einsum('nhtm,nmhd->nthd', p, mem_v).reshape(n, t, MEM_WIDTH)
    return jnp.einsum('nte,ed->ntd', o, w_xo)


def trunk_layer(x, pos, pool_prev, k_past, v_past, mem_k, mem_v, wl):
    (g_mix_pre, g_mix_post, g_mem_pre, g_mem_post, g_ffn_pre, g_ffn_post,
     w_in, w_pool, pool_scale, w_out, w_xq, w_xo, w_ff1, w_ff2) = wl
    n, t, _ = x.shape
    h = rmsnorm(x, g_mix_pre)
    proj = jnp.einsum('ntd,de->nte', h, w_in)
    u = proj[..., :POOL_WIDTH]
    q, k, v = [proj[..., POOL_WIDTH + i * ATTN_WIDTH:POOL_WIDTH + (i + 1) * ATTN_WIDTH].reshape(n, t, N_HEADS, HEAD_DIM)
               for i in range(3)]
    q = rope(q, pos)
    k = rope(k, pos)
    pool_out, pool_state = pool_mix(u, pool_prev, pos, w_pool, pool_scale)
    if k_past is None:
        attn = dilated_attention_prompt(q, k, v)
        keep = min(MAX_WINDOW, t)
        new_k, new_v = k[:, t - keep:], v[:, t - keep:]
    else:
        attn = dilated_attention_sample(q, k, v, k_past, v_past)
        new_k, new_v = k, v
    mixed = jnp.concatenate([pool_out, attn.astype(x.dtype).reshape(n, t, ATTN_WIDTH)], axis=-1)
    x = x + rmsnorm(jnp.einsum('nte,ed->ntd', mixed, w_out), g_mix_post)
    x = x + rmsnorm(memory_attention(rmsnorm(x, g_mem_pre), mem_k, mem_v, w_xq, w_xo), g_mem_post)
    hf = jnp.square(jax.nn.relu(jnp.einsum('ntd,df->ntf', rmsnorm(x, g_ffn_pre), w_ff1)))
    x = x + rmsnorm(jnp.einsum('ntf,fd->ntd', hf, w_ff2), g_ffn_post)
    return x, pool_state, new_k, new_v


def setup_inputs(seed: int = 0) -> dict:
    key = jax.random.key(seed)
    ks = jax.random.split(key, 32)
    f32 = jnp.float32

    def nrm(k, shape, fan_in):
        return jax.random.normal(k, shape, f32) * (fan_in ** -0.5)

    def gain(k):
        return 1.0 + 0.05 * jax.random.normal(k, (DEPTH, D_MODEL), f32)

    win = min(MAX_WINDOW, PAST_LEN)
    return {
        'x_prompt': jax.random.normal(ks[0], (BATCH, SEQ, D_MODEL), f32),
        'x_sample': jax.random.normal(ks[1], (DEC_BATCH, DEC_SEQ, D_MODEL), f32),
        'state_pool': jax.random.normal(ks[2], (DEPTH, DEC_BATCH, POOL_STATE, POOL_WIDTH), f32),
        'cache_attn_k': jax.random.normal(ks[3], (DEPTH, DEC_BATCH, win, N_HEADS, HEAD_DIM), f32),
        'cache_attn_v': jax.random.normal(ks[4], (DEPTH, DEC_BATCH, win, N_HEADS, HEAD_DIM), f32),
        'cache_mem_k': jax.random.normal(ks[5], (DEPTH, DEC_BATCH, N_MEM, MEM_HEADS, MEM_HEAD_DIM), f32),
        'cache_mem_v': jax.random.normal(ks[6], (DEPTH, DEC_BATCH, N_MEM, MEM_HEADS, MEM_HEAD_DIM), f32),
        'mem_prompt': jax.random.normal(ks[7], (BATCH, N_MEM, D_MODEL), f32),
        'g_mix_pre': gain(ks[8]),
        'g_mix_post': gain(ks[9]),
        'g_mem_pre': gain(ks[10]),
        'g_mem_post': gain(ks[11]),
        'g_ffn_pre': gain(ks[12]),
        'g_ffn_post': gain(ks[13]),
        'g_mem_kv': gain(ks[14]),
        'w_in': nrm(ks[15], (DEPTH, D_MODEL, IN_WIDTH), D_MODEL),
        'w_pool': nrm(ks[16], (DEPTH, N_POOL_GROUPS, POOL_GROUP, POOL_GROUP), POOL_GROUP),
        'pool_scale': 1.0 + 0.1 * jax.random.normal(ks[17], (DEPTH, POOL_WIDTH), f32),
        'w_out': nrm(ks[18], (DEPTH, D_MODEL, D_MODEL), D_MODEL),
        'w_xq': nrm(ks[19], (DEPTH, D_MODEL, MEM_WIDTH), D_MODEL),
        'w_mem_kv': nrm(ks[20], (DEPTH, D_MODEL, 2 * MEM_WIDTH), D_MODEL),
        'w_xo': nrm(ks[21], (DEPTH, MEM_WIDTH, D_MODEL), MEM_WIDTH),
        'w_ff1': nrm(ks[22], (DEPTH, D_MODEL, D_FF), D_MODEL),
        'w_ff2': nrm(ks[23], (DEPTH, D_FF, D_MODEL), D_FF),
    }


def reference(x_prompt, x_sample, state_pool, cache_attn_k, cache_attn_v, cache_mem_k, cache_mem_v,
              mem_prompt, g_mix_pre, g_mix_post, g_mem_pre, g_mem_post, g_ffn_pre, g_ffn_post, g_mem_kv,
              w_in, w_pool, pool_scale, w_out, w_xq, w_mem_kv, w_xo, w_ff1, w_ff2):
    pos_p = jnp.arange(x_prompt.shape[1], dtype=jnp.int32)
    pos_s = PAST_LEN + jnp.arange(x_sample.shape[1], dtype=jnp.int32)
    yp, ys = x_prompt, x_sample
    pool_p, pool_s, k_p, v_p, k_s, v_s, mk_p, mv_p = [], [], [], [], [], [], [], []
    for l in range(DEPTH):
        wl = (g_mix_pre[l], g_mix_post[l], g_mem_pre[l], g_mem_post[l], g_ffn_pre[l], g_ffn_post[l],
              w_in[l], w_pool[l], pool_scale[l], w_out[l], w_xq[l], w_xo[l], w_ff1[l], w_ff2[l])
        mk, mv = memory_kv(mem_prompt, g_mem_kv[l], w_mem_kv[l])
        yp, sp, kn, vn = trunk_layer(yp, pos_p, None, None, None, mk, mv, wl)
        ys, ss, ksn, vsn = trunk_layer(ys, pos_s, state_pool[l], cache_attn_k[l], cache_attn_v[l],
                                       cache_mem_k[l], cache_mem_v[l], wl)
        pool_p.append(sp)
        pool_s.append(ss)
        k_p.append(kn)
        v_p.append(vn)
        k_s.append(ksn)
        v_s.append(vsn)
        mk_p.append(mk)
        mv_p.append(mv)
    return (yp, ys, jnp.stack(pool_p), jnp.stack(pool_s), jnp.stack(k_p), jnp.stack(v_p),
            jnp.stack(k_s), jnp.stack(v_s), jnp.stack(mk_p), jnp.stack(mv_p))
```

```python
import numpy as np
import concourse.bass as bass
import concourse.mybir as mybir
from concourse.bass_utils import run_bass_kernel_spmd

F32 = mybir.dt.float32
BF16 = mybir.dt.bfloat16
AF = mybir.ActivationFunctionType
ALU = mybir.AluOpType
AX = mybir.AxisListType

D = 2048
NCORES = 8
NT = 1028
EPS = 1e-6
PAST = 8192
SBUF_BASE = 16640
SBUF_END = 229376
TOP_GUARD = 4096
GROUPS = [(0, 384), (384, 768), (768, 1028)]


class Res:
    __slots__ = ("name", "last_write", "reads")

    def __init__(self, name):
        self.name = name
        self.last_write = None
        self.reads = []


class Lane:
    def __init__(self, sem, name):
        self.sem = sem
        self.name = name
        self.count = 0


class Op:
    __slots__ = ("eng", "fn", "deps", "signals", "count", "lane", "lane_val", "is_dma", "idx")

    def __init__(self, eng, fn, is_dma=False, lane=None):
        self.eng = eng
        self.fn = fn
        self.deps = []
        self.signals = False
        self.count = None
        self.lane = lane
        self.lane_val = None
        self.is_dma = is_dma


class Sched:
    ENGS = ("pe", "act", "dve", "pool", "sp")

    def __init__(self, nc):
        self.nc = nc
        self.handles = {"pe": nc.tensor, "act": nc.scalar, "dve": nc.vector,
                        "pool": nc.gpsimd, "sp": nc.sync}
        self.ops = []
        self.sems = {}
        self._ctx = []

    def _new_sem(self, name):
        cm = self.nc.semaphore(name)
        h = cm.__enter__()
        self._ctx.append(cm)
        return h

    def lane(self, name):
        return Lane(self._new_sem("l_" + name), name)

    def op(self, eng, meth, kw=None, reads=(), writes=(), lane=None, is_dma=False, extra=()):
        fn = (lambda h, meth=meth, kw=dict(kw or {}): getattr(h, meth)(**kw))
        o = Op(eng, fn, is_dma=is_dma, lane=lane)
        deps = []
        for r in reads:
            if r.last_write is not None:
                deps.append(r.last_write)
        for w in writes:
            if w.last_write is not None:
                deps.append(w.last_write)
            deps.extend(w.reads)
        deps.extend(extra)
        seen = set()
        for d in deps:
            if d is o or id(d) in seen:
                continue
            seen.add(id(d))
            if (not d.is_dma) and (not is_dma) and d.eng == eng:
                if eng in ("pe", "sp"):
                    continue
            o.deps.append(d)
            d.signals = True
        for r in reads:
            r.reads.append(o)
        for w in writes:
            w.last_write = o
            w.reads = []
        if is_dma:
            lane.count += 16
            o.lane_val = lane.count
        self.ops.append(o)
        return o

    def dma(self, queue, out, in_, reads=(), writes=(), lane=None, extra=()):
        return self.op(queue, "dma_start", dict(out=out, in_=in_), reads=reads,
                       writes=writes, lane=lane, is_dma=True, extra=extra)

    def wait_all(self, eng, ops):
        o = Op(eng, None)
        for d in ops:
            o.deps.append(d)
            d.signals = True
        self.ops.append(o)

    def emit(self):
        for e in self.ENGS:
            self.sems[e] = self._new_sem("s_" + e)
        cnt = {e: 0 for e in self.ENGS}
        for o in self.ops:
            if (not o.is_dma) and o.signals and o.fn is not None:
                cnt[o.eng] += 1
                o.count = cnt[o.eng]
        waited = {e: {} for e in self.ENGS}
        for o in self.ops:
            h = self.handles[o.eng]
            w = waited[o.eng]
            need = {}
            for d in o.deps:
                if d.is_dma:
                    key, sem, val = ("l", id(d.lane)), d.lane.sem, d.lane_val
                else:
                    key, sem, val = ("e", d.eng), self.sems[d.eng], d.count
                if key not in need or need[key][1] < val:
                    need[key] = (sem, val)
            for key, (sem, val) in need.items():
                if w.get(key, 0) >= val:
                    continue
                h.wait_ge(sem, val)
                w[key] = val
            if o.fn is None:
                continue
            inst = o.fn(h)
            if o.is_dma:
                inst.then_inc(o.lane.sem, 16)
            elif o.signals:
                inst.then_inc(self.sems[o.eng], 1)

    def close(self):
        for cm in reversed(self._ctx):
            cm.__exit__(None, None, None)


def _rope_tab(pos):
    half = 16
    inv = np.power(np.float32(500000.0), -np.arange(half, dtype=np.float32) * np.float32(2.0) / np.float32(32.0)).astype(np.float32)
    ang = pos.astype(np.float32)[:, None] * inv[None, :]
    c = np.cos(ang).astype(np.float32)
    s = np.sin(ang).astype(np.float32)
    return np.concatenate([c, c, s, s], axis=1)


CB = {}
_o = 0
for _n, _w in [("ident", 128), ("ones", 128), ("validp", 128), ("valid16", 128), ("mu_ml", 256),
               ("m16", 64), ("ohb", 512), ("bmask", 1024)]:
    CB[_n] = (_o, _w)
    _o += _w
CB_W = _o
CF = {}
_o = 0
for _n, _w in [("gcols", 64), ("pscale", 8), ("invcnt", 64), ("nhalf", 1), ("coefm", 16), ("udiag", 16),
               ("oh4", 4), ("ln3", 1), ("zero", 1)]:
    CF[_n] = (_o, _w)
    _o += _w
CF_W = _o


def _consts(half):
    cb = np.zeros((128, CB_W), np.float32)
    k = np.arange(128)[:, None]
    q = np.arange(128)[None, :]
    o, w = CB["ident"]; cb[:, o:o + w] = (k == q)
    o, w = CB["ones"]; cb[:, o:o + w] = 1.0
    o, w = CB["validp"]; cb[:, o:o + w] = float(half)
    o, w = CB["valid16"]; cb[:, o:o + w] = 1.0; cb[:64, o:o + w] = float(half)
    o, w = CB["mu_ml"]; cb[:, o:o + 128] = (k >= q); cb[:, o + 128:o + 256] = (k <= q)
    o, w = CB["m16"]
    for hf in range(2):
        qq = np.arange(32)[None, :]
        cb[:, o + hf * 32:o + hf * 32 + 32] = (k <= 64 + 32 * hf + qq)
    o, w = CB["ohb"]
    for s in range(4):
        cb[s, o + s * 128:o + (s + 1) * 128] = 1.0
    o, w = CB["bmask"]
    for h in range(8):
        cb[h, o + h * 128:o + (h + 1) * 128] = 1.0
    cf = np.zeros((128, CF_W), np.float32)
    o, w = CF["nhalf"]; cf[:, o] = -0.5
    o, w = CF["ln3"]; cf[:, o] = np.log(3.0)
    o, w = CF["invcnt"]
    for g, win in enumerate((2, 4, 8, 16)):
        for t in range(16):
            cnt = min(win, half * 1024 + t + 1)
            cf[:, o + g * 16 + t] = 1.0 / cnt
    o, w = CF["coefm"]
    for g, win in enumerate((2, 4, 8, 16)):
        for s in range(4):
            for j in range(15):
                if j >= 16 - win:
                    cf[s * 15 + j, o + g * 4 + s] = 1.0 / win
    o, w = CF["udiag"]
    for g, win in enumerate((2, 4, 8, 16)):
        for s in range(4):
            cf[s, o + g * 4 + s] = 1.0 / win - 1.0
    o, w = CF["oh4"]
    for s in range(4):
        cf[s, o + s] = 1.0
    return cb, cf


def _piece_cols(w, ncols=512):
    K, N = w.shape
    c = K // 128
    a = w.reshape(c, 128, N // ncols, ncols).transpose(2, 1, 0, 3)
    return np.ascontiguousarray(a).reshape(N // ncols, 128, c * ncols)


def _piece_rows(w, nrows=512):
    K, N = w.shape
    cc = nrows // 128
    a = w.reshape(K // nrows, cc, 128, N).transpose(0, 2, 1, 3)
    return np.ascontiguousarray(a).reshape(K // nrows, 128, cc * N)


def _prep(inp):
    f = lambda a: np.ascontiguousarray(np.asarray(a, dtype=np.float32))
    shared = {}
    shared["w_in_l"] = _piece_cols(f(inp["w_in"])[0])
    shared["w_out_l"] = _piece_cols(f(inp["w_out"])[0])
    shared["w_mkv_l"] = _piece_cols(f(inp["w_mem_kv"])[0])
    shared["w_xq_l"] = _piece_cols(f(inp["w_xq"])[0])
    shared["w_xo_l"] = _piece_rows(f(inp["w_xo"])[0])
    shared["w_ff1_l"] = _piece_cols(f(inp["w_ff1"])[0])
    shared["w_ff2_l"] = _piece_rows(f(inp["w_ff2"])[0])
    wp = f(inp["w_pool"])[0]
    shared["w_pool_l"] = np.ascontiguousarray(wp.reshape(4, 2, 128, 256).transpose(2, 0, 1, 3)).reshape(128, 2048)
    shared["grows"] = np.stack([f(inp["g_mix_post"])[0], f(inp["g_mem_post"])[0], f(inp["g_ffn_post"])[0]])
    gc = np.concatenate([f(inp[n])[0].reshape(16, 128).T for n in ("g_mix_pre", "g_mem_pre", "g_ffn_pre", "g_mem_kv")], axis=1)
    psc = f(inp["pool_scale"])[0].reshape(8, 128).T
    xp_all = f(inp["x_prompt"])
    xs_all = f(inp["x_sample"])
    maps = []
    for core in range(NCORES):
        b, half = core // 2, core % 2
        m = dict(shared)
        m["xm"] = np.ascontiguousarray(xp_all[b, half * 1024:(half + 1) * 1024])
        m["xp"] = np.ascontiguousarray(xp_all[b, 0:1024]) if half == 1 else np.zeros((1024, D), np.float32)
        m["xs"] = np.ascontiguousarray(xs_all[core * 4:(core + 1) * 4, 0])
        m["memx"] = np.ascontiguousarray(f(inp["mem_prompt"])[b])
        m["cache_k"] = np.ascontiguousarray(f(inp["cache_attn_k"])[0, core * 4:(core + 1) * 4].reshape(4, 2048, 1024))
        m["cache_v"] = np.ascontiguousarray(f(inp["cache_attn_v"])[0, core * 4:(core + 1) * 4].reshape(4, 2048, 1024))
        m["cmem_k"] = np.ascontiguousarray(f(inp["cache_mem_k"])[0, core * 4:(core + 1) * 4].reshape(4, 256, 512))
        m["cmem_v"] = np.ascontiguousarray(f(inp["cache_mem_v"])[0, core * 4:(core + 1) * 4].reshape(4, 256, 512))
        m["spool"] = np.ascontiguousarray(f(inp["state_pool"])[0, core * 4:(core + 1) * 4].reshape(60, 1024))
        cb, cf = _consts(half)
        o, w = CF["gcols"]; cf[:, o:o + w] = gc
        o, w = CF["pscale"]; cf[:, o:o + w] = psc
        m["cbf"] = cb
        m["cf32"] = cf
        pos_m = half * 1024 + np.arange(1024)
        csm = np.zeros((9 * 128, 64), np.float32)
        csm[:1024] = _rope_tab(pos_m)
        csm[1024:1028] = _rope_tab(np.full(4, PAST))
        m["cs_main"] = csm
        m["cs_prev"] = _rope_tab(np.arange(1024))
        maps.append(m)
    return maps


class Builder:
    def __init__(self, upto="all", debug=()):
        self.upto = upto
        self.debug = set(debug)
        self.nc = nc = bass.Bass("TRN2", target_bir_lowering=False)
        self.S = Sched(nc)
        self.outs = []
        di = lambda n, s: nc.dram_tensor(n, list(s), F32, kind="ExternalInput").ap()
        do = lambda n, s: nc.dram_tensor(n, list(s), F32, kind="ExternalOutput").ap()
        self.xm = di("xm", (1024, D)); self.xp = di("xp", (1024, D)); self.xs = di("xs", (4, D))
        self.memx = di("memx", (256, D))
        self.w_in_l = di("w_in_l", (8, 128, 8192)); self.w_out_l = di("w_out_l", (4, 128, 8192))
        self.w_mkv_l = di("w_mkv_l", (2, 128, 8192)); self.w_xq_l = di("w_xq_l", (1, 128, 8192))
        self.w_xo_l = di("w_xo_l", (1, 128, 8192)); self.w_ff1_l = di("w_ff1_l", (16, 128, 8192))
        self.w_ff2_l = di("w_ff2_l", (16, 128, 8192)); self.w_pool_l = di("w_pool_l", (128, 2048))
        self.grows = di("grows", (3, D))
        self.cache_k = di("cache_k", (4, 2048, 1024)); self.cache_v = di("cache_v", (4, 2048, 1024))
        self.cmem_k = di("cmem_k", (4, 256, 512)); self.cmem_v = di("cmem_v", (4, 256, 512))
        self.spool = di("spool", (60, 1024))
        self.cbf = di("cbf", (128, CB_W)); self.cf32 = di("cf32", (128, CF_W))
        self.cs_main = di("cs_main", (9 * 128, 64)); self.cs_prev = di("cs_prev", (1024, 64))
        self.o_y = do("o_y", (1024, D)); self.o_ys = do("o_ys", (4, D))
        self.o_pool = do("o_pool", (15, 1024)); self.o_pools = do("o_pools", (4, 15, 1024))
        self.o_k = do("o_k", (1024, 1024)); self.o_v = do("o_v", (1024, 1024))
        self.o_ks = do("o_ks", (4, 1024)); self.o_vs = do("o_vs", (4, 1024))
        self.o_mk = do("o_mk", (256, 512)); self.o_mv = do("o_mv", (256, 512))
        self.vs_d = nc.dram_tensor("vs_scr", [2048, 1024], BF16).ap()
        self.x2_d = nc.dram_tensor("x2_scr", [9 * 128, D], F32).ap()
        self.banks = [nc.alloc_psum_tensor("bank%d" % i, [128, 512], F32) for i in range(8)]
        self.bres = [Res("bank%d" % i) for i in range(8)]
        self._names = 0

    def sb(self, name, shape, dt, off):
        nbytes = int(np.prod(shape[1:])) * (4 if dt == F32 else 2)
        assert off % 32 == 0, (name, off)
        assert SBUF_BASE <= off and off + nbytes <= SBUF_END, (name, off, nbytes)
        self._names += 1
        t = self.nc.alloc_sbuf_tensor_at("%s_%d" % (name, self._names), list(shape), dt, offset=off)
        return t

    def bank_bf(self, i):
        return self.banks[i][:].bitcast(BF16)

    def build(self):
        S = self.S
        nc = self.nc
        C0 = SBUF_BASE
        cbt = self.sb("cbt", [128, CB_W], BF16, C0)
        cft = self.sb("cft", [128, CF_W], F32, C0 + 4736)
        smalls = self.sb("smalls", [128, 128], F32, C0 + 4736 + 704)
        cs_t = self.sb("cs_t", [128, 2, 64], F32, C0 + 6144)
        gb = self.sb("gb", [128, D], F32, C0 + 6656)
        R0 = C0 + 6656 + 8192 + 128
        R0 = (R0 + 31) // 32 * 32
        self.cbt, self.cft, self.smalls, self.cs_t, self.gb = cbt, cft, smalls, cs_t, gb
        self.r_const = Res("const"); self.r_gb = Res("gb")
        self.cb = lambda n: cbt[:, CB[n][0]:CB[n][0] + CB[n][1]]
        self.cf = lambda n: cft[:, CF[n][0]:CF[n][0] + CF[n][1]]
        self.WR = [self.sb("wr%d" % i, [128, 8192], BF16, R0 + i * 16384) for i in range(4)]
        self.wres = [Res("wr%d" % i) for i in range(4)]
        self.wlane = [S.lane("wr%d" % i) for i in range(4)]
        A0 = R0 + 4 * 16384
        self.A0 = A0
        self.R0 = R0

        l_const = S.lane("const")
        l_const2 = S.lane("const2")
        self.r_constb = Res("constb")
        c1 = S.dma("pool", cbt[:], self.cbf, writes=[self.r_constb], lane=l_const)
        S.dma("sp", cft[:], self.cf32, writes=[self.r_const], lane=l_const2)
        j = S.op("sp", "nop", {}, reads=[self.r_constb], writes=[self.r_const])
        self.phase_kv()
        if self.upto == "kv":
            return self.finish()
        self.phase_qu()
        if self.upto in ("u", "qu"):
            return self.finish()
        self.phase_mkv()
        if self.upto == "mkv":
            return self.finish()
        self.phase_att()
        if self.upto == "att":
            self.dbg("mixed", [128, 16, NT], self.mixed[:], self.r_mixed)
            return self.finish()
        self.phase_satt()
        self.dbg("mixed", [128, 16, NT], self.mixed[:], self.r_mixed)
        if self.upto == "satt":
            return self.finish()
        self.alloc_epi()
        self.realias([self.wres[2]] , [self.r_junk, self.r_hb1, self.r_hb2])
        self.phase_wo()
        self.phase_epi1()
        if self.upto == "wo":
            self.dbg("x1", [128, 9, D], self.x1[:], self.r_x1)
            return self.finish()
        self.phase_xa()
        self.dbg("o2T", [128, 4, NT], self.o2T[:], self.r_o2T)
        if self.upto == "xa":
            return self.finish()
        self.phase_wxo()
        if self.upto == "wxo":
            self.dbg("x2", [128, 9, D], self.x1[:], self.r_x1)
            return self.finish()
        self.phase_ffn()
        return self.finish()

    def finish(self):
        S = self.S
        S.wait_all("sp", self.outs)
        S.emit()
        S.close()
        return self.nc

    def load_piece(self, slot, src):
        return self.S.dma("pool", self.WR[slot][:], src, writes=[self.wres[slot]], lane=self.wlane[slot])

    def prenorm_tile(self, xin, r_xin, hb, r_hb, rows, gidx, dst, r_dst, tb, ss_col):
        self.prenorm_a(xin, r_xin, hb, r_hb, rows, ss_col)
        self.prenorm_b(hb, r_hb, rows, gidx, dst, r_dst, tb)

    def prenorm_a(self, xin, r_xin, hb, r_hb, rows, ss_col):
        S = self.S
        junk = self.junk
        ss = self.smalls[:, ss_col:ss_col + 1]
        rs = self.smalls[:, ss_col + 1:ss_col + 2]
        r_ss = self.r_small[ss_col // 2]
        S.op("act", "activation", dict(out=junk[:rows, :], in_=xin[:rows, :], func=AF.Square, accum_out=ss[:rows, :]),
             reads=[r_xin], writes=[self.r_junk, r_ss])
        S.op("pool", "tensor_scalar", dict(out=rs[:rows, :], in0=ss[:rows, :], scalar1=float(D * EPS), scalar2=None, op0=ALU.add),
             reads=[r_ss], writes=[r_ss])
        nh = self.cf("nhalf")
        S.op("pool", "tensor_tensor", dict(out=rs[:rows, :], in0=rs[:rows, :], in1=nh[:rows, :], op=ALU.pow),
             reads=[r_ss, self.r_const], writes=[r_ss])
        S.op("dve", "tensor_scalar", dict(out=hb[:rows, :], in0=xin[:rows, :], scalar1=rs[:rows, 0:1], scalar2=float(np.sqrt(D)),
                                          op0=ALU.mult, op1=ALU.mult),
             reads=[r_xin, r_ss], writes=[r_hb])

    def prenorm_b(self, hb, r_hb, rows, gidx, dst, r_dst, tb):
        S = self.S
        ident = self.cb("ident")
        gcol = self.cf("gcols")
        for k in range(2):
            bk = self.bank_bf(tb[k])
            for c8 in range(8):
                c = k * 8 + c8
                S.op("pe", "transpose", dict(out=bk[:, c8 * 128:c8 * 128 + rows], in_=hb[:rows, c * 128:(c + 1) * 128],
                                             identity=ident[:rows, :rows]),
                     reads=[r_hb, self.r_const], writes=[self.bres[tb[k]]])
            g_ap = gcol[:, gidx * 16 + k * 8:gidx * 16 + k * 8 + 8]
            S.op("dve", "tensor_tensor", dict(
                out=dst[:, k * 8:(k + 1) * 8, 0:rows],
                in0=bk.rearrange("p (c t) -> p c t", t=128)[:, :, 0:rows],
                in1=g_ap.unsqueeze(2).to_broadcast([128, 8, rows]), op=ALU.mult),
                 reads=[self.bres[tb[k]], self.r_const], writes=[r_dst])

    def setup_common(self):
        S = self.S
        e = SBUF_END - TOP_GUARD
        e -= 32768; self.kT = self.sb("kT", [128, 8, 2048], BF16, e)
        e -= 32896; self.hT = self.sb("hT", [128, 16, NT], BF16, e)
        e -= 512; self.hTp15 = self.sb("hTp15", [128, 16, 16], BF16, e)
        e -= 2048; self.ksb = self.sb("ksb", [4, 1024], BF16, e)
        e -= 2048; self.vsb = self.sb("vsb", [4, 1024], BF16, e)
        e -= 2048; self.qsb = self.sb("qsb", [4, 1024], BF16, e)
        e -= 4096; self.usf = self.sb("usf", [4, 1024], F32, e)
        self.PERS0 = e
        self.r_small = [Res("small%d" % i) for i in range(40)]
        self.r_kT = [Res("kT%d" % i) for i in range(16)]
        self.r_hT = [Res("hT%d" % i) for i in range(9)]
        self.r_hTp15 = Res("hTp15")
        self.r_ksb = Res("ksb"); self.r_vsb = Res("vsb"); self.r_qsb = Res("qsb"); self.r_usf = Res("usf")
        self.r_junk = Res("junk")
        self.r_cs = [Res("cs0"), Res("cs1")]
        self.l_cs = [S.lane("cs0"), S.lane("cs1")]
        self.r_vs = [Res("vs_t%d" % i) for i in range(16)]
        self.l_vs = [S.lane("vs0"), S.lane("vs1")]
        self.l_st = [S.lane("st%d" % i) for i in range(4)]
        self.r_stage = [Res("st%d" % i) for i in range(4)]
        self.r_kb = [Res("kb0"), Res("kb1")]
        self.r_rtmp = [Res("rt0"), Res("rt1")]
        self.nst = 0
        self.npiece = 0

    def alloc_stage(self, o):
        self.stage = [self.sb("stage%d" % i, [128, 512], F32, o + i * 2048) for i in range(4)]; o += 8192
        self.kb = [self.sb("kb%d" % i, [128, 512], BF16, o + i * 1024) for i in range(2)]; o += 2048
        self.rtmp = [self.sb("rtmp%d" % i, [128, 4, 64], F32, o + i * 1024) for i in range(2)]; o += 2048
        return o

    def tm_proj(self, kind, pslot, lhs, r_lhs, rows, cs_sl, tile_kind, t, half_idx, cs_ap=None, r_csx=None):
        S = self.S
        nst = self.nst
        bank = 2 + (nst % 4)
        ps = self.banks[bank]
        for c in range(16):
            S.op("pe", "matmul", dict(out=ps[:rows, :], lhsT=lhs[:, c, 0:rows], rhs=self.WR[pslot][:, c * 512:(c + 1) * 512],
                                      start=(c == 0), stop=(c == 15)),
                 reads=[r_lhs, self.wres[pslot]], writes=[self.bres[bank]])
        self.flush_pending()
        new_pending = None
        st = nst % 4
        stg = self.stage[st]
        r_st = self.r_stage[st]
        S.op("act", "activation", dict(out=stg[:rows, :], in_=ps[:rows, :], func=AF.Copy),
             reads=[self.bres[bank]], writes=[r_st])
        if kind in ("k", "q"):
            rt = self.rtmp[nst % 2]
            r_rt = self.r_rtmp[nst % 2]
            sv = stg[:rows, :].rearrange("p (h d) -> p h d", d=128)
            if cs_ap is None:
                cs_ap = self.cs_t[:rows, cs_sl, :]
                r_csx = self.r_cs[cs_sl]
            cc = cs_ap[:, 0:32].unsqueeze(1).to_broadcast([rows, 4, 32])
            ss_ = cs_ap[:, 32:64].unsqueeze(1).to_broadcast([rows, 4, 32])
            rr = [r_st, r_csx]
            S.op("dve", "tensor_tensor", dict(out=rt[:rows, :, 0:32], in0=sv[:, :, 0:32], in1=cc, op=ALU.mult), reads=rr, writes=[r_rt])
            S.op("dve", "tensor_tensor", dict(out=rt[:rows, :, 32:64], in0=sv[:, :, 0:32], in1=ss_, op=ALU.mult), reads=rr, writes=[r_rt])
            S.op("dve", "tensor_tensor", dict(out=sv[:, :, 0:16], in0=rt[:rows, :, 0:16], in1=rt[:rows, :, 48:64], op=ALU.subtract),
                 reads=[r_rt], writes=[r_st])
            S.op("dve", "tensor_tensor", dict(out=sv[:, :, 16:32], in0=rt[:rows, :, 16:32], in1=rt[:rows, :, 32:48], op=ALU.add),
                 reads=[r_rt], writes=[r_st])
        cbs = slice(half_idx * 512, half_idx * 512 + 512)
        lane = self.l_st[st]
        if kind == "u":
            if tile_kind == "samp":
                S.op("dve", "tensor_copy", dict(out=self.usf[:, cbs], in_=stg[:4, :]), reads=[r_st], writes=[self.r_usf])
                self.outs.append(S.dma("sp", self.o_pools[:, 14, cbs], stg[:4, :], reads=[r_st], lane=lane))
            else:
                self.outs.append(S.dma("sp", self.o_pool[:, cbs], stg[113:128, :], reads=[r_st], lane=lane))
        elif tile_kind == "samp":
            dst_t, r_t, od = {"k": (self.ksb, self.r_ksb, self.o_ks), "v": (self.vsb, self.r_vsb, self.o_vs),
                              "q": (self.qsb, self.r_qsb, None)}[kind]
            S.op("dve", "tensor_copy", dict(out=dst_t[:, cbs], in_=stg[:4, :]), reads=[r_st], writes=[r_t])
            if od is not None:
                self.outs.append(S.dma("sp", od[:, cbs], stg[:4, :], reads=[r_st], lane=lane))
        else:
            kbs = self.kb[nst % 2]
            r_kbs = self.r_kb[nst % 2]
            S.op("dve", "tensor_copy", dict(out=kbs[:, :], in_=stg[:, :]), reads=[r_st], writes=[r_kbs])
            if tile_kind == "main" and kind in ("k", "v"):
                od = (self.o_k if kind == "k" else self.o_v)[t * 128:(t + 1) * 128, cbs]
                self.outs.append(S.dma("sp", od, stg[:, :], reads=[r_st], lane=lane))
            ext_tile = t if tile_kind == "prev" else 8 + t
            if kind in ("k", "q"):
                tb = 6 + (nst % 2)
                tbk = self.bank_bf(tb)
                if kind == "k":
                    dst = self.kT[:, half_idx * 4:(half_idx + 1) * 4, ext_tile * 128:(ext_tile + 1) * 128]
                    r_d = self.r_kT[ext_tile]
                else:
                    dst = self.qT[:, half_idx * 4:(half_idx + 1) * 4, t * 128:(t + 1) * 128]
                    r_d = self.r_qT[t]

                def deferred(tb=tb, tbk=tbk, kbs=kbs, r_kbs=r_kbs, dst=dst, r_d=r_d):
                    for hh in range(4):
                        S.op("pe", "transpose", dict(out=tbk[:, hh * 128:(hh + 1) * 128], in_=kbs[:, hh * 128:(hh + 1) * 128], identity=self.cb("ident")),
                             reads=[r_kbs, self.r_const], writes=[self.bres[tb]])
                    S.op("act", "activation", dict(out=dst, in_=tbk[:, 0:512].rearrange("p (h t) -> p h t", t=128), func=AF.Copy),
                         reads=[self.bres[tb]], writes=[r_d])
                new_pending = deferred
            else:
                vd = self.vs_d[ext_tile * 128:(ext_tile + 1) * 128, cbs]
                S.dma("sp", vd, kbs[:, :], reads=[r_kbs], writes=[self.r_vs[ext_tile]], lane=self.l_vs[nst % 2])
        self.pending = new_pending
        self.nst += 1

    def flush_pending(self):
        if getattr(self, "pending", None) is not None:
            p = self.pending
            self.pending = None
            p()

    def load_cs(self, sl, tile_kind, t, rows):
        if tile_kind == "prev":
            cs_src = self.cs_prev[t * 128:(t + 1) * 128, :]
        elif tile_kind == "main":
            cs_src = self.cs_main[t * 128:(t + 1) * 128, :]
        else:
            cs_src = self.cs_main[1024:1028, :]
        self.S.dma("sp", self.cs_t[:rows, sl, :], cs_src, writes=[self.r_cs[sl]], lane=self.l_cs[sl])

    def phase_kv(self):
        S = self.S
        self.setup_common()
        o = self.A0
        xin = [self.sb("xin%d" % i, [128, D], F32, o + i * 8192) for i in range(2)]; o += 16384
        hb = [self.sb("hb%d" % i, [128, D], BF16, o + i * 4096) for i in range(2)]; o += 8192
        self.junk = self.sb("junk", [128, D], BF16, o); o += 4096
        hTt = [self.sb("hTt%d" % i, [128, 16, 128], BF16, o + i * 4096) for i in range(2)]; o += 8192
        o = self.alloc_stage(o)
        assert o <= self.PERS0, (o, self.PERS0)
        r_xin = [Res("xin0"), Res("xin1")]; r_hb = [Res("hb0"), Res("hb1")]
        r_hTt = [Res("hTt0"), Res("hTt1")]
        l_xin = [S.lane("xin0"), S.lane("xin1")]

        for i, pj in enumerate((4, 5, 6, 7)):
            self.load_piece(i, self.w_in_l[pj])
        self.npiece = 4

        tiles = [("prev", t, 128) for t in range(8)] + [("main", t, 128) for t in range(8)] + [("samp", 8, 4)]

        def dst_of(ti):
            kind, t, rows = tiles[ti]
            if kind == "prev":
                return hTt[ti % 2], r_hTt[ti % 2]
            return self.hT[:, :, t * 128:t * 128 + rows], self.r_hT[t]

        def issue_x(ti):
            kind, t, rows = tiles[ti]
            src = {"prev": self.xp, "main": self.xm}.get(kind)
            src = self.xs if kind == "samp" else src[t * 128:(t + 1) * 128, :]
            S.dma("sp", xin[ti % 2][:rows, :], src, writes=[r_xin[ti % 2]], lane=l_xin[ti % 2])

        def pa(ti):
            kind, t, rows = tiles[ti]
            self.prenorm_a(xin[ti % 2], r_xin[ti % 2], hb[ti % 2], r_hb[ti % 2], rows, (ti % 4) * 2)

        def pb(ti):
            kind, t, rows = tiles[ti]
            dst, r_dst = dst_of(ti)
            self.prenorm_b(hb[ti % 2], r_hb[ti % 2], rows, 0, dst, r_dst, (0, 1))
            if kind == "prev" and t == 7:
                S.op("pool", "tensor_copy", dict(out=self.hTp15[:, :, 0:15], in_=dst[:, :, 113:128]),
                     reads=[r_dst], writes=[self.r_hTp15])

        n = len(tiles)
        issue_x(0); issue_x(1)
        self.load_cs(0, *[tiles[0][0], tiles[0][1], tiles[0][2]])
        self.load_cs(1, *[tiles[1][0], tiles[1][1], tiles[1][2]])
        pa(0); pb(0)
        for ti, (kind, t, rows) in enumerate(tiles):
            sl = ti % 2
            dst, r_dst = dst_of(ti)
            if ti + 1 < n:
                pa(ti + 1)
            if ti + 2 < n:
                issue_x(ti + 2)
            self.tm_proj("k", 0, dst, r_dst, rows, sl, kind, t, 0)
            if ti + 1 < n:
                pb(ti + 1)
            self.tm_proj("k", 1, dst, r_dst, rows, sl, kind, t, 1)
            if ti + 2 < n:
                k2, t2, rows2 = tiles[ti + 2]
                self.load_cs(sl, k2, t2, rows2)
            self.tm_proj("v", 2, dst, r_dst, rows, sl, kind, t, 0)
            self.tm_proj("v", 3, dst, r_dst, rows, sl, kind, t, 1)
        self.flush_pending()
        self.kv_tmp_res = r_xin + r_hb + r_hTt + [self.r_junk]

    def next_piece(self, src):
        slot = self.npiece % 3
        self.npiece += 1
        self.load_piece(slot, src)
        return slot

    def realias(self, old, new):
        bar = []
        for r in old:
            if r.last_write is not None:
                bar.append(r.last_write)
            bar.extend(r.reads)
        if not bar:
            return
        j = self.S.op("sp", "nop", {}, extra=bar)
        for r in new:
            r.last_write = j
            r.reads = []

    def dbg(self, name, shape, src, reads):
        if name not in self.debug:
            return
        t = self.nc.dram_tensor("dbg_" + name, list(shape), src.dtype, kind="ExternalOutput").ap()
        self.outs.append(self.S.dma("sp", t, src, reads=reads, lane=self.S.lane("dbg_" + name)))

    def phase_qu(self):
        S = self.S
        R0 = self.R0
        o = R0 + 3 * 16384
        self.mixed = self.sb("mixed", [128, 16, NT], BF16, o); o += 32896
        self.r_mixed = [Res("mixed%d" % i) for i in range(16)]
        o = (o + 31) // 32 * 32
        o = self.alloc_stage(o)
        X0 = o
        uT = [self.sb("uT0", [128, 1044], F32, X0)] * 2
        sa = self.sb("sa", [128, 1044], F32, X0 + 4192)
        sbb = self.sb("sbb", [128, 1044], F32, X0 + 8384)
        pooled = [self.sb("pooled%d" % i, [128, 2, NT], BF16, X0 + 12576 + i * 4128) for i in range(2)]
        GB0 = SBUF_BASE + 6656
        wpool = self.sb("wpool", [128, 2048], BF16, GB0)
        spool_t = self.sb("spool_t", [60, 1024], F32, GB0 + 4096)
        t16 = self.sb("t16", [128, 16], F32, X0 + 20832)
        xend = X0 + 20896
        assert xend <= self.PERS0, (xend, self.PERS0)
        self.qT = self.sb("qT", [128, 8, 1024], BF16, X0)
        self.qT_end = X0 + 16384
        self.r_qT = [Res("qT%d" % i) for i in range(8)]
        r_uT = [Res("uT0")] * 2
        r_sa = Res("sa"); r_sb = Res("sb")
        r_pooled = [Res("pooled0"), Res("pooled1")]
        r_wpool = Res("wpool"); r_spool = Res("spool"); r_t16 = Res("t16")
        self.gb_alias = [r_wpool, r_spool]
        old = list(self.kv_tmp_res) + self.r_stage + self.r_kb + self.r_rtmp + [self.wres[3]]
        self.r_stage = [Res("st%d" % i) for i in range(4)]
        self.r_kb = [Res("kb0"), Res("kb1")]
        self.r_rtmp = [Res("rt0"), Res("rt1")]
        newres = self.r_mixed + self.r_stage + self.r_kb + self.r_rtmp + [r_uT[0], r_sa, r_sb] + r_pooled + [r_t16]
        self.realias(old, newres)
        l_wp = S.lane("wpool"); l_sp = S.lane("spool"); l_po = S.lane("pools_out")
        S.dma("pool", wpool[:], self.w_pool_l, writes=[r_wpool], lane=l_wp)
        S.dma("sp", spool_t[:], self.spool, writes=[r_spool], lane=l_sp)
        for s_ in range(4):
            self.outs.append(S.dma("sp", self.o_pools[s_, 0:14, :], spool_t[s_ * 15 + 1:s_ * 15 + 15, :], reads=[r_spool], lane=l_po))

        groups4 = GROUPS + [None]
        gct = 0
        pending_map = []
        for piece in range(2):
            slot = (0, 1)[piece]
            self.load_piece(slot, self.w_in_l[piece])
            self.tm_proj("u", slot, self.hT[:, :, 7 * 128:8 * 128], self.r_hT[7], 128, 0, "main", 7, piece)
            self.tm_proj("u", slot, self.hT[:, :, 1024:1028], self.r_hT[8], 4, 0, "samp", 8, piece)
            for ct in range(4):
                g = gct // 2
                cc = gct % 2
                win = 2 << g
                u = uT[gct % 2]
                r_u = r_uT[gct % 2]
                bset = (2, 3, 4) if gct % 2 == 0 else (5, 6, 7)
                for c in range(16):
                    lw = self.WR[slot][:, c * 512 + ct * 128:c * 512 + (ct + 1) * 128]
                    for gi, grp in enumerate(groups4):
                        if grp is None:
                            S.op("pe", "matmul", dict(out=self.banks[bset[2]][:, 300:315], lhsT=lw, rhs=self.hTp15[:, c, 0:15], start=False, stop=(c == 15),
                                                      skip_group_check=True),
                                 reads=[self.r_hTp15, self.wres[slot]], writes=[self.bres[bset[2]]])
                        else:
                            n = grp[1] - grp[0]
                            rr = self.r_hT[grp[0] // 128:(grp[1] + 127) // 128]
                            S.op("pe", "matmul", dict(out=self.banks[bset[gi]][:, 0:n], lhsT=lw, rhs=self.hT[:, c, grp[0]:grp[1]], start=(c == 0), stop=(c == 15),
                                                      skip_group_check=True),
                                 reads=rr + [self.wres[slot]], writes=[self.bres[bset[gi]]])
                for gi, grp in enumerate(groups4):
                    if grp is None:
                        S.op("act", "activation", dict(out=u[:, 0:15], in_=self.banks[bset[2]][:, 300:315], func=AF.Copy),
                             reads=[self.bres[bset[2]]], writes=[r_u])
                    else:
                        n = grp[1] - grp[0]
                        S.op("act", "activation", dict(out=u[:, 15 + grp[0]:15 + grp[1]], in_=self.banks[bset[gi]][:, 0:n], func=AF.Copy),
                             reads=[self.bres[bset[gi]]], writes=[r_u])
                while pending_map:
                    pending_map.pop(0)()
                cur, r_cur = u, r_u
                bufs = [(sa, r_sa), (sbb, r_sb)]
                sh = 1
                k = 0
                while sh < win:
                    nxt, r_nxt = bufs[k % 2]
                    lo = 2 * sh - 1
                    S.op("dve", "tensor_tensor", dict(out=nxt[:, lo:1039], in0=cur[:, lo:1039], in1=cur[:, lo - sh:1039 - sh], op=ALU.add),
                         reads=[r_cur], writes=[r_nxt])
                    cur, r_cur = nxt, r_nxt
                    sh *= 2
                    k += 1
                pl = pooled[g % 2]
                r_pl = r_pooled[g % 2]
                S.op("dve", "scalar_tensor_tensor", dict(out=pl[:, cc, 16:1024], in0=cur[:, 31:1039], scalar=float(1.0 / win), in1=u[:, 31:1039],
                                                       op0=ALU.mult, op1=ALU.subtract),
                     reads=[r_cur, r_u], writes=[r_pl])
                ic = self.cf("invcnt")[:, g * 16:(g + 1) * 16]
                S.op("dve", "tensor_tensor", dict(out=t16[:, :], in0=cur[:, 15:31], in1=ic, op=ALU.mult), reads=[r_cur, self.r_const], writes=[r_t16])
                S.op("dve", "tensor_tensor", dict(out=pl[:, cc, 0:16], in0=t16[:, :], in1=u[:, 15:31], op=ALU.subtract), reads=[r_t16, r_u], writes=[r_pl])
                cs_ = slice(gct * 128, (gct + 1) * 128)
                S.op("pe", "matmul", dict(out=self.banks[1][:, 400:404], lhsT=spool_t[0:60, cs_], rhs=self.cf("coefm")[0:60, g * 4:(g + 1) * 4],
                                          start=True, stop=False, skip_group_check=True), reads=[r_spool, self.r_const], writes=[self.bres[1]])
                S.op("pe", "matmul", dict(out=self.banks[1][:, 400:404], lhsT=self.usf[0:4, cs_], rhs=self.cf("udiag")[0:4, g * 4:(g + 1) * 4],
                                          start=False, stop=True, skip_group_check=True), reads=[self.r_usf, self.r_const], writes=[self.bres[1]])
                S.op("act", "activation", dict(out=pl[:, cc, 1024:1028], in_=self.banks[1][:, 400:404], func=AF.Copy),
                     reads=[self.bres[1]], writes=[r_pl])
                if cc == 1:
                  def group_map(g=g, pl=pl, r_pl=r_pl):
                    for et in range(2):
                        for gi, grp in enumerate(GROUPS):
                            n = grp[1] - grp[0]
                            bk = 2 + gi if False else (0 + (gi + et * 3) % 2)
                            for c2 in range(2):
                                off = g * 512 + c2 * 256 + et * 128
                                S.op("pe", "matmul", dict(out=self.banks[bk][:, 0:n], lhsT=wpool[:, off:off + 128], rhs=pl[:, c2, grp[0]:grp[1]],
                                                          start=(c2 == 0), stop=(c2 == 1)),
                                     reads=[r_wpool, r_pl], writes=[self.bres[bk]])
                            mt = 2 * g + et
                            S.op("dve", "tensor_scalar", dict(out=self.mixed[:, mt, grp[0]:grp[1]], in0=self.banks[bk][:, 0:n],
                                                              scalar1=self.cf("pscale")[:, mt:mt + 1], scalar2=None, op0=ALU.mult),
                                 reads=[self.bres[bk], self.r_const], writes=[self.r_mixed[mt]])
                  pending_map.append(group_map)
                gct += 1
        while pending_map:
            pending_map.pop(0)()
        if self.upto == "u":
            return
        self.realias([r_uT[0], r_sa, r_sb] + r_pooled + [r_t16], self.r_qT)
        tiles = [("main", t, 128) for t in range(8)] + [("samp", 8, 4)]
        csq = self.sb("csq", [128, 9, 64], F32, self.PERS0)
        r_csq = Res("csq")
        self.r_csq = r_csq
        self.realias([self.r_usf], [r_csq])
        S.dma("sp", csq[:, :, :], self.cs_main.rearrange("(t p) c -> p t c", p=128), writes=[r_csq], lane=S.lane("csq"))
        for piece in range(2):
            slot = (2, 0)[piece]
            self.load_piece(slot, self.w_in_l[2 + piece])
            for ti, (kind, t, rows) in enumerate(tiles):
                lhs = self.hT[:, :, t * 128:t * 128 + rows]
                self.tm_proj("q", slot, lhs, self.r_hT[t], rows, 0, kind, t, piece, cs_ap=csq[:rows, t, :], r_csx=r_csq)
        self.flush_pending()
        self.qu_tmp = r_uT + [r_sa, r_sb] + r_pooled + [r_t16]


    def phase_mkv(self):
        S = self.S
        hT0 = SBUF_END - TOP_GUARD - 32768 - 32896
        o = hT0
        mxin = self.sb("mxin", [128, D], F32, o); o += 8192
        mhb = self.sb("mhb", [128, D], BF16, o); o += 4096
        self.junk = self.sb("junk2", [128, D], BF16, o); o += 4096
        hmT = self.sb("hmT", [128, 16, 256], BF16, o); o += 8192
        mxin2 = self.sb("mxin2", [128, D], F32, o); o += 8192
        assert o <= hT0 + 32896 + 512
        g0 = self.R0 + 16384 + 24576
        self.memKT = self.sb("memKT", [128, 4, 256], BF16, g0)
        self.memV = self.sb("memV", [128, 2, 512], BF16, g0 + 2048)
        r_mxin = Res("mxin"); r_mxin2 = Res("mxin2"); r_mhb = Res("mhb"); self.r_junk = Res("junk2"); r_hmT = [Res("hmT0"), Res("hmT1")]
        self.r_memKT = Res("memKT"); self.r_memV = Res("memV")
        self.realias(self.r_hT + [self.r_hTp15], [r_mxin, r_mxin2, r_mhb, self.r_junk] + r_hmT)
        self.realias([self.wres[2]], [self.r_memKT, self.r_memV])
        l_mx = S.lane("mxin")
        sK = 1; sV = 0
        self.load_piece(sK, self.w_mkv_l[0])
        self.load_piece(sV, self.w_mkv_l[1])
        mx = [(mxin, r_mxin, l_mx), (mxin2, r_mxin2, S.lane("mxin2"))]
        for t in range(2):
            S.dma("sp", mx[t][0][:, :], self.memx[t * 128:(t + 1) * 128, :], writes=[mx[t][1]], lane=mx[t][2])
        for t in range(2):
            self.prenorm_tile(mx[t][0], mx[t][1], mhb, r_mhb, 128, 3, hmT[:, :, t * 128:(t + 1) * 128], r_hmT[t], (0, 1), 8 + 2 * t)
        for t in range(2):
            for kind, slot in (("mk", sK), ("mv", sV)):
                nst = self.nst
                bank = 2 + (nst % 4)
                ps = self.banks[bank]
                for c in range(16):
                    S.op("pe", "matmul", dict(out=ps[:, :], lhsT=hmT[:, c, t * 128:(t + 1) * 128], rhs=self.WR[slot][:, c * 512:(c + 1) * 512],
                                              start=(c == 0), stop=(c == 15)), reads=[r_hmT[t], self.wres[slot]], writes=[self.bres[bank]])
                st = nst % 4
                stg = self.stage[st]
                S.op("act", "activation", dict(out=stg[:, :], in_=ps[:, :], func=AF.Copy), reads=[self.bres[bank]], writes=[self.r_stage[st]])
                od = (self.o_mk if kind == "mk" else self.o_mv)[t * 128:(t + 1) * 128, :]
                self.outs.append(S.dma("sp", od, stg[:, :], reads=[self.r_stage[st]], lane=self.l_st[st]))
                if kind == "mv":
                    S.op("dve", "tensor_copy", dict(out=self.memV[:, t, :], in_=stg[:, :]), reads=[self.r_stage[st]], writes=[self.r_memV])
                self.nst += 1
        for hh in range(4):
            bank = 2 + (hh % 4)
            for c in range(16):
                S.op("pe", "matmul", dict(out=self.banks[bank][:, 0:256], lhsT=self.WR[sK][:, c * 512 + hh * 128:c * 512 + (hh + 1) * 128],
                                          rhs=hmT[:, c, 0:256], start=(c == 0), stop=(c == 15)),
                     reads=r_hmT + [self.wres[sK]], writes=[self.bres[bank]])
            S.op("act", "activation", dict(out=self.memKT[:, hh, :], in_=self.banks[bank][:, 0:256], func=AF.Copy),
                 reads=[self.bres[bank]], writes=[self.r_memKT])
        self.mkv_tmp = [r_mxin, r_mxin2, r_mhb, self.r_junk] + r_hmT

    def phase_att(self):
        S = self.S
        hT0 = SBUF_END - TOP_GUARD - 32768 - 32896
        vA = self.sb("vbufA", [128, 48, 256], BF16, hT0)
        pT = [self.sb("pT%d" % i, [128, 512], BF16, hT0 + 24576 + i * 1024) for i in range(3)]
        rden = [self.sb("rden%d" % i, [128, 512], F32, hT0 + 24576 + 3072 + i * 2048) for i in range(2)]
        assert hT0 + 24576 + 3072 + 4096 <= hT0 + 32896 + 512
        vB = self.sb("vbufB", [128, 48, 256], BF16, self.R0 + 16384)
        vbuf = [vA, vB]
        r_vA = Res("vbufA"); r_pT = [Res("pT%d" % i) for i in range(3)]; r_rden = [Res("rden0"), Res("rden1")]
        self.realias(self.mkv_tmp, [r_vA] + r_pT + r_rden)
        r_vAp = [Res("vA_nat"), Res("vA_d4"), Res("vA_d16")]
        r_vBp = [Res("vB_nat"), Res("vB_d4"), Res("vB_d16")]
        self.realias([r_vA], r_vAp)
        self.realias([self.wres[1], self.wres[2]], r_vBp)
        r_vbuf = [r_vAp, r_vBp]
        l_v = [[S.lane("vbufA%d" % i) for i in range(3)], [S.lane("vbufB%d" % i) for i in range(3)]]
        scale = float(128 ** -0.5)
        ones = self.cb("ones"); validp = self.cb("validp"); valid16 = self.cb("valid16")
        mu_ml = self.cb("mu_ml"); m16 = self.cb("m16")
        mask4 = mu_ml.unsqueeze(1).to_broadcast([128, 2, 256])
        all_vs = self.r_vs

        def load_v(hp):
            b = hp % 2
            cols = slice(hp * 256, (hp + 1) * 256)
            vsrc = self.vs_d[:, cols]
            wr = r_vbuf[b]
            S.dma("sp", vbuf[b][:, 0:16, :], vsrc.rearrange("(b p) c -> p b c", p=128), reads=all_vs, writes=[wr[0]], lane=l_v[b][0])
            S.dma("sp", vbuf[b][:, 16:32, :].rearrange("p (r b) c -> p r b c", r=4),
                  vsrc.rearrange("(b l r) c -> l r b c", l=128, r=4), reads=all_vs, writes=[wr[1]], lane=l_v[b][1])
            S.dma("sp", vbuf[b][:, 32:48, :], vsrc.rearrange("(l r) c -> l r c", r=16), reads=all_vs, writes=[wr[2]], lane=l_v[b][2])

        load_v(0)
        load_v(1)
        self.load_piece(0, self.w_out_l[0])
        batches = []
        for hp in range(4):
            for hh in range(2):
                for hf in range(2):
                    unit = (hp * 2 + hh) * 2 + hf
                    h = hp * 2 + hh
                    hc = slice(hh * 128, (hh + 1) * 128)
                    vb = vbuf[hp % 2]; r_vb = r_vbuf[hp % 2]
                    ob = 4 + (unit % 2); db = 6 + (unit % 2)
                    q0 = 512 * hf
                    ubatches = []
                    for ip in range(2):
                        smm = []; pvs = []
                        for ii in range(2):
                            i = 4 * hf + 2 * ip + ii
                            for j, kb in enumerate((7 + i, 8 + i)):
                                col = (ii * 2 + j) * 128
                                smm.append((slice(col, col + 128), self.kT[:, h, kb * 128:(kb + 1) * 128], self.qT[:, h, i * 128:(i + 1) * 128],
                                            [self.r_kT[kb], self.r_qT[i]]))
                                pvs.append((vb[:, kb, hc], validp if kb < 8 else ones, slice(col, col + 128),
                                            slice((i - 4 * hf) * 128, (i - 4 * hf + 1) * 128)))
                        ubatches.append((smm, pvs, mask4, 256, 0))
                    qb = 2 + hf
                    for rp in range(2):
                        smm = []; pvs = []
                        for ri in range(2):
                            r4 = rp * 2 + ri
                            for j, blk in enumerate((qb - 1, qb)):
                                col = (ri * 2 + j) * 128
                                smm.append((slice(col, col + 128), self.kT[:, h, 512 * blk + r4:512 * blk + 512:4], self.qT[:, h, q0 + r4:q0 + 512:4],
                                            self.r_kT[4 * blk:4 * blk + 4] + self.r_qT[4 * hf:4 * hf + 4]))
                                pvs.append((vb[:, 16 + r4 * 4 + blk, hc], validp if blk < 2 else ones, slice(col, col + 128), slice(r4, 512, 4)))
                        ubatches.append((smm, pvs, mask4, 256, 1))
                    smm = []; pvs = []
                    for r16 in range(16):
                        smm.append((slice(r16 * 32, (r16 + 1) * 32), self.kT[:, h, r16:2048:16], self.qT[:, h, q0 + r16:q0 + 512:16],
                                    self.r_kT + self.r_qT[4 * hf:4 * hf + 4]))
                        pvs.append((vb[:, 32 + r16, hc], valid16, slice(r16 * 32, (r16 + 1) * 32), slice(r16, 512, 16)))
                    m16b = m16[:, hf * 32:(hf + 1) * 32].unsqueeze(1).to_broadcast([128, 16, 32])
                    ubatches.append((smm, pvs, m16b, 32, 2))
                    for bi, ub in enumerate(ubatches):
                        batches.append(dict(unit=unit, h=h, q0=q0, ob=ob, db=db, r_vb=[r_vb[ub[4]]], first=(bi == 0), last=(bi == len(ubatches) - 1),
                                            hp=hp, smm=ub[0], pvs=ub[1], mask=ub[2], shape3=ub[3]))

        def emit_S(bi):
            B = batches[bi]
            sb_ = bi % 4
            for (cols, lhsT, rhs, rr) in B["smm"]:
                S.op("pe", "matmul", dict(out=self.banks[sb_][:, cols], lhsT=lhsT, rhs=rhs, start=True, stop=True, skip_group_check=True),
                     reads=rr, writes=[self.bres[sb_]])
            p = pT[bi % 3]; r_p = r_pT[bi % 3]
            S.op("act", "activation", dict(out=p[:, :], in_=self.banks[sb_][:, :], func=AF.Exp, scale=scale), reads=[self.bres[sb_]], writes=[r_p])
            pv_ = p[:, :].rearrange("p (a b) -> p a b", b=B["shape3"])
            S.op("pool", "tensor_tensor", dict(out=pv_, in0=pv_, in1=B["mask"], op=ALU.mult), reads=[r_p, self.r_const], writes=[r_p])

        def emit_PV(bi):
            B = batches[bi]
            p = pT[bi % 3]; r_p = r_pT[bi % 3]
            ob, db = B["ob"], B["db"]
            O = self.banks[ob]; DEN = self.banks[db]
            for k, (vl, dl, pcols, ocols) in enumerate(B["pvs"]):
                st = B["first"] and k == 0
                S.op("pe", "matmul", dict(out=O[:, ocols], lhsT=vl, rhs=p[:, pcols], start=st, stop=False, skip_group_check=True),
                     reads=[r_p] + B["r_vb"], writes=[self.bres[ob]])
                S.op("pe", "matmul", dict(out=DEN[:, ocols], lhsT=dl, rhs=p[:, pcols], start=st, stop=False, skip_group_check=True),
                     reads=[r_p, self.r_const], writes=[self.bres[db]])
            if B["last"]:
                unit = B["unit"]
                rd = rden[unit % 2]; r_rd = r_rden[unit % 2]
                S.op("dve", "reciprocal", dict(out=rd[:, :], in_=DEN[:, :]), reads=[self.bres[db]], writes=[r_rd])
                S.op("dve", "tensor_tensor", dict(out=self.mixed[:, 8 + B["h"], B["q0"]:B["q0"] + 512], in0=O[:, :], in1=rd[:, :], op=ALU.mult),
                     reads=[self.bres[ob], r_rd], writes=[self.r_mixed[8 + B["h"]]])
                if unit % 4 == 3 and B["hp"] + 2 < 4:
                    load_v(B["hp"] + 2)

        nbt = len(batches)
        emit_S(0)
        for bi in range(nbt):
            if bi + 1 < nbt:
                emit_S(bi + 1)
            emit_PV(bi)
        self.realias(r_vBp, [self.wres[1], self.wres[2]])
        self.att_tmp = [r_vA] + r_vAp + r_pT + r_rden


    def sample_attn(self, s_, W, nh, sets, qrow, r_qrow, selfkv, dest, r_dest, sc):
        self.sample_p1(s_, W, nh, sets, qrow, r_qrow, selfkv, sc, load_v=True)
        self.sample_p2(s_, W, nh, len(sets), selfkv, dest, r_dest, sc)

    def sample_p1(self, s_, W, nh, sets, qrow, r_qrow, selfkv, sc, load_v=True, qb=(0, 1)):
        S = self.S
        scale = float(128 ** -0.5)
        Kc, Vc, r_K, r_V, l_K, l_V = sc["Kc"], sc["Vc"], sc["r_K"], sc["r_V"], sc["l_K"], sc["l_V"]
        qbc, tmp, scr, e = sc["qbc"], sc["tmp"], sc["scr"], sc["e"]
        r_qbc, r_tmp, r_scr, r_e = sc["r_qbc"], sc["r_tmp"], sc["r_scr"], sc["r_e"]
        ns = len(sets)
        for i, (ks, vs) in enumerate(sets):
            S.dma("pool", Kc[i][:, 0:W], ks, writes=[r_K[i]], lane=l_K[i])
            if load_v:
                S.dma("pool", Vc[i][:, 0:W], vs, writes=[r_V[i]], lane=l_V[i])
        nhalf = W // 512
        ohb = self.cb("ohb")
        for hf in range(nhalf):
            S.op("pe", "matmul", dict(out=self.banks[qb[hf]][:, :], lhsT=ohb[0:4, s_ * 128:(s_ + 1) * 128], rhs=qrow[0:4, hf * 512:(hf + 1) * 512],
                                      start=True, stop=True), reads=[r_qrow, self.r_const], writes=[self.bres[qb[hf]]])
            S.op("act", "activation", dict(out=qbc[:, hf * 512:(hf + 1) * 512], in_=self.banks[qb[hf]][:, :], func=AF.Copy),
                 reads=[self.bres[qb[hf]]], writes=[r_qbc])
        for i in range(ns):
            S.op("dve", "tensor_tensor", dict(out=tmp[:, 0:W], in0=Kc[i][:, 0:W], in1=qbc[:, 0:W], op=ALU.mult), reads=[r_K[i], r_qbc], writes=[r_tmp])
            S.op("dve", "tensor_reduce", dict(out=scr[:, i * nh:(i + 1) * nh], in_=tmp[:, 0:W].rearrange("p (h d) -> p h d", d=128),
                                              axis=AX.X, op=ALU.add), reads=[r_tmp], writes=[r_scr])
        S.op("act", "activation", dict(out=e[:, 0:ns * nh], in_=scr[:, 0:ns * nh], func=AF.Exp, scale=scale), reads=[r_scr], writes=[r_e])
        if selfkv is not None:
            e4f, r_e4f, e4s, r_e4s, vrow, r_vrow = selfkv
            S.op("dve", "tensor_scalar", dict(out=e4s[0:4, 0:nh], in0=e4f[0:4, 0:nh], scalar1=self.cf("oh4")[0:4, s_:s_ + 1], scalar2=None, op0=ALU.mult),
                 reads=[r_e4f, self.r_const], writes=[r_e4s])

    def sample_p2(self, s_, W, nh, ns, selfkv, dest, r_dest, sc):
        S = self.S
        Vc, r_V = sc["Vc"], sc["r_V"]
        e, po, od, odn, rd = sc["e"], sc["po"], sc["od"], sc["odn"], sc["rd"]
        r_e, r_po, r_od = sc["r_e"], sc["r_po"], sc["r_od"]
        nhalf = W // 512
        ones = self.cb("ones")
        if selfkv is not None:
            e4f, r_e4f, e4s, r_e4s, vrow, r_vrow = selfkv
        for hf in range(nhalf):
            bk = 2 + hf
            for i in range(ns):
                S.op("pe", "matmul", dict(out=self.banks[bk][0:nh, :], lhsT=e[:, i * nh:(i + 1) * nh], rhs=Vc[i][:, hf * 512:(hf + 1) * 512],
                                          start=(i == 0), stop=(i == ns - 1 and selfkv is None)),
                     reads=[r_e, r_V[i]], writes=[self.bres[bk]])
            if selfkv is not None:
                S.op("pe", "matmul", dict(out=self.banks[bk][0:nh, :], lhsT=e4s[0:4, 0:nh], rhs=vrow[0:4, hf * 512:(hf + 1) * 512], start=False, stop=True),
                     reads=[r_e4s, r_vrow], writes=[self.bres[bk]])
        for i in range(ns):
            S.op("pe", "matmul", dict(out=self.banks[4][0:nh, 0:1], lhsT=e[:, i * nh:(i + 1) * nh], rhs=ones[:, 0:1],
                                      start=(i == 0), stop=(i == ns - 1 and selfkv is None)), reads=[r_e, self.r_const], writes=[self.bres[4]])
        if selfkv is not None:
            S.op("pe", "matmul", dict(out=self.banks[4][0:nh, 0:1], lhsT=e4s[0:4, 0:nh], rhs=ones[0:4, 0:1], start=False, stop=True),
                 reads=[r_e4s, self.r_const], writes=[self.bres[4]])
        bmask = self.cb("bmask")
        for hf in range(nhalf):
            S.op("dve", "tensor_tensor", dict(out=po[0:nh, hf * 512:(hf + 1) * 512], in0=self.banks[2 + hf][0:nh, :],
                                              in1=bmask[0:nh, hf * 512:(hf + 1) * 512], op=ALU.mult),
                 reads=[self.bres[2 + hf], self.r_const], writes=[r_po])
        S.op("dve", "tensor_reduce", dict(out=od[0:nh, 0:128], in_=po[0:nh, 0:W].rearrange("p (h d) -> p d h", d=128), axis=AX.X, op=ALU.add),
             reads=[r_po], writes=[r_od])
        S.op("dve", "reciprocal", dict(out=rd[0:nh, 0:1], in_=self.banks[4][0:nh, 0:1]), reads=[self.bres[4]], writes=[r_od])
        S.op("dve", "tensor_scalar", dict(out=odn[0:nh, 0:128], in0=od[0:nh, 0:128], scalar1=rd[0:nh, 0:1], scalar2=None, op0=ALU.mult),
             reads=[r_od], writes=[r_od])
        tb = self.bank_bf(5)
        S.op("pe", "transpose", dict(out=tb[:, 0:nh], in_=odn[0:nh, 0:128], identity=self.cb("ident")[0:nh, 0:nh]),
             reads=[r_od, self.r_const], writes=[self.bres[5]])
        S.op("act", "activation", dict(out=dest, in_=tb[:, 0:nh], func=AF.Copy), reads=[self.bres[5]], writes=r_dest)

    def sample_scratch(self, o, W, ns, tag, ext=None):
        S = self.S
        sc = {}
        sc["Kc"] = [self.sb("Kc%s%d" % (tag, i), [128, W], BF16, o + i * 2 * W) for i in range(ns)]; o += ns * 2 * W
        sc["Vc"] = [self.sb("Vc%s%d" % (tag, i), [128, W], BF16, o + i * 2 * W) for i in range(ns)]; o += ns * 2 * W
        sc["qbc"] = self.sb("qbc" + tag, [128, W], BF16, o); o += 2 * W
        if ext is None:
            sc["tmp"] = self.sb("tmp" + tag, [128, W], F32, o); o += 4 * W
            sc["po"] = self.sb("po" + tag, [8, W], F32, o); o += 4 * W
        else:
            sc["tmp"], sc["po"] = ext[0], ext[1]
        sc["scr"] = self.sb("scr" + tag, [128, 32], F32, o); o += 128
        sc["e"] = self.sb("e" + tag, [128, 32], BF16, o); o += 64
        sc["od"] = self.sb("od" + tag, [8, 128], F32, o); o += 512
        sc["odn"] = self.sb("odn" + tag, [8, 128], BF16, o); o += 256
        sc["rd"] = self.sb("rd" + tag, [8, 8], F32, o); o += 32
        sc["r_K"] = [Res("Kc%d" % i) for i in range(ns)]; sc["r_V"] = [Res("Vc%d" % i) for i in range(ns)]
        sc["l_K"] = [S.lane("Kc%s%d" % (tag, i)) for i in range(ns)]; sc["l_V"] = [S.lane("Vc%s%d" % (tag, i)) for i in range(ns)]
        for n in ("qbc", "tmp", "scr", "e", "po", "od"):
            sc["r_" + n] = Res(n + tag)
        if ext is not None:
            sc["r_tmp"], sc["r_po"] = ext[2], ext[3]
        sc["allres"] = sc["r_K"] + sc["r_V"] + [sc["r_" + n] for n in ("qbc", "scr", "e", "od")] + ([sc["r_tmp"], sc["r_po"]] if ext is None else [])
        return sc, o

    def phase_satt(self):
        S = self.S
        kT0 = SBUF_END - TOP_GUARD - 32768
        sc0, o = self.sample_scratch(kT0, 1024, 3, "d")
        sc1 = dict(sc0)
        sc1["qbc"] = self.sb("qbcd1", [128, 1024], BF16, o); o += 2048
        sc1["scr"] = self.sb("scrd1", [128, 32], F32, o); o += 128
        sc1["e"] = self.sb("ed1", [128, 32], BF16, o); o += 64
        sc1["od"] = self.sb("odd1", [8, 128], F32, o); o += 512
        sc1["odn"] = self.sb("odnd1", [8, 128], BF16, o); o += 256
        sc1["rd"] = self.sb("rdd1", [8, 8], F32, o); o += 32
        for nme in ("qbc", "scr", "e", "od"):
            sc1["r_" + nme] = Res(nme + "d1")
        e4s1 = self.sb("e4s1", [4, 8], BF16, o); o += 32
        tmp4 = self.sb("tmp4", [4, 1024], F32, o); o += 4096
        ssf = self.sb("ssf", [4, 8], F32, o); o += 32
        e4f = self.sb("e4f", [4, 8], F32, o); o += 32
        e4s0 = self.sb("e4s", [4, 8], BF16, o); o += 32
        assert o <= SBUF_END - TOP_GUARD, o
        r_t4 = Res("tmp4"); r_e4f = Res("e4f"); r_e4s = [Res("e4s0"), Res("e4s1")]
        extra = [sc1["r_" + nme] for nme in ("qbc", "scr", "e", "od")]
        self.realias(self.r_kT, sc0["allres"] + extra + [r_t4, r_e4f] + r_e4s)
        scale = float(128 ** -0.5)
        S.op("dve", "tensor_tensor", dict(out=tmp4[:, :], in0=self.qsb[:, :], in1=self.ksb[:, :], op=ALU.mult),
             reads=[self.r_qsb, self.r_ksb], writes=[r_t4])
        S.op("dve", "tensor_reduce", dict(out=ssf[:, :], in_=tmp4[:, :].rearrange("p (h d) -> p h d", d=128), axis=AX.X, op=ALU.add),
             reads=[r_t4], writes=[r_t4])
        S.op("act", "activation", dict(out=e4f[:, :], in_=ssf[:, :], func=AF.Exp, scale=scale, bias=self.cf("ln3")[0:4, :]),
             reads=[r_t4, self.r_const], writes=[r_e4f])
        scs = [sc0, sc1]
        e4ss = [e4s0, e4s1]

        def sets_of(s_):
            return [(self.cache_k[s_, 1920:2048, :], self.cache_v[s_, 1920:2048, :]),
                    (self.cache_k[s_, 1536:2048:4, :], self.cache_v[s_, 1536:2048:4, :]),
                    (self.cache_k[s_, 0:2048:16, :], self.cache_v[s_, 0:2048:16, :])]

        def selfkv(s_):
            return (e4f, r_e4f, e4ss[s_ % 2], r_e4s[s_ % 2], self.vsb, self.r_vsb)

        def load_vsets(s_):
            sc = scs[s_ % 2]
            for i, (ks, vs) in enumerate(sets_of(s_)):
                S.dma("pool", sc["Vc"][i][:, 0:1024], vs, writes=[sc["r_V"][i]], lane=sc["l_V"][i])

        def P1(s_):
            self.sample_p1(s_, 1024, 8, sets_of(s_), self.qsb, self.r_qsb, selfkv(s_), scs[s_ % 2], load_v=False,
                           qb=(0, 1) if s_ % 2 == 0 else (6, 7))

        def P2(s_):
            self.sample_p2(s_, 1024, 8, 3, selfkv(s_), self.mixed[:, 8:16, 1024 + s_], self.r_mixed[8:16], scs[s_ % 2])

        load_vsets(0)
        P1(0)
        for s_ in range(4):
            if s_ + 1 < 4:
                P1(s_ + 1)
            P2(s_)
            if s_ + 1 < 4:
                load_vsets(s_ + 1)
        self.satt_tmp = sc0["allres"] + extra + [r_t4, r_e4f] + r_e4s

    def rstd_chain(self, rows, src_cols, r_src, nred):
        S = self.S
        n = self.nepi
        self.nepi += 1
        sm = self.smalls
        base = 64 + (n % 16) * 2
        ssc = sm[:, base:base + 1]
        rs = sm[:, base + 1:base + 2]
        r_s = self.r_small[16 + n % 16]
        if nred > 1:
            S.op("dve", "tensor_reduce", dict(out=ssc[:rows, :], in_=src_cols, axis=AX.X, op=ALU.add), reads=[r_src], writes=[r_s])
            src, rr = ssc[:rows, :], [r_s]
        else:
            src, rr = src_cols, [r_src]
        S.op("pool", "tensor_scalar", dict(out=rs[:rows, :], in0=src, scalar1=float(D * EPS), scalar2=None, op0=ALU.add), reads=rr, writes=[r_s])
        S.op("pool", "tensor_tensor", dict(out=rs[:rows, :], in0=rs[:rows, :], in1=self.cf("nhalf")[:rows, :], op=ALU.pow), reads=[r_s, self.r_const], writes=[r_s])
        S.op("pool", "tensor_scalar", dict(out=rs[:rows, :], in0=rs[:rows, :], scalar1=float(np.sqrt(D)), scalar2=None, op0=ALU.mult), reads=[r_s], writes=[r_s])
        return rs, r_s

    def prenorm_a2(self, xt, r_x, hb, r_hb, rows, scale_eng="act"):
        S = self.S
        n = self.nepi
        col = 100 + (n % 8)
        ss = self.smalls[:, col:col + 1]
        r_ss = self.r_small[8 + n % 8]
        S.op("act", "activation", dict(out=self.junk[:rows, :], in_=xt, func=AF.Square, accum_out=ss[:rows, :]), reads=[r_x], writes=[self.r_junk, r_ss])
        rs, r_s = self.rstd_chain(rows, ss[:rows, :], r_ss, 1)
        if scale_eng == "act":
            S.op("act", "activation", dict(out=hb[:rows, :], in_=xt, func=AF.Copy, scale=rs[:rows, 0:1]), reads=[r_x, r_s], writes=[r_hb])
        else:
            S.op("dve", "tensor_scalar", dict(out=hb[:rows, :], in0=xt, scalar1=rs[:rows, 0:1], scalar2=None, op0=ALU.mult), reads=[r_x, r_s], writes=[r_hb])

    def load_gb(self, idx):
        self.S.dma("sp", self.gb[:, :], self.grows[idx:idx + 1, :].partition_broadcast(128), writes=[self.r_gb], lane=self.l_gb)

    def phase_wo(self):
        S = self.S
        X1_0 = SBUF_END - TOP_GUARD - 73728
        self.x1 = self.sb("x1", [128, 9, D], F32, X1_0)
        self.r_x1 = [Res("x1_%d" % i) for i in range(9)]
        dead = self.att_tmp + self.satt_tmp + self.r_kT + [self.r_ksb, self.r_vsb, self.r_qsb, self.r_usf] + self.r_qT
        self.realias(dead, self.r_x1)
        self.nepi = 0
        self.l_gb = S.lane("gb")
        self.realias(self.gb_alias, [self.r_gb])
        self.load_gb(0)
        sm = self.smalls
        tiles = [(t, 128) for t in range(8)] + [(8, 4)]
        slots = [0, 1, 0, 1]
        self.wo_last_mm = {}
        for cb in range(4):
            slot = slots[cb]
            if cb > 0:
                self.load_piece(slot, self.w_out_l[cb])
            for t, rows in tiles:
                nst = self.nst
                bank = 2 + (nst % 4)
                ps = self.banks[bank]
                for c in range(16):
                    mm_last = S.op("pe", "matmul", dict(out=ps[:rows, :], lhsT=self.mixed[:, c, t * 128:t * 128 + rows], rhs=self.WR[slot][:, c * 512:(c + 1) * 512],
                                                        start=(c == 0), stop=(c == 15)), reads=[self.r_mixed[c], self.wres[slot]], writes=[self.bres[bank]])
                ydst = self.x1[:rows, t, cb * 512:(cb + 1) * 512]
                S.op("act", "activation", dict(out=ydst, in_=ps[:rows, :], func=AF.Copy), reads=[self.bres[bank]], writes=[self.r_x1[t]])
                S.op("act", "activation", dict(out=self.junkw[:rows, 0:512], in_=ps[:rows, :], func=AF.Square, accum_out=sm[:rows, 16 + t * 4 + cb:16 + t * 4 + cb + 1]),
                     reads=[self.bres[bank]], writes=[self.r_x1[t], self.r_junkw])
                self.nst += 1
                if cb == 3:
                    self.wo_last_mm[t] = mm_last
                    if t == 0:
                        self.epi1_setup()
                    else:
                        self.epi1_step(t - 1)

    def alloc_epi(self):
        S = self.S
        X0 = self.qT_end - 16384
        self.xin = [self.sb("exin%d" % i, [128, D], F32, X0 + i * 8192) for i in range(2)]
        self.r_xin = [Res("exin0"), Res("exin1")]
        self.l_xin = [S.lane("exin0"), S.lane("exin1")]
        self.realias(self.r_qT, self.r_xin)
        s2 = self.R0 + 2 * 16384
        self.junk = self.sb("ejunk", [128, D], BF16, s2)
        self.hb1 = self.sb("ehb", [128, D], BF16, self.R0 + 16384 + 24576 + 4096)
        self.hb2 = self.sb("ehb2", [128, D], BF16, self.R0 + 16384 + 24576 - 4096)
        self.ehb = [self.hb1, self.hb2]
        self.r_junk = Res("ejunk"); self.r_hb1 = Res("ehb"); self.r_hb2 = Res("ehb2")
        self.r_ehb = [self.r_hb1, self.r_hb2]
        self.junkw = self.junk
        self.r_junkw = self.r_junk

    def epi1_setup(self):
        S = self.S
        tiles = [(t, 128) for t in range(8)] + [(8, 4)]
        self.e1_tiles = tiles
        sm = self.smalls
        n = len(tiles)

        def issue(ti):
            t, rows = tiles[ti]
            src = self.xs if t == 8 else self.xm[t * 128:(t + 1) * 128, :]
            S.dma("sp", self.xin[ti % 2][:rows, :], src, writes=[self.r_xin[ti % 2]], lane=self.l_xin[ti % 2])
        issue(0); issue(1)
        self.h2T = self.sb("h2T", [128, 16, NT], BF16, self.R0 + 3 * 16384)
        self.r_h2T = [Res("h2T%d" % i) for i in range(9)]
        self.load_piece(0, self.w_xq_l[0])
        rs_of = {}

        def stageA1a(ti):
            t, rows = tiles[ti]
            rs_of[ti] = self.rstd_chain(rows, sm[:rows, 16 + t * 4:16 + t * 4 + 4], self.r_x1[t], 4)

        def stageA1b(ti):
            t, rows = tiles[ti]
            xt = self.x1[:rows, t, :]
            rs, r_s = rs_of[ti]
            S.op("dve", "scalar_tensor_tensor", dict(out=xt, in0=xt, scalar=rs[:rows, 0:1], in1=self.gb[:rows, :], op0=ALU.mult, op1=ALU.mult),
                 reads=[self.r_x1[t], r_s, self.r_gb], writes=[self.r_x1[t]])
            xi = self.xin[ti % 2]
            S.op("dve", "tensor_tensor", dict(out=xt[:, 0:768], in0=xt[:, 0:768], in1=xi[:rows, 0:768], op=ALU.add),
                 reads=[self.r_x1[t], self.r_xin[ti % 2]], writes=[self.r_x1[t]])
            S.op("pool", "tensor_tensor", dict(out=xt[:, 768:D], in0=xt[:, 768:D], in1=xi[:rows, 768:D], op=ALU.add),
                 reads=[self.r_x1[t], self.r_xin[ti % 2]], writes=[self.r_x1[t]])

        def stageA2(ti):
            t, rows = tiles[ti]
            self.prenorm_a2(self.x1[:rows, t, :], self.r_x1[t], self.ehb[ti % 2], self.r_ehb[ti % 2], rows)

        def stageB(ti):
            t, rows = tiles[ti]
            self.r_h2T[t].last_write = self.wo_last_mm[t]
            self.prenorm_b(self.ehb[ti % 2], self.r_ehb[ti % 2], rows, 1, self.h2T[:, :, t * 128:t * 128 + rows], self.r_h2T[t], (0, 1))

        def step(it):
            if it + 1 < n:
                stageA1a(it + 1)
            if it < n:
                stageA1b(it)
            if it + 2 < n:
                issue(it + 2)
            if 0 <= it - 1 < n:
                stageA2(it - 1)
            if 0 <= it - 2 < n:
                stageB(it - 2)
        self._e1_step = step
        stageA1a(0)

    def epi1_step(self, it):
        self._e1_step(it)

    def phase_epi1(self):
        n = len(self.e1_tiles)
        for it in range(n - 1, n + 2):
            self._e1_step(it)
        self.load_piece(1, self.w_xo_l[0])

    def phase_xa(self):
        S = self.S
        scale = float(128 ** -0.5)
        X0 = self.qT_end - 16384
        self.q2T = self.sb("q2T", [128, 4, NT], BF16, X0)
        self.o2T = self.sb("o2T", [128, 4, NT], BF16, X0 + 8224)
        o = X0 + 16448
        pT2 = [self.sb("pT2_%d" % i, [128, 2, 384], BF16, o + i * 1536) for i in range(2)]; o += 3072
        rd2 = [self.sb("rd2_%d" % i, [128, 384], F32, o + i * 1536) for i in range(2)]; o += 3072
        assert o <= self.PERS0
        self.q2sb = self.sb("q2sb", [4, 512], BF16, self.PERS0)
        self.r_q2T = [Res("q2T%d" % i) for i in range(4)]
        self.r_o2T = [Res("o2T%d" % i) for i in range(4)]
        r_pT2 = [Res("pT2_0"), Res("pT2_1")]; r_rd2 = [Res("rd2_0"), Res("rd2_1")]
        self.r_q2sb = Res("q2sb")
        self.realias(self.r_xin + [self.r_csq], self.r_q2T + self.r_o2T + r_pT2 + r_rd2 + [self.r_q2sb])
        self.xa_small = r_pT2 + r_rd2
        sq = 0
        for hh in range(4):
            for c in range(16):
                lw = self.WR[sq][:, c * 512 + hh * 128:c * 512 + (hh + 1) * 128]
                for gi, (lo, hi) in enumerate(GROUPS):
                    S.op("pe", "matmul", dict(out=self.banks[2 + gi][:, 0:hi - lo], lhsT=lw, rhs=self.h2T[:, c, lo:hi], start=(c == 0), stop=(c == 15)),
                         reads=self.r_h2T[lo // 128:(hi + 127) // 128] + [self.wres[sq]], writes=[self.bres[2 + gi]])
            for gi, (lo, hi) in enumerate(GROUPS):
                S.op("act", "activation", dict(out=self.q2T[:, hh, lo:hi], in_=self.banks[2 + gi][:, 0:hi - lo], func=AF.Copy),
                     reads=[self.bres[2 + gi]], writes=[self.r_q2T[hh]])
        for c in range(16):
            S.op("pe", "matmul", dict(out=self.banks[5][0:4, :], lhsT=self.h2T[:, c, 1024:1028], rhs=self.WR[sq][:, c * 512:(c + 1) * 512],
                                      start=(c == 0), stop=(c == 15)), reads=[self.r_h2T[8], self.wres[sq]], writes=[self.bres[5]])
        S.op("act", "activation", dict(out=self.q2sb[:, :], in_=self.banks[5][0:4, :], func=AF.Copy), reads=[self.bres[5]], writes=[self.r_q2sb])
        self.load_piece(0, self.w_ff1_l[0])
        ones = self.cb("ones")
        iters = [(hh, lo, hi) for hh in range(4) for (lo, hi) in [(0, 384), (384, 768), (768, 1024)]]

        def xS(nb):
            hh, lo, hi = iters[nb]
            n = hi - lo
            p = pT2[nb % 2]; r_p = r_pT2[nb % 2]
            for kt in range(2):
                bk = (nb % 2) * 2 + kt
                S.op("pe", "matmul", dict(out=self.banks[bk][:, 0:n], lhsT=self.memKT[:, hh, kt * 128:(kt + 1) * 128], rhs=self.q2T[:, hh, lo:hi],
                                          start=True, stop=True), reads=[self.r_memKT, self.r_q2T[hh]], writes=[self.bres[bk]])
                S.op("act", "activation", dict(out=p[:, kt, 0:n], in_=self.banks[bk][:, 0:n], func=AF.Exp, scale=scale),
                     reads=[self.bres[bk]], writes=[r_p])

        def xPV(nb):
            hh, lo, hi = iters[nb]
            n = hi - lo
            p = pT2[nb % 2]; r_p = r_pT2[nb % 2]
            ob = 4 + (nb % 2); db = 6 + (nb % 2)
            for kt in range(2):
                S.op("pe", "matmul", dict(out=self.banks[ob][:, 0:n], lhsT=self.memV[:, kt, hh * 128:(hh + 1) * 128], rhs=p[:, kt, 0:n],
                                          start=(kt == 0), stop=(kt == 1)), reads=[self.r_memV, r_p], writes=[self.bres[ob]])
            for kt in range(2):
                S.op("pe", "matmul", dict(out=self.banks[db][:, 0:n], lhsT=ones, rhs=p[:, kt, 0:n], start=(kt == 0), stop=(kt == 1)),
                     reads=[self.r_const, r_p], writes=[self.bres[db]])
            rd = rd2[nb % 2]; r_rd = r_rd2[nb % 2]
            S.op("dve", "reciprocal", dict(out=rd[:, 0:n], in_=self.banks[db][:, 0:n]), reads=[self.bres[db]], writes=[r_rd])
            S.op("dve", "tensor_tensor", dict(out=self.o2T[:, hh, lo:hi], in0=self.banks[ob][:, 0:n], in1=rd[:, 0:n], op=ALU.mult),
                 reads=[self.bres[ob], r_rd], writes=[self.r_o2T[hh]])

        xS(0)
        for nb in range(len(iters)):
            if nb + 1 < len(iters):
                xS(nb + 1)
            xPV(nb)
        GB0 = SBUF_BASE + 6656
        sc, o = self.sample_scratch(GB0, 512, 2, "m", ext=(self.stage[0], self.stage[1], self.r_stage[0], self.r_stage[1]))
        assert o <= GB0 + 8192 + 128, o
        self.realias([self.r_gb], sc["allres"])
        for s_ in range(4):
            sets = [(self.cmem_k[s_, 0:128, :], self.cmem_v[s_, 0:128, :]), (self.cmem_k[s_, 128:256, :], self.cmem_v[s_, 128:256, :])]
            self.sample_attn(s_, 512, 4, sets, self.q2sb, self.r_q2sb, None, self.o2T[:, 0:4, 1024 + s_], self.r_o2T, sc)
        self.xa_tmp = sc["allres"]

    def phase_wxo(self):
        S = self.S
        self.r_gb2 = Res("gb2")
        self.realias(self.xa_tmp, [self.r_gb2])
        self.r_gb = self.r_gb2
        self.load_gb(1)
        self.h3T = self.h2T
        self.r_h3T = [Res("h3T%d" % i) for i in range(9)]
        self.realias(self.r_h2T, self.r_h3T)
        l_x2 = S.lane("x2spill")
        self.r_x2d = [Res("x2d%d" % i) for i in range(9)]
        sm = self.smalls
        tiles = [(t, 128) for t in range(8)] + [(8, 4)]
        n = len(tiles)
        so = 1
        ybuf = self.stage

        def bank_of(ti, cb):
            return 2 + (ti * 4 + cb) % 6

        def stageMM(ti):
            t, rows = tiles[ti]
            for cb in range(4):
                bank = bank_of(ti, cb)
                for c in range(4):
                    S.op("pe", "matmul", dict(out=self.banks[bank][:rows, :], lhsT=self.o2T[:, c, t * 128:t * 128 + rows],
                                              rhs=self.WR[so][:, c * 2048 + cb * 512:c * 2048 + (cb + 1) * 512], start=(c == 0), stop=(c == 3)),
                         reads=[self.r_o2T[c], self.wres[so]], writes=[self.bres[bank]])
                S.op("act", "activation", dict(out=self.junk[:rows, 0:512], in_=self.banks[bank][:rows, :], func=AF.Square,
                                               accum_out=sm[:rows, 16 + t * 4 + cb:16 + t * 4 + cb + 1]),
                     reads=[self.bres[bank]], writes=[self.r_junk, self.r_small[32 + cb]])

        rs_of = {}

        def stageA1a(ti):
            t, rows = tiles[ti]
            rs_of[ti] = self.rstd_chain(rows, sm[:rows, 16 + t * 4:16 + t * 4 + 4], self.r_small[35], 4)

        def stageA1b(ti):
            t, rows = tiles[ti]
            xt = self.x1[:rows, t, :]
            rs, r_s = rs_of[ti]
            for cb in range(4):
                bank = bank_of(ti, cb)
                S.op("dve", "scalar_tensor_tensor", dict(out=ybuf[cb][:rows, :], in0=self.banks[bank][:rows, :], scalar=rs[:rows, 0:1],
                                                       in1=self.gb[:rows, cb * 512:(cb + 1) * 512], op0=ALU.mult, op1=ALU.mult),
                     reads=[self.bres[bank], r_s, self.r_gb], writes=[self.r_stage[cb]])
            for cb in range(4):
                S.op("pool", "tensor_tensor", dict(out=xt[:, cb * 512:(cb + 1) * 512], in0=xt[:, cb * 512:(cb + 1) * 512], in1=ybuf[cb][:rows, :], op=ALU.add),
                     reads=[self.r_stage[cb], self.r_x1[t]], writes=[self.r_x1[t]])
            S.dma("sp", self.x2_d[t * 128:t * 128 + rows, :], xt, reads=[self.r_x1[t]], writes=[self.r_x2d[t]], lane=l_x2)

        def stageA2(ti):
            t, rows = tiles[ti]
            self.prenorm_a2(self.x1[:rows, t, :], self.r_x1[t], self.ehb[ti % 2], self.r_ehb[ti % 2], rows, scale_eng="dve")

        def stageB(ti):
            t, rows = tiles[ti]
            self.prenorm_b(self.ehb[ti % 2], self.r_ehb[ti % 2], rows, 2, self.h3T[:, :, t * 128:t * 128 + rows], self.r_h3T[t], (0, 1))

        stageMM(0)
        stageA1a(0)
        for it in range(n + 2):
            if it < n:
                stageA1b(it)
            if it + 1 < n:
                stageMM(it + 1)
                stageA1a(it + 1)
            if 0 <= it - 1 < n:
                stageA2(it - 1)
            if 0 <= it - 2 < n:
                stageB(it - 2)

    def phase_ffn(self):
        S = self.S
        acc = self.x1
        self.r_acc = [Res("acc%d" % i) for i in range(9)]
        self.realias(self.r_x1 + self.r_x2d, self.r_acc)
        X0 = self.qT_end - 16384
        hid = [self.sb("hid%d" % i, [128, 4, NT], BF16, X0 + i * 8224) for i in range(2)]
        rl = [self.sb("rl%d" % i, [128, 384], F32, X0 + 16448 + i * 1536) for i in range(2)]
        r_hid = [Res("hid0"), Res("hid1")]; r_rl = [Res("rl0"), Res("rl1")]
        self.realias(self.r_q2T + self.r_o2T + [self.r_q2sb] + self.r_xin + self.xa_small, r_hid + r_rl)
        r_slot2 = Res("slot2b")
        self.realias([self.r_junk, self.r_hb1, self.r_hb2, self.r_memKT, self.r_memV, self.wres[2]], [r_slot2])
        self.wres[2] = r_slot2
        seq = []
        for j in range(16):
            seq.append(("f1", j))
            if j >= 1:
                seq.append(("f2", j - 1))
        seq.append(("f2", 15))
        order = [0, 1, 2]
        tiles = [(t, 128) for t in range(8)] + [(8, 4)]
        loaded = {0: 0}
        nload = [1]

        def ensure_loaded(k):
            while nload[0] <= k and nload[0] < len(seq):
                kind, j = seq[nload[0]]
                slot = order[nload[0] % 3]
                self.load_piece(slot, (self.w_ff1_l if kind == "f1" else self.w_ff2_l)[j])
                loaded[nload[0]] = slot
                nload[0] += 1

        nrl = 0
        nfb = 0
        for k, (kind, j) in enumerate(seq):
            ensure_loaded(k + 1)
            slot = loaded[k]
            hb_ = hid[j % 2]; r_hb_ = r_hid[j % 2]
            if kind == "f1":
                for ft in range(4):
                    banks = [(nfb + gi) % 4 for gi in range(3)]
                    nfb += 3
                    for c in range(16):
                        lw = self.WR[slot][:, c * 512 + ft * 128:c * 512 + (ft + 1) * 128]
                        for gi, (lo, hi) in enumerate(GROUPS):
                            S.op("pe", "matmul", dict(out=self.banks[banks[gi]][:, 0:hi - lo], lhsT=lw, rhs=self.h3T[:, c, lo:hi], start=(c == 0), stop=(c == 15)),
                                 reads=self.r_h3T[lo // 128:(hi + 127) // 128] + [self.wres[slot]], writes=[self.bres[banks[gi]]])
                    for gi, (lo, hi) in enumerate(GROUPS):
                        n = hi - lo
                        r_ = rl[nrl % 2]; r_r = r_rl[nrl % 2]
                        S.op("act", "activation", dict(out=r_[:, 0:n], in_=self.banks[banks[gi]][:, 0:n], func=AF.Relu), reads=[self.bres[banks[gi]]], writes=[r_r])
                        S.op("pool", "tensor_tensor", dict(out=hb_[:, ft, lo:hi], in0=r_[:, 0:n], in1=r_[:, 0:n], op=ALU.mult), reads=[r_r], writes=[r_hb_])
                        nrl += 1
            else:
                for t, rows in tiles:
                    for cb in range(4):
                        bank = 4 + cb
                        for cc in range(4):
                            S.op("pe", "matmul", dict(out=self.banks[bank][:rows, :], lhsT=hb_[:, cc, t * 128:t * 128 + rows],
                                                      rhs=self.WR[slot][:, cc * 2048 + cb * 512:cc * 2048 + (cb + 1) * 512], start=(cc == 0), stop=(cc == 3)),
                                 reads=[r_hb_, self.wres[slot]], writes=[self.bres[bank]])
                        a = acc[:rows, t, cb * 512:(cb + 1) * 512]
                        if j == 0:
                            S.op("act", "activation", dict(out=a, in_=self.banks[bank][:rows, :], func=AF.Copy), reads=[self.bres[bank]], writes=[self.r_acc[t]])
                        else:
                            S.op("dve", "tensor_tensor", dict(out=a, in0=a, in1=self.banks[bank][:rows, :], op=ALU.add),
                                 reads=[self.bres[bank], self.r_acc[t]], writes=[self.r_acc[t]])
                    if j == 15:
                        ti = t
                        if ti == 0:
                            self.final_setup()
                        self.final_F1(ti)
                        if ti >= 1:
                            self.final_F2(ti - 1)
                        if ti == 8:
                            self.final_F2(8)
        self.ffn_tmp = r_hid + r_rl

    def final_setup(self):
        S = self.S
        self.r_gb3 = Res("gb3")
        self.realias([self.r_gb], [self.r_gb3])
        self.r_gb = self.r_gb3
        self.load_gb(2)
        H0 = self.R0 + 3 * 16384
        self.fxin = [self.sb("fxin%d" % i, [128, D], F32, H0 + i * 8192) for i in range(2)]
        self.r_fxin = [Res("fxin0"), Res("fxin1")]
        self.l_fxin = [S.lane("fxin0"), S.lane("fxin1")]
        self.fj = self.sb("fjunk", [128, D], BF16, H0 + 16384)
        self.r_fj = Res("fjunk")
        self.realias(self.r_h3T, self.r_fxin + [self.r_fj])
        self.l_out = [S.lane("yout0"), S.lane("yout1")]
        self.ftiles = [(t, 128) for t in range(8)] + [(8, 4)]
        self.rs_of = {}
        self.final_issue(0)
        self.final_issue(1)

    def final_issue(self, ti):
        t, rows = self.ftiles[ti]
        self.S.dma("sp", self.fxin[ti % 2][:rows, :], self.x2_d[t * 128:t * 128 + rows, :], reads=[self.r_x2d[t]], writes=[self.r_fxin[ti % 2]],
                   lane=self.l_fxin[ti % 2])

    def final_F1(self, ti):
        S = self.S
        t, rows = self.ftiles[ti]
        at = self.x1[:rows, t, :]
        col = 16 + t * 4
        sm = self.smalls
        S.op("act", "activation", dict(out=self.fj[:rows, :], in_=at, func=AF.Square, accum_out=sm[:rows, col:col + 1]),
             reads=[self.r_acc[t]], writes=[self.r_fj, self.r_small[36]])
        self.rs_of[ti] = self.rstd_chain(rows, sm[:rows, col:col + 1], self.r_small[36], 1)

    def final_F2(self, ti):
        S = self.S
        t, rows = self.ftiles[ti]
        at = self.x1[:rows, t, :]
        rs, r_s = self.rs_of[ti]
        S.op("dve", "scalar_tensor_tensor", dict(out=at, in0=at, scalar=rs[:rows, 0:1], in1=self.gb[:rows, :], op0=ALU.mult, op1=ALU.mult),
             reads=[self.r_acc[t], r_s, self.r_gb], writes=[self.r_acc[t]])
        S.op("pool", "tensor_tensor", dict(out=at, in0=at, in1=self.fxin[ti % 2][:rows, :], op=ALU.add),
             reads=[self.r_acc[t], self.r_fxin[ti % 2]], writes=[self.r_acc[t]])
        od = self.o_ys if t == 8 else self.o_y[t * 128:(t + 1) * 128, :]
        self.outs.append(S.dma("sp", od, at, reads=[self.r_acc[t]], lane=self.l_out[ti % 2]))
        if ti + 2 < 9:
            self.final_issue(ti + 2)


def _build(upto="all", debug=()):
    b = Builder(upto, debug)
    return b.build()


_NC_CACHE = {}


def kernel(**inputs):
    maps = _prep(inputs)
    upto = "all"
    if upto not in _NC_CACHE:
        _NC_CACHE[upto] = _build(upto)
    nc = _NC_CACHE[upto]
    res = run_bass_kernel_spmd(nc, maps, core_ids=list(range(NCORES)))
    return _assemble(res.results)


def _assemble(results):
    y = np.zeros((4, 2048, D), np.float32)
    ys = np.zeros((32, 1, D), np.float32)
    pool_p = np.zeros((1, 4, 15, 1024), np.float32)
    pool_s = np.zeros((1, 32, 15, 1024), np.float32)
    k_p = np.zeros((1, 4, 2048, 8, 128), np.float32)
    v_p = np.zeros((1, 4, 2048, 8, 128), np.float32)
    k_s = np.zeros((1, 32, 1, 8, 128), np.float32)
    v_s = np.zeros((1, 32, 1, 8, 128), np.float32)
    mk = np.zeros((1, 4, 256, 4, 128), np.float32)
    mv = np.zeros((1, 4, 256, 4, 128), np.float32)
    for core, r in enumerate(results):
        b, half = core // 2, core % 2
        sl = slice(half * 1024, (half + 1) * 1024)
        y[b, sl] = r["o_y"]
        ys[core * 4:(core + 1) * 4, 0] = r["o_ys"]
        if half == 1:
            pool_p[0, b] = r["o_pool"]
            mk[0, b] = r["o_mk"].reshape(256, 4, 128)
            mv[0, b] = r["o_mv"].reshape(256, 4, 128)
        pool_s[0, core * 4:(core + 1) * 4] = r["o_pools"]
        k_p[0, b, sl] = r["o_k"].reshape(1024, 8, 128)
        v_p[0, b, sl] = r["o_v"].reshape(1024, 8, 128)
        k_s[0, core * 4:(core + 1) * 4, 0] = r["o_ks"].reshape(4, 8, 128)
        v_s[0, core * 4:(core + 1) * 4, 0] = r["o_vs"].reshape(4, 8, 128)
    return (y, ys, pool_p, pool_s, k_p, v_p, k_s, v_s, mk, mv)
```

```python
import numpy as np
import concourse.bass as bass
import concourse.mybir as mybir
from concourse.bass_utils import run_bass_kernel_spmd

F32 = mybir.dt.float32
BF16 = mybir.dt.bfloat16
AF = mybir.ActivationFunctionType
ALU = mybir.AluOpType
AX = mybir.AxisListType

D = 2048
NCORES = 8
NT = 1028
EPS = 1e-6
PAST = 8192
SBUF_BASE = 16640
SBUF_END = 229376
TOP_GUARD = 4096
GROUPS = [(0, 384), (384, 768), (768, 1028)]


class Res:
    __slots__ = ("name", "last_write", "reads")

    def __init__(self, name):
        self.name = name
        self.last_write = None
        self.reads = []


class Lane:
    def __init__(self, sem, name):
        self.sem = sem
        self.name = name
        self.count = 0


class Op:
    __slots__ = ("eng", "fn", "deps", "signals", "count", "lane", "lane_val", "is_dma", "idx")

    def __init__(self, eng, fn, is_dma=False, lane=None):
        self.eng = eng
        self.fn = fn
        self.deps = []
        self.signals = False
        self.count = None
        self.lane = lane
        self.lane_val = None
        self.is_dma = is_dma


class Sched:
    ENGS = ("pe", "act", "dve", "pool", "sp")

    def __init__(self, nc):
        self.nc = nc
        self.handles = {"pe": nc.tensor, "act": nc.scalar, "dve": nc.vector,
                        "pool": nc.gpsimd, "sp": nc.sync}
        self.ops = []
        self.sems = {}
        self._ctx = []

    def _new_sem(self, name):
        cm = self.nc.semaphore(name)
        h = cm.__enter__()
        self._ctx.append(cm)
        return h

    def lane(self, name):
        return Lane(self._new_sem("l_" + name), name)

    def op(self, eng, meth, kw=None, reads=(), writes=(), lane=None, is_dma=False, extra=()):
        fn = (lambda h, meth=meth, kw=dict(kw or {}): getattr(h, meth)(**kw))
        o = Op(eng, fn, is_dma=is_dma, lane=lane)
        deps = []
        for r in reads:
            if r.last_write is not None:
                deps.append(r.last_write)
        for w in writes:
            if w.last_write is not None:
                deps.append(w.last_write)
            deps.extend(w.reads)
        deps.extend(extra)
        seen = set()
        for d in deps:
            if d is o or id(d) in seen:
                continue
            seen.add(id(d))
            if (not d.is_dma) and (not is_dma) and d.eng == eng:
                if eng in ("pe", "sp"):
                    continue
            o.deps.append(d)
            d.signals = True
        for r in reads:
            r.reads.append(o)
        for w in writes:
            w.last_write = o
            w.reads = []
        if is_dma:
            lane.count += 16
            o.lane_val = lane.count
        self.ops.append(o)
        return o

    def dma(self, queue, out, in_, reads=(), writes=(), lane=None, extra=()):
        return self.op(queue, "dma_start", dict(out=out, in_=in_), reads=reads,
                       writes=writes, lane=lane, is_dma=True, extra=extra)

    def wait_all(self, eng, ops):
        o = Op(eng, None)
        for d in ops:
            o.deps.append(d)
            d.signals = True
        self.ops.append(o)

    def emit(self):
        for e in self.ENGS:
            self.sems[e] = self._new_sem("s_" + e)
        cnt = {e: 0 for e in self.ENGS}
        for o in self.ops:
            if (not o.is_dma) and o.signals and o.fn is not None:
                cnt[o.eng] += 1
                o.count = cnt[o.eng]
        waited = {e: {} for e in self.ENGS}
        for o in self.ops:
            h = self.handles[o.eng]
            w = waited[o.eng]
            need = {}
            for d in o.deps:
                if d.is_dma:
                    key, sem, val = ("l", id(d.lane)), d.lane.sem, d.lane_val
                else:
                    key, sem, val = ("e", d.eng), self.sems[d.eng], d.count
                if key not in need or need[key][1] < val:
                    need[key] = (sem, val)
            for key, (sem, val) in need.items():
                if w.get(key, 0) >= val:
                    continue
                h.wait_ge(sem, val)
                w[key] = val
            if o.fn is None:
                continue
            inst = o.fn(h)
            if o.is_dma:
                inst.then_inc(o.lane.sem, 16)
            elif o.signals:
                inst.then_inc(self.sems[o.eng], 1)

    def close(self):
        for cm in reversed(self._ctx):
            cm.__exit__(None, None, None)


def _rope_tab(pos):
    half = 16
    inv = np.power(np.float32(500000.0), -np.arange(half, dtype=np.float32) * np.float32(2.0) / np.float32(32.0)).astype(np.float32)
    ang = pos.astype(np.float32)[:, None] * inv[None, :]
    c = np.cos(ang).astype(np.float32)
    s = np.sin(ang).astype(np.float32)
    return np.concatenate([c, c, s, s], axis=1)


CB = {}
_o = 0
for _n, _w in [("ident", 128), ("ones", 128), ("validp", 128), ("valid16", 128), ("mu_ml", 256),
               ("m16", 64), ("ohb", 512), ("bmask", 1024)]:
    CB[_n] = (_o, _w)
    _o += _w
CB_W = _o
CF = {}
_o = 0
for _n, _w in [("gcols", 64), ("pscale", 8), ("invcnt", 64), ("nhalf", 1), ("coefm", 16), ("udiag", 16),
               ("oh4", 4), ("ln3", 1), ("zero", 1)]:
    CF[_n] = (_o, _w)
    _o += _w
CF_W = _o


def _consts(half):
    cb = np.zeros((128, CB_W), np.float32)
    k = np.arange(128)[:, None]
    q = np.arange(128)[None, :]
    o, w = CB["ident"]; cb[:, o:o + w] = (k == q)
    o, w = CB["ones"]; cb[:, o:o + w] = 1.0
    o, w = CB["validp"]; cb[:, o:o + w] = float(half)
    o, w = CB["valid16"]; cb[:, o:o + w] = 1.0; cb[:64, o:o + w] = float(half)
    o, w = CB["mu_ml"]; cb[:, o:o + 128] = (k >= q); cb[:, o + 128:o + 256] = (k <= q)
    o, w = CB["m16"]
    for hf in range(2):
        qq = np.arange(32)[None, :]
        cb[:, o + hf * 32:o + hf * 32 + 32] = (k <= 64 + 32 * hf + qq)
    o, w = CB["ohb"]
    for s in range(4):
        cb[s, o + s * 128:o + (s + 1) * 128] = 1.0
    o, w = CB["bmask"]
    for h in range(8):
        cb[h, o + h * 128:o + (h + 1) * 128] = 1.0
    cf = np.zeros((128, CF_W), np.float32)
    o, w = CF["nhalf"]; cf[:, o] = -0.5
    o, w = CF["ln3"]; cf[:, o] = np.log(3.0)
    o, w = CF["invcnt"]
    for g, win in enumerate((2, 4, 8, 16)):
        for t in range(16):
            cnt = min(win, half * 1024 + t + 1)
            cf[:, o + g * 16 + t] = 1.0 / cnt
    o, w = CF["coefm"]
    for g, win in enumerate((2, 4, 8, 16)):
        for s in range(4):
            for j in range(15):
                if j >= 16 - win:
                    cf[s * 15 + j, o + g * 4 + s] = 1.0 / win
    o, w = CF["udiag"]
    for g, win in enumerate((2, 4, 8, 16)):
        for s in range(4):
            cf[s, o + g * 4 + s] = 1.0 / win - 1.0
    o, w = CF["oh4"]
    for s in range(4):
        cf[s, o + s] = 1.0
    return cb, cf


def _piece_cols(w, ncols=512):
    K, N = w.shape
    c = K // 128
    a = w.reshape(c, 128, N // ncols, ncols).transpose(2, 1, 0, 3)
    return np.ascontiguousarray(a).reshape(N // ncols, 128, c * ncols)


def _piece_rows(w, nrows=512):
    K, N = w.shape
    cc = nrows // 128
    a = w.reshape(K // nrows, cc, 128, N).transpose(0, 2, 1, 3)
    return np.ascontiguousarray(a).reshape(K // nrows, 128, cc * N)


def _prep(inp):
    f = lambda a: np.ascontiguousarray(np.asarray(a, dtype=np.float32))
    shared = {}
    shared["w_in_l"] = _piece_cols(f(inp["w_in"])[0])
    shared["w_out_l"] = _piece_cols(f(inp["w_out"])[0])
    shared["w_mkv_l"] = _piece_cols(f(inp["w_mem_kv"])[0])
    shared["w_xq_l"] = _piece_cols(f(inp["w_xq"])[0])
    shared["w_xo_l"] = _piece_rows(f(inp["w_xo"])[0])
    shared["w_ff1_l"] = _piece_cols(f(inp["w_ff1"])[0])
    shared["w_ff2_l"] = _piece_rows(f(inp["w_ff2"])[0])
    wp = f(inp["w_pool"])[0]
    shared["w_pool_l"] = np.ascontiguousarray(wp.reshape(4, 2, 128, 256).transpose(2, 0, 1, 3)).reshape(128, 2048)
    shared["grows"] = np.stack([f(inp["g_mix_post"])[0], f(inp["g_mem_post"])[0], f(inp["g_ffn_post"])[0]])
    gc = np.concatenate([f(inp[n])[0].reshape(16, 128).T for n in ("g_mix_pre", "g_mem_pre", "g_ffn_pre", "g_mem_kv")], axis=1)
    psc = f(inp["pool_scale"])[0].reshape(8, 128).T
    xp_all = f(inp["x_prompt"])
    xs_all = f(inp["x_sample"])
    maps = []
    for core in range(NCORES):
        b, half = core // 2, core % 2
        m = dict(shared)
        m["xm"] = np.ascontiguousarray(xp_all[b, half * 1024:(half + 1) * 1024])
        m["xp"] = np.ascontiguousarray(xp_all[b, 0:1024]) if half == 1 else np.zeros((1024, D), np.float32)
        m["xs"] = np.ascontiguousarray(xs_all[core * 4:(core + 1) * 4, 0])
        m["memx"] = np.ascontiguousarray(f(inp["mem_prompt"])[b])
        m["cache_k"] = np.ascontiguousarray(f(inp["cache_attn_k"])[0, core * 4:(core + 1) * 4].reshape(4, 2048, 1024))
        m["cache_v"] = np.ascontiguousarray(f(inp["cache_attn_v"])[0, core * 4:(core + 1) * 4].reshape(4, 2048, 1024))
        m["cmem_k"] = np.ascontiguousarray(f(inp["cache_mem_k"])[0, core * 4:(core + 1) * 4].reshape(4, 256, 512))
        m["cmem_v"] = np.ascontiguousarray(f(inp["cache_mem_v"])[0, core * 4:(core + 1) * 4].reshape(4, 256, 512))
        m["spool"] = np.ascontiguousarray(f(inp["state_pool"])[0, core * 4:(core + 1) * 4].reshape(60, 1024))
        cb, cf = _consts(half)
        o, w = CF["gcols"]; cf[:, o:o + w] = gc
        o, w = CF["pscale"]; cf[:, o:o + w] = psc
        m["cbf"] = cb
        m["cf32"] = cf
        pos_m = half * 1024 + np.arange(1024)
        csm = np.zeros((9 * 128, 64), np.float32)
        csm[:1024] = _rope_tab(pos_m)
        csm[1024:1028] = _rope_tab(np.full(4, PAST))
        m["cs_main"] = csm
        m["cs_prev"] = _rope_tab(np.arange(1024))
        maps.append(m)
    return maps


class Builder:
    def __init__(self, upto="all", debug=()):
        self.upto = upto
        self.debug = set(debug)
        self.nc = nc = bass.Bass("TRN2", target_bir_lowering=False)
        self.S = Sched(nc)
        self.outs = []
        di = lambda n, s: nc.dram_tensor(n, list(s), F32, kind="ExternalInput").ap()
        do = lambda n, s: nc.dram_tensor(n, list(s), F32, kind="ExternalOutput").ap()
        self.xm = di("xm", (1024, D)); self.xp = di("xp", (1024, D)); self.xs = di("xs", (4, D))
        self.memx = di("memx", (256, D))
        self.w_in_l = di("w_in_l", (8, 128, 8192)); self.w_out_l = di("w_out_l", (4, 128, 8192))
        self.w_mkv_l = di("w_mkv_l", (2, 128, 8192)); self.w_xq_l = di("w_xq_l", (1, 128, 8192))
        self.w_xo_l = di("w_xo_l", (1, 128, 8192)); self.w_ff1_l = di("w_ff1_l", (16, 128, 8192))
        self.w_ff2_l = di("w_ff2_l", (16, 128, 8192)); self.w_pool_l = di("w_pool_l", (128, 2048))
        self.grows = di("grows", (3, D))
        self.cache_k = di("cache_k", (4, 2048, 1024)); self.cache_v = di("cache_v", (4, 2048, 1024))
        self.cmem_k = di("cmem_k", (4, 256, 512)); self.cmem_v = di("cmem_v", (4, 256, 512))
        self.spool = di("spool", (60, 1024))
        self.cbf = di("cbf", (128, CB_W)); self.cf32 = di("cf32", (128, CF_W))
        self.cs_main = di("cs_main", (9 * 128, 64)); self.cs_prev = di("cs_prev", (1024, 64))
        self.o_y = do("o_y", (1024, D)); self.o_ys = do("o_ys", (4, D))
        self.o_pool = do("o_pool", (15, 1024)); self.o_pools = do("o_pools", (4, 15, 1024))
        self.o_k = do("o_k", (1024, 1024)); self.o_v = do("o_v", (1024, 1024))
        self.o_ks = do("o_ks", (4, 1024)); self.o_vs = do("o_vs", (4, 1024))
        self.o_mk = do("o_mk", (256, 512)); self.o_mv = do("o_mv", (256, 512))
        self.vs_d = nc.dram_tensor("vs_scr", [2048, 1024], BF16).ap()
        self.x2_d = nc.dram_tensor("x2_scr", [9 * 128, D], F32).ap()
        self.banks = [nc.alloc_psum_tensor("bank%d" % i, [128, 512], F32) for i in range(8)]
        self.bres = [Res("bank%d" % i) for i in range(8)]
        self._names = 0

    def sb(self, name, shape, dt, off):
        nbytes = int(np.prod(shape[1:])) * (4 if dt == F32 else 2)
        assert off % 32 == 0, (name, off)
        assert SBUF_BASE <= off and off + nbytes <= SBUF_END, (name, off, nbytes)
        self._names += 1
        t = self.nc.alloc_sbuf_tensor_at("%s_%d" % (name, self._names), list(shape), dt, offset=off)
        return t

    def bank_bf(self, i):
        return self.banks[i][:].bitcast(BF16)

    def build(self):
        S = self.S
        nc = self.nc
        C0 = SBUF_BASE
        cbt = self.sb("cbt", [128, CB_W], BF16, C0)
        cft = self.sb("cft", [128, CF_W], F32, C0 + 4736)
        smalls = self.sb("smalls", [128, 128], F32, C0 + 4736 + 704)
        cs_t = self.sb("cs_t", [128, 2, 64], F32, C0 + 6144)
        gb = self.sb("gb", [128, D], F32, C0 + 6656)
        R0 = C0 + 6656 + 8192 + 128
        R0 = (R0 + 31) // 32 * 32
        self.cbt, self.cft, self.smalls, self.cs_t, self.gb = cbt, cft, smalls, cs_t, gb
        self.r_const = Res("const"); self.r_gb = Res("gb")
        self.cb = lambda n: cbt[:, CB[n][0]:CB[n][0] + CB[n][1]]
        self.cf = lambda n: cft[:, CF[n][0]:CF[n][0] + CF[n][1]]
        self.WR = [self.sb("wr%d" % i, [128, 8192], BF16, R0 + i * 16384) for i in range(4)]
        self.wres = [Res("wr%d" % i) for i in range(4)]
        self.wlane = [S.lane("wr%d" % i) for i in range(4)]
        A0 = R0 + 4 * 16384
        self.A0 = A0
        self.R0 = R0

        l_const = S.lane("const")
        l_const2 = S.lane("const2")
        self.r_constb = Res("constb")
        c1 = S.dma("pool", cbt[:], self.cbf, writes=[self.r_constb], lane=l_const)
        S.dma("sp", cft[:], self.cf32, writes=[self.r_const], lane=l_const2)
        j = S.op("sp", "nop", {}, reads=[self.r_constb], writes=[self.r_const])
        self.phase_kv()
        if self.upto == "kv":
            return self.finish()
        self.phase_qu()
        if self.upto in ("u", "qu"):
            return self.finish()
        self.phase_mkv()
        if self.upto == "mkv":
            return self.finish()
        self.phase_att()
        if self.upto == "att":
            self.dbg("mixed", [128, 16, NT], self.mixed[:], self.r_mixed)
            return self.finish()
        self.phase_satt()
        self.dbg("mixed", [128, 16, NT], self.mixed[:], self.r_mixed)
        if self.upto == "satt":
            return self.finish()
        self.alloc_epi()
        self.realias([self.wres[2]] , [self.r_junk, self.r_hb1, self.r_hb2])
        self.phase_wo()
        self.phase_epi1()
        if self.upto == "wo":
            self.dbg("x1", [128, 9, D], self.x1[:], self.r_x1)
            return self.finish()
        self.phase_xa()
        self.dbg("o2T", [128, 4, NT], self.o2T[:], self.r_o2T)
        if self.upto == "xa":
            return self.finish()
        self.phase_wxo()
        if self.upto == "wxo":
            self.dbg("x2", [128, 9, D], self.x1[:], self.r_x1)
            return self.finish()
        self.phase_ffn()
        return self.finish()

    def finish(self):
        S = self.S
        S.wait_all("sp", self.outs)
        S.emit()
        S.close()
        return self.nc

    def load_piece(self, slot, src):
        return self.S.dma("pool", self.WR[slot][:], src, writes=[self.wres[slot]], lane=self.wlane[slot])

    def prenorm_tile(self, xin, r_xin, hb, r_hb, rows, gidx, dst, r_dst, tb, ss_col):
        self.prenorm_a(xin, r_xin, hb, r_hb, rows, ss_col)
        self.prenorm_b(hb, r_hb, rows, gidx, dst, r_dst, tb)

    def prenorm_a(self, xin, r_xin, hb, r_hb, rows, ss_col):
        S = self.S
        junk = self.junk
        ss = self.smalls[:, ss_col:ss_col + 1]
        rs = self.smalls[:, ss_col + 1:ss_col + 2]
        r_ss = self.r_small[ss_col // 2]
        S.op("act", "activation", dict(out=junk[:rows, :], in_=xin[:rows, :], func=AF.Square, accum_out=ss[:rows, :]),
             reads=[r_xin], writes=[self.r_junk, r_ss])
        S.op("pool", "tensor_scalar", dict(out=rs[:rows, :], in0=ss[:rows, :], scalar1=float(D * EPS), scalar2=None, op0=ALU.add),
             reads=[r_ss], writes=[r_ss])
        nh = self.cf("nhalf")
        S.op("pool", "tensor_tensor", dict(out=rs[:rows, :], in0=rs[:rows, :], in1=nh[:rows, :], op=ALU.pow),
             reads=[r_ss, self.r_const], writes=[r_ss])
        S.op("dve", "tensor_scalar", dict(out=hb[:rows, :], in0=xin[:rows, :], scalar1=rs[:rows, 0:1], scalar2=float(np.sqrt(D)),
                                          op0=ALU.mult, op1=ALU.mult),
             reads=[r_xin, r_ss], writes=[r_hb])

    def prenorm_b(self, hb, r_hb, rows, gidx, dst, r_dst, tb):
        S = self.S
        ident = self.cb("ident")
        gcol = self.cf("gcols")
        for k in range(2):
            bk = self.bank_bf(tb[k])
            for c8 in range(8):
                c = k * 8 + c8
                S.op("pe", "transpose", dict(out=bk[:, c8 * 128:c8 * 128 + rows], in_=hb[:rows, c * 128:(c + 1) * 128],
                                             identity=ident[:rows, :rows]),
                     reads=[r_hb, self.r_const], writes=[self.bres[tb[k]]])
            g_ap = gcol[:, gidx * 16 + k * 8:gidx * 16 + k * 8 + 8]
            S.op("dve", "tensor_tensor", dict(
                out=dst[:, k * 8:(k + 1) * 8, 0:rows],
                in0=bk.rearrange("p (c t) -> p c t", t=128)[:, :, 0:rows],
                in1=g_ap.unsqueeze(2).to_broadcast([128, 8, rows]), op=ALU.mult),
                 reads=[self.bres[tb[k]], self.r_const], writes=[r_dst])

    def setup_common(self):
        S = self.S
        e = SBUF_END - TOP_GUARD
        e -= 32768; self.kT = self.sb("kT", [128, 8, 2048], BF16, e)
        e -= 32896; self.hT = self.sb("hT", [128, 16, NT], BF16, e)
        e -= 512; self.hTp15 = self.sb("hTp15", [128, 16, 16], BF16, e)
        e -= 2048; self.ksb = self.sb("ksb", [4, 1024], BF16, e)
        e -= 2048; self.vsb = self.sb("vsb", [4, 1024], BF16, e)
        e -= 2048; self.qsb = self.sb("qsb", [4, 1024], BF16, e)
        e -= 4096; self.usf = self.sb("usf", [4, 1024], F32, e)
        self.PERS0 = e
        self.r_small = [Res("small%d" % i) for i in range(40)]
        self.r_kT = [Res("kT%d" % i) for i in range(16)]
        self.r_hT = [Res("hT%d" % i) for i in range(9)]
        self.r_hTp15 = Res("hTp15")
        self.r_ksb = Res("ksb"); self.r_vsb = Res("vsb"); self.r_qsb = Res("qsb"); self.r_usf = Res("usf")
        self.r_junk = Res("junk")
        self.r_cs = [Res("cs0"), Res("cs1")]
        self.l_cs = [S.lane("cs0"), S.lane("cs1")]
        self.r_vs = [Res("vs_t%d" % i) for i in range(16)]
        self.l_vs = [S.lane("vs0"), S.lane("vs1")]
        self.l_st = [S.lane("st%d" % i) for i in range(4)]
        self.r_stage = [Res("st%d" % i) for i in range(4)]
        self.r_kb = [Res("kb0"), Res("kb1")]
        self.r_rtmp = [Res("rt0"), Res("rt1")]
        self.nst = 0
        self.npiece = 0

    def alloc_stage(self, o):
        self.stage = [self.sb("stage%d" % i, [128, 512], F32, o + i * 2048) for i in range(4)]; o += 8192
        self.kb = [self.sb("kb%d" % i, [128, 512], BF16, o + i * 1024) for i in range(2)]; o += 2048
        self.rtmp = [self.sb("rtmp%d" % i, [128, 4, 64], F32, o + i * 1024) for i in range(2)]; o += 2048
        return o

    def tm_proj(self, kind, pslot, lhs, r_lhs, rows, cs_sl, tile_kind, t, half_idx, cs_ap=None, r_csx=None):
        S = self.S
        nst = self.nst
        bank = 2 + (nst % 4)
        ps = self.banks[bank]
        for c in range(16):
            S.op("pe", "matmul", dict(out=ps[:rows, :], lhsT=lhs[:, c, 0:rows], rhs=self.WR[pslot][:, c * 512:(c + 1) * 512],
                                      start=(c == 0), stop=(c == 15)),
                 reads=[r_lhs, self.wres[pslot]], writes=[self.bres[bank]])
        self.flush_pending()
        new_pending = None
        st = nst % 4
        stg = self.stage[st]
        r_st = self.r_stage[st]
        S.op("act", "activation", dict(out=stg[:rows, :], in_=ps[:rows, :], func=AF.Copy),
             reads=[self.bres[bank]], writes=[r_st])
        if kind in ("k", "q"):
            rt = self.rtmp[nst % 2]
            r_rt = self.r_rtmp[nst % 2]
            sv = stg[:rows, :].rearrange("p (h d) -> p h d", d=128)
            if cs_ap is None:
                cs_ap = self.cs_t[:rows, cs_sl, :]
                r_csx = self.r_cs[cs_sl]
            cc = cs_ap[:, 0:32].unsqueeze(1).to_broadcast([rows, 4, 32])
            ss_ = cs_ap[:, 32:64].unsqueeze(1).to_broadcast([rows, 4, 32])
            rr = [r_st, r_csx]
            S.op("dve", "tensor_tensor", dict(out=rt[:rows, :, 0:32], in0=sv[:, :, 0:32], in1=cc, op=ALU.mult), reads=rr, writes=[r_rt])
            S.op("dve", "tensor_tensor", dict(out=rt[:rows, :, 32:64], in0=sv[:, :, 0:32], in1=ss_, op=ALU.mult), reads=rr, writes=[r_rt])
            S.op("dve", "tensor_tensor", dict(out=sv[:, :, 0:16], in0=rt[:rows, :, 0:16], in1=rt[:rows, :, 48:64], op=ALU.subtract),
                 reads=[r_rt], writes=[r_st])
            S.op("dve", "tensor_tensor", dict(out=sv[:, :, 16:32], in0=rt[:rows, :, 16:32], in1=rt[:rows, :, 32:48], op=ALU.add),
                 reads=[r_rt], writes=[r_st])
        cbs = slice(half_idx * 512, half_idx * 512 + 512)
        lane = self.l_st[st]
        if kind == "u":
            if tile_kind == "samp":
                S.op("dve", "tensor_copy", dict(out=self.usf[:, cbs], in_=stg[:4, :]), reads=[r_st], writes=[self.r_usf])
                self.outs.append(S.dma("sp", self.o_pools[:, 14, cbs], stg[:4, :], reads=[r_st], lane=lane))
            else:
                self.outs.append(S.dma("sp", self.o_pool[:, cbs], stg[113:128, :], reads=[r_st], lane=lane))
        elif tile_kind == "samp":
            dst_t, r_t, od = {"k": (self.ksb, self.r_ksb, self.o_ks), "v": (self.vsb, self.r_vsb, self.o_vs),
                              "q": (self.qsb, self.r_qsb, None)}[kind]
            S.op("dve", "tensor_copy", dict(out=dst_t[:, cbs], in_=stg[:4, :]), reads=[r_st], writes=[r_t])
            if od is not None:
                self.outs.append(S.dma("sp", od[:, cbs], stg[:4, :], reads=[r_st], lane=lane))
        else:
            kbs = self.kb[nst % 2]
            r_kbs = self.r_kb[nst % 2]
            S.op("dve", "tensor_copy", dict(out=kbs[:, :], in_=stg[:, :]), reads=[r_st], writes=[r_kbs])
            if tile_kind == "main" and kind in ("k", "v"):
                od = (self.o_k if kind == "k" else self.o_v)[t * 128:(t + 1) * 128, cbs]
                self.outs.append(S.dma("sp", od, stg[:, :], reads=[r_st], lane=lane))
            ext_tile = t if tile_kind == "prev" else 8 + t
            if kind in ("k", "q"):
                tb = 6 + (nst % 2)
                tbk = self.bank_bf(tb)
                if kind == "k":
                    dst = self.kT[:, half_idx * 4:(half_idx + 1) * 4, ext_tile * 128:(ext_tile + 1) * 128]
                    r_d = self.r_kT[ext_tile]
                else:
                    dst = self.qT[:, half_idx * 4:(half_idx + 1) * 4, t * 128:(t + 1) * 128]
                    r_d = self.r_qT[t]

                def deferred(tb=tb, tbk=tbk, kbs=kbs, r_kbs=r_kbs, dst=dst, r_d=r_d):
                    for hh in range(4):
                        S.op("pe", "transpose", dict(out=tbk[:, hh * 128:(hh + 1) * 128], in_=kbs[:, hh * 128:(hh + 1) * 128], identity=self.cb("ident")),
                             reads=[r_kbs, self.r_const], writes=[self.bres[tb]])
                    S.op("act", "activation", dict(out=dst, in_=tbk[:, 0:512].rearrange("p (h t) -> p h t", t=128), func=AF.Copy),
                         reads=[self.bres[tb]], writes=[r_d])
                new_pending = deferred
            else:
                vd = self.vs_d[ext_tile * 128:(ext_tile + 1) * 128, cbs]
                S.dma("sp", vd, kbs[:, :], reads=[r_kbs], writes=[self.r_vs[ext_tile]], lane=self.l_vs[nst % 2])
        self.pending = new_pending
        self.nst += 1

    def flush_pending(self):
        if getattr(self, "pending", None) is not None:
            p = self.pending
            self.pending = None
            p()

    def load_cs(self, sl, tile_kind, t, rows):
        if tile_kind == "prev":
            cs_src = self.cs_prev[t * 128:(t + 1) * 128, :]
        elif tile_kind == "main":
            cs_src = self.cs_main[t * 128:(t + 1) * 128, :]
        else:
            cs_src = self.cs_main[1024:1028, :]
        self.S.dma("sp", self.cs_t[:rows, sl, :], cs_src, writes=[self.r_cs[sl]], lane=self.l_cs[sl])

    def phase_kv(self):
        S = self.S
        self.setup_common()
        o = self.A0
        xin = [self.sb("xin%d" % i, [128, D], F32, o + i * 8192) for i in range(2)]; o += 16384
        hb = [self.sb("hb%d" % i, [128, D], BF16, o + i * 4096) for i in range(2)]; o += 8192
        self.junk = self.sb("junk", [128, D], BF16, o); o += 4096
        hTt = [self.sb("hTt%d" % i, [128, 16, 128], BF16, o + i * 4096) for i in range(2)]; o += 8192
        o = self.alloc_stage(o)
        assert o <= self.PERS0, (o, self.PERS0)
        r_xin = [Res("xin0"), Res("xin1")]; r_hb = [Res("hb0"), Res("hb1")]
        r_hTt = [Res("hTt0"), Res("hTt1")]
        l_xin = [S.lane("xin0"), S.lane("xin1")]

        for i, pj in enumerate((4, 5, 6, 7)):
            self.load_piece(i, self.w_in_l[pj])
        self.npiece = 4

        tiles = [("prev", t, 128) for t in range(8)] + [("main", t, 128) for t in range(8)] + [("samp", 8, 4)]

        def dst_of(ti):
            kind, t, rows = tiles[ti]
            if kind == "prev":
                return hTt[ti % 2], r_hTt[ti % 2]
            return self.hT[:, :, t * 128:t * 128 + rows], self.r_hT[t]

        def issue_x(ti):
            kind, t, rows = tiles[ti]
            src = {"prev": self.xp, "main": self.xm}.get(kind)
            src = self.xs if kind == "samp" else src[t * 128:(t + 1) * 128, :]
            S.dma("sp", xin[ti % 2][:rows, :], src, writes=[r_xin[ti % 2]], lane=l_xin[ti % 2])

        def pa(ti):
            kind, t, rows = tiles[ti]
            self.prenorm_a(xin[ti % 2], r_xin[ti % 2], hb[ti % 2], r_hb[ti % 2], rows, (ti % 4) * 2)

        def pb(ti):
            kind, t, rows = tiles[ti]
            dst, r_dst = dst_of(ti)
            self.prenorm_b(hb[ti % 2], r_hb[ti % 2], rows, 0, dst, r_dst, (0, 1))
            if kind == "prev" and t == 7:
                S.op("pool", "tensor_copy", dict(out=self.hTp15[:, :, 0:15], in_=dst[:, :, 113:128]),
                     reads=[r_dst], writes=[self.r_hTp15])

        n = len(tiles)
        issue_x(0); issue_x(1)
        self.load_cs(0, *[tiles[0][0], tiles[0][1], tiles[0][2]])
        self.load_cs(1, *[tiles[1][0], tiles[1][1], tiles[1][2]])
        pa(0); pb(0)
        for ti, (kind, t, rows) in enumerate(tiles):
            sl = ti % 2
            dst, r_dst = dst_of(ti)
            if ti + 1 < n:
                pa(ti + 1)
            if ti + 2 < n:
                issue_x(ti + 2)
            self.tm_proj("k", 0, dst, r_dst, rows, sl, kind, t, 0)
            if ti + 1 < n:
                pb(ti + 1)
            self.tm_proj("k", 1, dst, r_dst, rows, sl, kind, t, 1)
            if ti + 2 < n:
                k2, t2, rows2 = tiles[ti + 2]
                self.load_cs(sl, k2, t2, rows2)
            self.tm_proj("v", 2, dst, r_dst, rows, sl, kind, t, 0)
            self.tm_proj("v", 3, dst, r_dst, rows, sl, kind, t, 1)
        self.flush_pending()
        self.kv_tmp_res = r_xin + r_hb + r_hTt + [self.r_junk]

    def next_piece(self, src):
        slot = self.npiece % 3
        self.npiece += 1
        self.load_piece(slot, src)
        return slot

    def realias(self, old, new):
        bar = []
        for r in old:
            if r.last_write is not None:
                bar.append(r.last_write)
            bar.extend(r.reads)
        if not bar:
            return
        j = self.S.op("sp", "nop", {}, extra=bar)
        for r in new:
            r.last_write = j
            r.reads = []

    def dbg(self, name, shape, src, reads):
        if name not in self.debug:
            return
        t = self.nc.dram_tensor("dbg_" + name, list(shape), src.dtype, kind="ExternalOutput").ap()
        self.outs.append(self.S.dma("sp", t, src, reads=reads, lane=self.S.lane("dbg_" + name)))

    def phase_qu(self):
        S = self.S
        R0 = self.R0
        o = R0 + 3 * 16384
        self.mixed = self.sb("mixed", [128, 16, NT], BF16, o); o += 32896
        self.r_mixed = [Res("mixed%d" % i) for i in range(16)]
        o = (o + 31) // 32 * 32
        o = self.alloc_stage(o)
        X0 = o
        uT = [self.sb("uT0", [128, 1044], F32, X0)] * 2
        sa = self.sb("sa", [128, 1044], F32, X0 + 4192)
        sbb = self.sb("sbb", [128, 1044], F32, X0 + 8384)
        pooled = [self.sb("pooled%d" % i, [128, 2, NT], BF16, X0 + 12576 + i * 4128) for i in range(2)]
        GB0 = SBUF_BASE + 6656
        wpool = self.sb("wpool", [128, 2048], BF16, GB0)
        spool_t = self.sb("spool_t", [60, 1024], F32, GB0 + 4096)
        t16 = self.sb("t16", [128, 16], F32, X0 + 20832)
        xend = X0 + 20896
        assert xend <= self.PERS0, (xend, self.PERS0)
        self.qT = self.sb("qT", [128, 8, 1024], BF16, X0)
        self.qT_end = X0 + 16384
        self.r_qT = [Res("qT%d" % i) for i in range(8)]
        r_uT = [Res("uT0")] * 2
        r_sa = Res("sa"); r_sb = Res("sb")
        r_pooled = [Res("pooled0"), Res("pooled1")]
        r_wpool = Res("wpool"); r_spool = Res("spool"); r_t16 = Res("t16")
        self.gb_alias = [r_wpool, r_spool]
        old = list(self.kv_tmp_res) + self.r_stage + self.r_kb + self.r_rtmp + [self.wres[3]]
        self.r_stage = [Res("st%d" % i) for i in range(4)]
        self.r_kb = [Res("kb0"), Res("kb1")]
        self.r_rtmp = [Res("rt0"), Res("rt1")]
        newres = self.r_mixed + self.r_stage + self.r_kb + self.r_rtmp + [r_uT[0], r_sa, r_sb] + r_pooled + [r_t16]
        self.realias(old, newres)
        l_wp = S.lane("wpool"); l_sp = S.lane("spool"); l_po = S.lane("pools_out")
        S.dma("pool", wpool[:], self.w_pool_l, writes=[r_wpool], lane=l_wp)
        S.dma("sp", spool_t[:], self.spool, writes=[r_spool], lane=l_sp)
        for s_ in range(4):
            self.outs.append(S.dma("sp", self.o_pools[s_, 0:14, :], spool_t[s_ * 15 + 1:s_ * 15 + 15, :], reads=[r_spool], lane=l_po))

        groups4 = GROUPS + [None]
        gct = 0
        pending_map = []
        for piece in range(2):
            slot = (0, 1)[piece]
            self.load_piece(slot, self.w_in_l[piece])
            self.tm_proj("u", slot, self.hT[:, :, 7 * 128:8 * 128], self.r_hT[7], 128, 0, "main", 7, piece)
            self.tm_proj("u", slot, self.hT[:, :, 1024:1028], self.r_hT[8], 4, 0, "samp", 8, piece)
            for ct in range(4):
                g = gct // 2
                cc = gct % 2
                win = 2 << g
                u = uT[gct % 2]
                r_u = r_uT[gct % 2]
                bset = (2, 3, 4) if gct % 2 == 0 else (5, 6, 7)
                for c in range(16):
                    lw = self.WR[slot][:, c * 512 + ct * 128:c * 512 + (ct + 1) * 128]
                    for gi, grp in enumerate(groups4):
                        if grp is None:
                            S.op("pe", "matmul", dict(out=self.banks[bset[2]][:, 300:315], lhsT=lw, rhs=self.hTp15[:, c, 0:15], start=False, stop=(c == 15),
                                                      skip_group_check=True),
                                 reads=[self.r_hTp15, self.wres[slot]], writes=[self.bres[bset[2]]])
                        else:
                            n = grp[1] - grp[0]
                            rr = self.r_hT[grp[0] // 128:(grp[1] + 127) // 128]
                            S.op("pe", "matmul", dict(out=self.banks[bset[gi]][:, 0:n], lhsT=lw, rhs=self.hT[:, c, grp[0]:grp[1]], start=(c == 0), stop=(c == 15),
                                                      skip_group_check=True),
                                 reads=rr + [self.wres[slot]], writes=[self.bres[bset[gi]]])
                for gi, grp in enumerate(groups4):
                    if grp is None:
                        S.op("act", "activation", dict(out=u[:, 0:15], in_=self.banks[bset[2]][:, 300:315], func=AF.Copy),
                             reads=[self.bres[bset[2]]], writes=[r_u])
                    else:
                        n = grp[1] - grp[0]
                        S.op("act", "activation", dict(out=u[:, 15 + grp[0]:15 + grp[1]], in_=self.banks[bset[gi]][:, 0:n], func=AF.Copy),
                             reads=[self.bres[bset[gi]]], writes=[r_u])
                while pending_map:
                    pending_map.pop(0)()
                cur, r_cur = u, r_u
                bufs = [(sa, r_sa), (sbb, r_sb)]
                sh = 1
                k = 0
                while sh < win:
                    nxt, r_nxt = bufs[k % 2]
                    lo = 2 * sh - 1
                    S.op("dve", "tensor_tensor", dict(out=nxt[:, lo:1039], in0=cur[:, lo:1039], in1=cur[:, lo - sh:1039 - sh], op=ALU.add),
                         reads=[r_cur], writes=[r_nxt])
                    cur, r_cur = nxt, r_nxt
                    sh *= 2
                    k += 1
                pl = pooled[g % 2]
                r_pl = r_pooled[g % 2]
                S.op("dve", "scalar_tensor_tensor", dict(out=pl[:, cc, 16:1024], in0=cur[:, 31:1039], scalar=float(1.0 / win), in1=u[:, 31:1039],
                                                       op0=ALU.mult, op1=ALU.subtract),
                     reads=[r_cur, r_u], writes=[r_pl])
                ic = self.cf("invcnt")[:, g * 16:(g + 1) * 16]
                S.op("dve", "tensor_tensor", dict(out=t16[:, :], in0=cur[:, 15:31], in1=ic, op=ALU.mult), reads=[r_cur, self.r_const], writes=[r_t16])
                S.op("dve", "tensor_tensor", dict(out=pl[:, cc, 0:16], in0=t16[:, :], in1=u[:, 15:31], op=ALU.subtract), reads=[r_t16, r_u], writes=[r_pl])
                cs_ = slice(gct * 128, (gct + 1) * 128)
                S.op("pe", "matmul", dict(out=self.banks[1][:, 400:404], lhsT=spool_t[0:60, cs_], rhs=self.cf("coefm")[0:60, g * 4:(g + 1) * 4],
                                          start=True, stop=False, skip_group_check=True), reads=[r_spool, self.r_const], writes=[self.bres[1]])
                S.op("pe", "matmul", dict(out=self.banks[1][:, 400:404], lhsT=self.usf[0:4, cs_], rhs=self.cf("udiag")[0:4, g * 4:(g + 1) * 4],
                                          start=False, stop=True, skip_group_check=True), reads=[self.r_usf, self.r_const], writes=[self.bres[1]])
                S.op("act", "activation", dict(out=pl[:, cc, 1024:1028], in_=self.banks[1][:, 400:404], func=AF.Copy),
                     reads=[self.bres[1]], writes=[r_pl])
                if cc == 1:
                  def group_map(g=g, pl=pl, r_pl=r_pl):
                    for et in range(2):
                        for gi, grp in enumerate(GROUPS):
                            n = grp[1] - grp[0]
                            bk = 2 + gi if False else (0 + (gi + et * 3) % 2)
                            for c2 in range(2):
                                off = g * 512 + c2 * 256 + et * 128
                                S.op("pe", "matmul", dict(out=self.banks[bk][:, 0:n], lhsT=wpool[:, off:off + 128], rhs=pl[:, c2, grp[0]:grp[1]],
                                                          start=(c2 == 0), stop=(c2 == 1)),
                                     reads=[r_wpool, r_pl], writes=[self.bres[bk]])
                            mt = 2 * g + et
                            S.op("dve", "tensor_scalar", dict(out=self.mixed[:, mt, grp[0]:grp[1]], in0=self.banks[bk][:, 0:n],
                                                              scalar1=self.cf("pscale")[:, mt:mt + 1], scalar2=None, op0=ALU.mult),
                                 reads=[self.bres[bk], self.r_const], writes=[self.r_mixed[mt]])
                  pending_map.append(group_map)
                gct += 1
        while pending_map:
            pending_map.pop(0)()
        if self.upto == "u":
            return
        self.realias([r_uT[0], r_sa, r_sb] + r_pooled + [r_t16], self.r_qT)
        tiles = [("main", t, 128) for t in range(8)] + [("samp", 8, 4)]
        csq = self.sb("csq", [128, 9, 64], F32, self.PERS0)
        r_csq = Res("csq")
        self.r_csq = r_csq
        self.realias([self.r_usf], [r_csq])
        S.dma("sp", csq[:, :, :], self.cs_main.rearrange("(t p) c -> p t c", p=128), writes=[r_csq], lane=S.lane("csq"))
        for piece in range(2):
            slot = (2, 0)[piece]
            self.load_piece(slot, self.w_in_l[2 + piece])
            for ti, (kind, t, rows) in enumerate(tiles):
                lhs = self.hT[:, :, t * 128:t * 128 + rows]
                self.tm_proj("q", slot, lhs, self.r_hT[t], rows, 0, kind, t, piece, cs_ap=csq[:rows, t, :], r_csx=r_csq)
        self.flush_pending()
        self.qu_tmp = r_uT + [r_sa, r_sb] + r_pooled + [r_t16]


    def phase_mkv(self):
        S = self.S
        hT0 = SBUF_END - TOP_GUARD - 32768 - 32896
        o = hT0
        mxin = self.sb("mxin", [128, D], F32, o); o += 8192
        mhb = self.sb("mhb", [128, D], BF16, o); o += 4096
        self.junk = self.sb("junk2", [128, D], BF16, o); o += 4096
        hmT = self.sb("hmT", [128, 16, 256], BF16, o); o += 8192
        mxin2 = self.sb("mxin2", [128, D], F32, o); o += 8192
        assert o <= hT0 + 32896 + 512
        g0 = self.R0 + 16384 + 24576
        self.memKT = self.sb("memKT", [128, 4, 256], BF16, g0)
        self.memV = self.sb("memV", [128, 2, 512], BF16, g0 + 2048)
        r_mxin = Res("mxin"); r_mxin2 = Res("mxin2"); r_mhb = Res("mhb"); self.r_junk = Res("junk2"); r_hmT = [Res("hmT0"), Res("hmT1")]
        self.r_memKT = Res("memKT"); self.r_memV = Res("memV")
        self.realias(self.r_hT + [self.r_hTp15], [r_mxin, r_mxin2, r_mhb, self.r_junk] + r_hmT)
        self.realias([self.wres[2]], [self.r_memKT, self.r_memV])
        l_mx = S.lane("mxin")
        sK = 1; sV = 0
        self.load_piece(sK, self.w_mkv_l[0])
        self.load_piece(sV, self.w_mkv_l[1])
        mx = [(mxin, r_mxin, l_mx), (mxin2, r_mxin2, S.lane("mxin2"))]
        for t in range(2):
            S.dma("sp", mx[t][0][:, :], self.memx[t * 128:(t + 1) * 128, :], writes=[mx[t][1]], lane=mx[t][2])
        for t in range(2):
            self.prenorm_tile(mx[t][0], mx[t][1], mhb, r_mhb, 128, 3, hmT[:, :, t * 128:(t + 1) * 128], r_hmT[t], (0, 1), 8 + 2 * t)
        for t in range(2):
            for kind, slot in (("mk", sK), ("mv", sV)):
                nst = self.nst
                bank = 2 + (nst % 4)
                ps = self.banks[bank]
                for c in range(16):
                    S.op("pe", "matmul", dict(out=ps[:, :], lhsT=hmT[:, c, t * 128:(t + 1) * 128], rhs=self.WR[slot][:, c * 512:(c + 1) * 512],
                                              start=(c == 0), stop=(c == 15)), reads=[r_hmT[t], self.wres[slot]], writes=[self.bres[bank]])
                st = nst % 4
                stg = self.stage[st]
                S.op("act", "activation", dict(out=stg[:, :], in_=ps[:, :], func=AF.Copy), reads=[self.bres[bank]], writes=[self.r_stage[st]])
                od = (self.o_mk if kind == "mk" else self.o_mv)[t * 128:(t + 1) * 128, :]
                self.outs.append(S.dma("sp", od, stg[:, :], reads=[self.r_stage[st]], lane=self.l_st[st]))
                if kind == "mv":
                    S.op("dve", "tensor_copy", dict(out=self.memV[:, t, :], in_=stg[:, :]), reads=[self.r_stage[st]], writes=[self.r_memV])
                self.nst += 1
        for hh in range(4):
            bank = 2 + (hh % 4)
            for c in range(16):
                S.op("pe", "matmul", dict(out=self.banks[bank][:, 0:256], lhsT=self.WR[sK][:, c * 512 + hh * 128:c * 512 + (hh + 1) * 128],
                                          rhs=hmT[:, c, 0:256], start=(c == 0), stop=(c == 15)),
                     reads=r_hmT + [self.wres[sK]], writes=[self.bres[bank]])
            S.op("act", "activation", dict(out=self.memKT[:, hh, :], in_=self.banks[bank][:, 0:256], func=AF.Copy),
                 reads=[self.bres[bank]], writes=[self.r_memKT])
        self.mkv_tmp = [r_mxin, r_mxin2, r_mhb, self.r_junk] + r_hmT

    def phase_att(self):
        S = self.S
        hT0 = SBUF_END - TOP_GUARD - 32768 - 32896
        vA = self.sb("vbufA", [128, 48, 256], BF16, hT0)
        pT = [self.sb("pT%d" % i, [128, 512], BF16, hT0 + 24576 + i * 1024) for i in range(3)]
        rden = [self.sb("rden%d" % i, [128, 512], F32, hT0 + 24576 + 3072 + i * 2048) for i in range(2)]
        assert hT0 + 24576 + 3072 + 4096 <= hT0 + 32896 + 512
        vB = self.sb("vbufB", [128, 48, 256], BF16, self.R0 + 16384)
        vbuf = [vA, vB]
        r_vA = Res("vbufA"); r_pT = [Res("pT%d" % i) for i in range(3)]; r_rden = [Res("rden0"), Res("rden1")]
        self.realias(self.mkv_tmp, [r_vA] + r_pT + r_rden)
        r_vAp = [Res("vA_nat"), Res("vA_d4"), Res("vA_d16")]
        r_vBp = [Res("vB_nat"), Res("vB_d4"), Res("vB_d16")]
        self.realias([r_vA], r_vAp)
        self.realias([self.wres[1], self.wres[2]], r_vBp)
        r_vbuf = [r_vAp, r_vBp]
        l_v = [[S.lane("vbufA%d" % i) for i in range(3)], [S.lane("vbufB%d" % i) for i in range(3)]]
        scale = float(128 ** -0.5)
        ones = self.cb("ones"); validp = self.cb("validp"); valid16 = self.cb("valid16")
        mu_ml = self.cb("mu_ml"); m16 = self.cb("m16")
        mask4 = mu_ml.unsqueeze(1).to_broadcast([128, 2, 256])
        all_vs = self.r_vs

        def load_v(hp):
            b = hp % 2
            cols = slice(hp * 256, (hp + 1) * 256)
            vsrc = self.vs_d[:, cols]
            wr = r_vbuf[b]
            S.dma("sp", vbuf[b][:, 0:16, :], vsrc.rearrange("(b p) c -> p b c", p=128), reads=all_vs, writes=[wr[0]], lane=l_v[b][0])
            S.dma("sp", vbuf[b][:, 16:32, :].rearrange("p (r b) c -> p r b c", r=4),
                  vsrc.rearrange("(b l r) c -> l r b c", l=128, r=4), reads=all_vs, writes=[wr[1]], lane=l_v[b][1])
            S.dma("sp", vbuf[b][:, 32:48, :], vsrc.rearrange("(l r) c -> l r c", r=16), reads=all_vs, writes=[wr[2]], lane=l_v[b][2])

        load_v(0)
        load_v(1)
        self.load_piece(0, self.w_out_l[0])
        batches = []
        for hp in range(4):
            for hh in range(2):
                for hf in range(2):
                    unit = (hp * 2 + hh) * 2 + hf
                    h = hp * 2 + hh
                    hc = slice(hh * 128, (hh + 1) * 128)
                    vb = vbuf[hp % 2]; r_vb = r_vbuf[hp % 2]
                    ob = 4 + (unit % 2); db = 6 + (unit % 2)
                    q0 = 512 * hf
                    ubatches = []
                    for ip in range(2):
                        smm = []; pvs = []
                        for ii in range(2):
                            i = 4 * hf + 2 * ip + ii
                            for j, kb in enumerate((7 + i, 8 + i)):
                                col = (ii * 2 + j) * 128
                                smm.append((slice(col, col + 128), self.kT[:, h, kb * 128:(kb + 1) * 128], self.qT[:, h, i * 128:(i + 1) * 128],
                                            [self.r_kT[kb], self.r_qT[i]]))
                                pvs.append((vb[:, kb, hc], validp if kb < 8 else ones, slice(col, col + 128),
                                            slice((i - 4 * hf) * 128, (i - 4 * hf + 1) * 128)))
                        ubatches.append((smm, pvs, mask4, 256, 0))
                    qb = 2 + hf
                    for rp in range(2):
                        smm = []; pvs = []
                        for ri in range(2):
                            r4 = rp * 2 + ri
                            for j, blk in enumerate((qb - 1, qb)):
                                col = (ri * 2 + j) * 128
                                smm.append((slice(col, col + 128), self.kT[:, h, 512 * blk + r4:512 * blk + 512:4], self.qT[:, h, q0 + r4:q0 + 512:4],
                                            self.r_kT[4 * blk:4 * blk + 4] + self.r_qT[4 * hf:4 * hf + 4]))
                                pvs.append((vb[:, 16 + r4 * 4 + blk, hc], validp if blk < 2 else ones, slice(col, col + 128), slice(r4, 512, 4)))
                        ubatches.append((smm, pvs, mask4, 256, 1))
                    smm = []; pvs = []
                    for r16 in range(16):
                        smm.append((slice(r16 * 32, (r16 + 1) * 32), self.kT[:, h, r16:2048:16], self.qT[:, h, q0 + r16:q0 + 512:16],
                                    self.r_kT + self.r_qT[4 * hf:4 * hf + 4]))
                        pvs.append((vb[:, 32 + r16, hc], valid16, slice(r16 * 32, (r16 + 1) * 32), slice(r16, 512, 16)))
                    m16b = m16[:, hf * 32:(hf + 1) * 32].unsqueeze(1).to_broadcast([128, 16, 32])
                    ubatches.append((smm, pvs, m16b, 32, 2))
                    for bi, ub in enumerate(ubatches):
                        batches.append(dict(unit=unit, h=h, q0=q0, ob=ob, db=db, r_vb=[r_vb[ub[4]]], first=(bi == 0), last=(bi == len(ubatches) - 1),
                                            hp=hp, smm=ub[0], pvs=ub[1], mask=ub[2], shape3=ub[3]))

        def emit_S(bi):
            B = batches[bi]
            sb_ = bi % 4
            for (cols, lhsT, rhs, rr) in B["smm"]:
                S.op("pe", "matmul", dict(out=self.banks[sb_][:, cols], lhsT=lhsT, rhs=rhs, start=True, stop=True, skip_group_check=True),
                     reads=rr, writes=[self.bres[sb_]])
            p = pT[bi % 3]; r_p = r_pT[bi % 3]
            S.op("act", "activation", dict(out=p[:, :], in_=self.banks[sb_][:, :], func=AF.Exp, scale=scale), reads=[self.bres[sb_]], writes=[r_p])
            pv_ = p[:, :].rearrange("p (a b) -> p a b", b=B["shape3"])
            S.op("pool", "tensor_tensor", dict(out=pv_, in0=pv_, in1=B["mask"], op=ALU.mult), reads=[r_p, self.r_const], writes=[r_p])

        def emit_PV(bi):
            B = batches[bi]
            p = pT[bi % 3]; r_p = r_pT[bi % 3]
            ob, db = B["ob"], B["db"]
            O = self.banks[ob]; DEN = self.banks[db]
            for k, (vl, dl, pcols, ocols) in enumerate(B["pvs"]):
                st = B["first"] and k == 0
                S.op("pe", "matmul", dict(out=O[:, ocols], lhsT=vl, rhs=p[:, pcols], start=st, stop=False, skip_group_check=True),
                     reads=[r_p] + B["r_vb"], writes=[self.bres[ob]])
                S.op("pe", "matmul", dict(out=DEN[:, ocols], lhsT=dl, rhs=p[:, pcols], start=st, stop=False, skip_group_check=True),
                     reads=[r_p, self.r_const], writes=[self.bres[db]])
            if B["last"]:
                unit = B["unit"]
                rd = rden[unit % 2]; r_rd = r_rden[unit % 2]
                S.op("dve", "reciprocal", dict(out=rd[:, :], in_=DEN[:, :]), reads=[self.bres[db]], writes=[r_rd])
                S.op("dve", "tensor_tensor", dict(out=self.mixed[:, 8 + B["h"], B["q0"]:B["q0"] + 512], in0=O[:, :], in1=rd[:, :], op=ALU.mult),
                     reads=[self.bres[ob], r_rd], writes=[self.r_mixed[8 + B["h"]]])
                if unit % 4 == 3 and B["hp"] + 2 < 4:
                    load_v(B["hp"] + 2)

        nbt = len(batches)
        emit_S(0)
        for bi in range(nbt):
            if bi + 1 < nbt:
                emit_S(bi + 1)
            emit_PV(bi)
        self.realias(r_vBp, [self.wres[1], self.wres[2]])
        self.att_tmp = [r_vA] + r_vAp + r_pT + r_rden


    def sample_attn(self, s_, W, nh, sets, qrow, r_qrow, selfkv, dest, r_dest, sc):
        self.sample_p1(s_, W, nh, sets, qrow, r_qrow, selfkv, sc, load_v=True)
        self.sample_p2(s_, W, nh, len(sets), selfkv, dest, r_dest, sc)

    def sample_p1(self, s_, W, nh, sets, qrow, r_qrow, selfkv, sc, load_v=True, qb=(0, 1)):
        S = self.S
        scale = float(128 ** -0.5)
        Kc, Vc, r_K, r_V, l_K, l_V = sc["Kc"], sc["Vc"], sc["r_K"], sc["r_V"], sc["l_K"], sc["l_V"]
        qbc, tmp, scr, e = sc["qbc"], sc["tmp"], sc["scr"], sc["e"]
        r_qbc, r_tmp, r_scr, r_e = sc["r_qbc"], sc["r_tmp"], sc["r_scr"], sc["r_e"]
        ns = len(sets)
        for i, (ks, vs) in enumerate(sets):
            S.dma("pool", Kc[i][:, 0:W], ks, writes=[r_K[i]], lane=l_K[i])
            if load_v:
                S.dma("pool", Vc[i][:, 0:W], vs, writes=[r_V[i]], lane=l_V[i])
        nhalf = W // 512
        ohb = self.cb("ohb")
        for hf in range(nhalf):
            S.op("pe", "matmul", dict(out=self.banks[qb[hf]][:, :], lhsT=ohb[0:4, s_ * 128:(s_ + 1) * 128], rhs=qrow[0:4, hf * 512:(hf + 1) * 512],
                                      start=True, stop=True), reads=[r_qrow, self.r_const], writes=[self.bres[qb[hf]]])
            S.op("act", "activation", dict(out=qbc[:, hf * 512:(hf + 1) * 512], in_=self.banks[qb[hf]][:, :], func=AF.Copy),
                 reads=[self.bres[qb[hf]]], writes=[r_qbc])
        for i in range(ns):
            S.op("dve", "tensor_tensor", dict(out=tmp[:, 0:W], in0=Kc[i][:, 0:W], in1=qbc[:, 0:W], op=ALU.mult), reads=[r_K[i], r_qbc], writes=[r_tmp])
            S.op("dve", "tensor_reduce", dict(out=scr[:, i * nh:(i + 1) * nh], in_=tmp[:, 0:W].rearrange("p (h d) -> p h d", d=128),
                                              axis=AX.X, op=ALU.add), reads=[r_tmp], writes=[r_scr])
        S.op("act", "activation", dict(out=e[:, 0:ns * nh], in_=scr[:, 0:ns * nh], func=AF.Exp, scale=scale), reads=[r_scr], writes=[r_e])
        if selfkv is not None:
            e4f, r_e4f, e4s, r_e4s, vrow, r_vrow = selfkv
            S.op("dve", "tensor_scalar", dict(out=e4s[0:4, 0:nh], in0=e4f[0:4, 0:nh], scalar1=self.cf("oh4")[0:4, s_:s_ + 1], scalar2=None, op0=ALU.mult),
                 reads=[r_e4f, self.r_const], writes=[r_e4s])

    def sample_p2(self, s_, W, nh, ns, selfkv, dest, r_dest, sc):
        S = self.S
        Vc, r_V = sc["Vc"], sc["r_V"]
        e, po, od, odn, rd = sc["e"], sc["po"], sc["od"], sc["odn"], sc["rd"]
        r_e, r_po, r_od = sc["r_e"], sc["r_po"], sc["r_od"]
        nhalf = W // 512
        ones = self.cb("ones")
        if selfkv is not None:
            e4f, r_e4f, e4s, r_e4s, vrow, r_vrow = selfkv
        for hf in range(nhalf):
            bk = 2 + hf
            for i in range(ns):
                S.op("pe", "matmul", dict(out=self.banks[bk][0:nh, :], lhsT=e[:, i * nh:(i + 1) * nh], rhs=Vc[i][:, hf * 512:(hf + 1) * 512],
                                          start=(i == 0), stop=(i == ns - 1 and selfkv is None)),
                     reads=[r_e, r_V[i]], writes=[self.bres[bk]])
            if selfkv is not None:
                S.op("pe", "matmul", dict(out=self.banks[bk][0:nh, :], lhsT=e4s[0:4, 0:nh], rhs=vrow[0:4, hf * 512:(hf + 1) * 512], start=False, stop=True),
                     reads=[r_e4s, r_vrow], writes=[self.bres[bk]])
        for i in range(ns):
            S.op("pe", "matmul", dict(out=self.banks[4][0:nh, 0:1], lhsT=e[:, i * nh:(i + 1) * nh], rhs=ones[:, 0:1],
                                      start=(i == 0), stop=(i == ns - 1 and selfkv is None)), reads=[r_e, self.r_const], writes=[self.bres[4]])
        if selfkv is not None:
            S.op("pe", "matmul", dict(out=self.banks[4][0:nh, 0:1], lhsT=e4s[0:4, 0:nh], rhs=ones[0:4, 0:1], start=False, stop=True),
                 reads=[r_e4s, self.r_const], writes=[self.bres[4]])
        bmask = self.cb("bmask")
        for hf in range(nhalf):
            S.op("dve", "tensor_tensor", dict(out=po[0:nh, hf * 512:(hf + 1) * 512], in0=self.banks[2 + hf][0:nh, :],
                                              in1=bmask[0:nh, hf * 512:(hf + 1) * 512], op=ALU.mult),
                 reads=[self.bres[2 + hf], self.r_const], writes=[r_po])
        S.op("dve", "tensor_reduce", dict(out=od[0:nh, 0:128], in_=po[0:nh, 0:W].rearrange("p (h d) -> p d h", d=128), axis=AX.X, op=ALU.add),
             reads=[r_po], writes=[r_od])
        S.op("dve", "reciprocal", dict(out=rd[0:nh, 0:1], in_=self.banks[4][0:nh, 0:1]), reads=[self.bres[4]], writes=[r_od])
        S.op("dve", "tensor_scalar", dict(out=odn[0:nh, 0:128], in0=od[0:nh, 0:128], scalar1=rd[0:nh, 0:1], scalar2=None, op0=ALU.mult),
             reads=[r_od], writes=[r_od])
        tb = self.bank_bf(5)
        S.op("pe", "transpose", dict(out=tb[:, 0:nh], in_=odn[0:nh, 0:128], identity=self.cb("ident")[0:nh, 0:nh]),
             reads=[r_od, self.r_const], writes=[self.bres[5]])
        S.op("act", "activation", dict(out=dest, in_=tb[:, 0:nh], func=AF.Copy), reads=[self.bres[5]], writes=r_dest)

    def sample_scratch(self, o, W, ns, tag, ext=None):
        S = self.S
        sc = {}
        sc["Kc"] = [self.sb("Kc%s%d" % (tag, i), [128, W], BF16, o + i * 2 * W) for i in range(ns)]; o += ns * 2 * W
        sc["Vc"] = [self.sb("Vc%s%d" % (tag, i), [128, W], BF16, o + i * 2 * W) for i in range(ns)]; o += ns * 2 * W
        sc["qbc"] = self.sb("qbc" + tag, [128, W], BF16, o); o += 2 * W
        if ext is None:
            sc["tmp"] = self.sb("tmp" + tag, [128, W], F32, o); o += 4 * W
            sc["po"] = self.sb("po" + tag, [8, W], F32, o); o += 4 * W
        else:
            sc["tmp"], sc["po"] = ext[0], ext[1]
        sc["scr"] = self.sb("scr" + tag, [128, 32], F32, o); o += 128
        sc["e"] = self.sb("e" + tag, [128, 32], BF16, o); o += 64
        sc["od"] = self.sb("od" + tag, [8, 128], F32, o); o += 512
        sc["odn"] = self.sb("odn" + tag, [8, 128], BF16, o); o += 256
        sc["rd"] = self.sb("rd" + tag, [8, 8], F32, o); o += 32
        sc["r_K"] = [Res("Kc%d" % i) for i in range(ns)]; sc["r_V"] = [Res("Vc%d" % i) for i in range(ns)]
        sc["l_K"] = [S.lane("Kc%s%d" % (tag, i)) for i in range(ns)]; sc["l_V"] = [S.lane("Vc%s%d" % (tag, i)) for i in range(ns)]
        for n in ("qbc", "tmp", "scr", "e", "po", "od"):
            sc["r_" + n] = Res(n + tag)
        if ext is not None:
            sc["r_tmp"], sc["r_po"] = ext[2], ext[3]
        sc["allres"] = sc["r_K"] + sc["r_V"] + [sc["r_" + n] for n in ("qbc", "scr", "e", "od")] + ([sc["r_tmp"], sc["r_po"]] if ext is None else [])
        return sc, o

    def phase_satt(self):
        S = self.S
        kT0 = SBUF_END - TOP_GUARD - 32768
        sc0, o = self.sample_scratch(kT0, 1024, 3, "d")
        sc1 = dict(sc0)
        sc1["qbc"] = self.sb("qbcd1", [128, 1024], BF16, o); o += 2048
        sc1["scr"] = self.sb("scrd1", [128, 32], F32, o); o += 128
        sc1["e"] = self.sb("ed1", [128, 32], BF16, o); o += 64
        sc1["od"] = self.sb("odd1", [8, 128], F32, o); o += 512
        sc1["odn"] = self.sb("odnd1", [8, 128], BF16, o); o += 256
        sc1["rd"] = self.sb("rdd1", [8, 8], F32, o); o += 32
        for nme in ("qbc", "scr", "e", "od"):
            sc1["r_" + nme] = Res(nme + "d1")
        e4s1 = self.sb("e4s1", [4, 8], BF16, o); o += 32
        tmp4 = self.sb("tmp4", [4, 1024], F32, o); o += 4096
        ssf = self.sb("ssf", [4, 8], F32, o); o += 32
        e4f = self.sb("e4f", [4, 8], F32, o); o += 32
        e4s0 = self.sb("e4s", [4, 8], BF16, o); o += 32
        assert o <= SBUF_END - TOP_GUARD, o
        r_t4 = Res("tmp4"); r_e4f = Res("e4f"); r_e4s = [Res("e4s0"), Res("e4s1")]
        extra = [sc1["r_" + nme] for nme in ("qbc", "scr", "e", "od")]
        self.realias(self.r_kT, sc0["allres"] + extra + [r_t4, r_e4f] + r_e4s)
        scale = float(128 ** -0.5)
        S.op("dve", "tensor_tensor", dict(out=tmp4[:, :], in0=self.qsb[:, :], in1=self.ksb[:, :], op=ALU.mult),
             reads=[self.r_qsb, self.r_ksb], writes=[r_t4])
        S.op("dve", "tensor_reduce", dict(out=ssf[:, :], in_=tmp4[:, :].rearrange("p (h d) -> p h d", d=128), axis=AX.X, op=ALU.add),
             reads=[r_t4], writes=[r_t4])
        S.op("act", "activation", dict(out=e4f[:, :], in_=ssf[:, :], func=AF.Exp, scale=scale, bias=self.cf("ln3")[0:4, :]),
             reads=[r_t4, self.r_const], writes=[r_e4f])
        scs = [sc0, sc1]
        e4ss = [e4s0, e4s1]

        def sets_of(s_):
            return [(self.cache_k[s_, 1920:2048, :], self.cache_v[s_, 1920:2048, :]),
                    (self.cache_k[s_, 1536:2048:4, :], self.cache_v[s_, 1536:2048:4, :]),
                    (self.cache_k[s_, 0:2048:16, :], self.cache_v[s_, 0:2048:16, :])]

        def selfkv(s_):
            return (e4f, r_e4f, e4ss[s_ % 2], r_e4s[s_ % 2], self.vsb, self.r_vsb)

        def load_vsets(s_):
            sc = scs[s_ % 2]
            for i, (ks, vs) in enumerate(sets_of(s_)):
                S.dma("pool", sc["Vc"][i][:, 0:1024], vs, writes=[sc["r_V"][i]], lane=sc["l_V"][i])

        def P1(s_):
            self.sample_p1(s_, 1024, 8, sets_of(s_), self.qsb, self.r_qsb, selfkv(s_), scs[s_ % 2], load_v=False,
                           qb=(0, 1) if s_ % 2 == 0 else (6, 7))

        def P2(s_):
            self.sample_p2(s_, 1024, 8, 3, selfkv(s_), self.mixed[:, 8:16, 1024 + s_], self.r_mixed[8:16], scs[s_ % 2])

        load_vsets(0)
        P1(0)
        for s_ in range(4):
            if s_ + 1 < 4:
                P1(s_ + 1)
            P2(s_)
            if s_ + 1 < 4:
                load_vsets(s_ + 1)
        self.satt_tmp = sc0["allres"] + extra + [r_t4, r_e4f] + r_e4s

    def rstd_chain(self, rows, src_cols, r_src, nred):
        S = self.S
        n = self.nepi
        self.nepi += 1
        sm = self.smalls
        base = 64 + (n % 16) * 2
        ssc = sm[:, base:base + 1]
        rs = sm[:, base + 1:base + 2]
        r_s = self.r_small[16 + n % 16]
        if nred > 1:
            S.op("dve", "tensor_reduce", dict(out=ssc[:rows, :], in_=src_cols, axis=AX.X, op=ALU.add), reads=[r_src], writes=[r_s])
            src, rr = ssc[:rows, :], [r_s]
        else:
            src, rr = src_cols, [r_src]
        S.op("pool", "tensor_scalar", dict(out=rs[:rows, :], in0=src, scalar1=float(1.0 / D), scalar2=float(EPS), op0=ALU.mult, op1=ALU.add), reads=rr, writes=[r_s])
        S.op("pool", "tensor_tensor", dict(out=rs[:rows, :], in0=rs[:rows, :], in1=self.cf("nhalf")[:rows, :], op=ALU.pow), reads=[r_s, self.r_const], writes=[r_s])
        return rs, r_s

    def prenorm_a2(self, xt, r_x, hb, r_hb, rows, scale_eng="act"):
        S = self.S
        n = self.nepi
        col = 100 + (n % 8)
        ss = self.smalls[:, col:col + 1]
        r_ss = self.r_small[8 + n % 8]
        S.op("act", "activation", dict(out=self.junk[:rows, :], in_=xt, func=AF.Square, accum_out=ss[:rows, :]), reads=[r_x], writes=[self.r_junk, r_ss])
        rs, r_s = self.rstd_chain(rows, ss[:rows, :], r_ss, 1)
        if scale_eng == "act":
            S.op("act", "activation", dict(out=hb[:rows, :], in_=xt, func=AF.Copy, scale=rs[:rows, 0:1]), reads=[r_x, r_s], writes=[r_hb])
        else:
            S.op("dve", "tensor_scalar", dict(out=hb[:rows, :], in0=xt, scalar1=rs[:rows, 0:1], scalar2=None, op0=ALU.mult), reads=[r_x, r_s], writes=[r_hb])

    def load_gb(self, idx):
        self.S.dma("sp", self.gb[:, :], self.grows[idx:idx + 1, :].partition_broadcast(128), writes=[self.r_gb], lane=self.l_gb)

    def phase_wo(self):
        S = self.S
        X1_0 = SBUF_END - TOP_GUARD - 73728
        self.x1 = self.sb("x1", [128, 9, D], F32, X1_0)
        self.r_x1 = [Res("x1_%d" % i) for i in range(9)]
        dead = self.att_tmp + self.satt_tmp + self.r_kT + [self.r_ksb, self.r_vsb, self.r_qsb, self.r_usf] + self.r_qT
        self.realias(dead, self.r_x1)
        self.nepi = 0
        self.l_gb = S.lane("gb")
        self.realias(self.gb_alias, [self.r_gb])
        self.load_gb(0)
        sm = self.smalls
        tiles = [(t, 128) for t in range(8)] + [(8, 4)]
        slots = [0, 1, 0, 1]
        self.wo_last_mm = {}
        for cb in range(4):
            slot = slots[cb]
            if cb > 0:
                self.load_piece(slot, self.w_out_l[cb])
            for t, rows in tiles:
                nst = self.nst
                bank = 2 + (nst % 4)
                ps = self.banks[bank]
                for c in range(16):
                    mm_last = S.op("pe", "matmul", dict(out=ps[:rows, :], lhsT=self.mixed[:, c, t * 128:t * 128 + rows], rhs=self.WR[slot][:, c * 512:(c + 1) * 512],
                                                        start=(c == 0), stop=(c == 15)), reads=[self.r_mixed[c], self.wres[slot]], writes=[self.bres[bank]])
                ydst = self.x1[:rows, t, cb * 512:(cb + 1) * 512]
                S.op("act", "activation", dict(out=ydst, in_=ps[:rows, :], func=AF.Copy), reads=[self.bres[bank]], writes=[self.r_x1[t]])
                S.op("act", "activation", dict(out=self.junkw[:rows, 0:512], in_=ps[:rows, :], func=AF.Square, accum_out=sm[:rows, 16 + t * 4 + cb:16 + t * 4 + cb + 1]),
                     reads=[self.bres[bank]], writes=[self.r_x1[t], self.r_junkw])
                self.nst += 1
                if cb == 3:
                    self.wo_last_mm[t] = mm_last
                    if t == 0:
                        self.epi1_setup()
                    else:
                        self.epi1_step(t - 1)

    def alloc_epi(self):
        S = self.S
        X0 = self.qT_end - 16384
        self.xin = [self.sb("exin%d" % i, [128, D], F32, X0 + i * 8192) for i in range(2)]
        self.r_xin = [Res("exin0"), Res("exin1")]
        self.l_xin = [S.lane("exin0"), S.lane("exin1")]
        self.realias(self.r_qT, self.r_xin)
        s2 = self.R0 + 2 * 16384
        self.junk = self.sb("ejunk", [128, D], BF16, s2)
        self.hb1 = self.sb("ehb", [128, D], BF16, self.R0 + 16384 + 24576 + 4096)
        self.hb2 = self.sb("ehb2", [128, D], BF16, self.R0 + 16384 + 24576 - 4096)
        self.ehb = [self.hb1, self.hb2]
        self.r_junk = Res("ejunk"); self.r_hb1 = Res("ehb"); self.r_hb2 = Res("ehb2")
        self.r_ehb = [self.r_hb1, self.r_hb2]
        self.junkw = self.junk
        self.r_junkw = self.r_junk

    def epi1_setup(self):
        S = self.S
        tiles = [(t, 128) for t in range(8)] + [(8, 4)]
        self.e1_tiles = tiles
        sm = self.smalls
        n = len(tiles)

        def issue(ti):
            t, rows = tiles[ti]
            src = self.xs if t == 8 else self.xm[t * 128:(t + 1) * 128, :]
            S.dma("sp", self.xin[ti % 2][:rows, :], src, writes=[self.r_xin[ti % 2]], lane=self.l_xin[ti % 2])
        issue(0); issue(1)
        self.h2T = self.sb("h2T", [128, 16, NT], BF16, self.R0 + 3 * 16384)
        self.r_h2T = [Res("h2T%d" % i) for i in range(9)]
        self.load_piece(0, self.w_xq_l[0])
        rs_of = {}

        def stageA1a(ti):
            t, rows = tiles[ti]
            rs_of[ti] = self.rstd_chain(rows, sm[:rows, 16 + t * 4:16 + t * 4 + 4], self.r_x1[t], 4)

        def stageA1b(ti):
            t, rows = tiles[ti]
            xt = self.x1[:rows, t, :]
            rs, r_s = rs_of[ti]
            S.op("dve", "scalar_tensor_tensor", dict(out=xt, in0=xt, scalar=rs[:rows, 0:1], in1=self.gb[:rows, :], op0=ALU.mult, op1=ALU.mult),
                 reads=[self.r_x1[t], r_s, self.r_gb], writes=[self.r_x1[t]])
            xi = self.xin[ti % 2]
            S.op("dve", "tensor_tensor", dict(out=xt[:, 0:768], in0=xt[:, 0:768], in1=xi[:rows, 0:768], op=ALU.add),
                 reads=[self.r_x1[t], self.r_xin[ti % 2]], writes=[self.r_x1[t]])
            S.op("pool", "tensor_tensor", dict(out=xt[:, 768:D], in0=xt[:, 768:D], in1=xi[:rows, 768:D], op=ALU.add),
                 reads=[self.r_x1[t], self.r_xin[ti % 2]], writes=[self.r_x1[t]])

        def stageA2(ti):
            t, rows = tiles[ti]
            self.prenorm_a2(self.x1[:rows, t, :], self.r_x1[t], self.ehb[ti % 2], self.r_ehb[ti % 2], rows)

        def stageB(ti):
            t, rows = tiles[ti]
            self.r_h2T[t].last_write = self.wo_last_mm[t]
            self.prenorm_b(self.ehb[ti % 2], self.r_ehb[ti % 2], rows, 1, self.h2T[:, :, t * 128:t * 128 + rows], self.r_h2T[t], (0, 1))

        def step(it):
            if it + 1 < n:
                stageA1a(it + 1)
            if it < n:
                stageA1b(it)
            if it + 2 < n:
                issue(it + 2)
            if 0 <= it - 1 < n:
                stageA2(it - 1)
            if 0 <= it - 2 < n:
                stageB(it - 2)
        self._e1_step = step
        stageA1a(0)

    def epi1_step(self, it):
        self._e1_step(it)

    def phase_epi1(self):
        n = len(self.e1_tiles)
        for it in range(n - 1, n + 2):
            self._e1_step(it)
        self.load_piece(1, self.w_xo_l[0])

    def phase_xa(self):
        S = self.S
        scale = float(128 ** -0.5)
        X0 = self.qT_end - 16384
        self.q2T = self.sb("q2T", [128, 4, NT], BF16, X0)
        self.o2T = self.sb("o2T", [128, 4, NT], BF16, X0 + 8224)
        o = X0 + 16448
        pT2 = [self.sb("pT2_%d" % i, [128, 2, 384], BF16, o + i * 1536) for i in range(2)]; o += 3072
        rd2 = [self.sb("rd2_%d" % i, [128, 384], F32, o + i * 1536) for i in range(2)]; o += 3072
        assert o <= self.PERS0
        self.q2sb = self.sb("q2sb", [4, 512], BF16, self.PERS0)
        self.r_q2T = [Res("q2T%d" % i) for i in range(4)]
        self.r_o2T = [Res("o2T%d" % i) for i in range(4)]
        r_pT2 = [Res("pT2_0"), Res("pT2_1")]; r_rd2 = [Res("rd2_0"), Res("rd2_1")]
        self.r_q2sb = Res("q2sb")
        self.realias(self.r_xin + [self.r_csq], self.r_q2T + self.r_o2T + r_pT2 + r_rd2 + [self.r_q2sb])
        self.xa_small = r_pT2 + r_rd2
        sq = 0
        for hh in range(4):
            for c in range(16):
                lw = self.WR[sq][:, c * 512 + hh * 128:c * 512 + (hh + 1) * 128]
                for gi, (lo, hi) in enumerate(GROUPS):
                    S.op("pe", "matmul", dict(out=self.banks[2 + gi][:, 0:hi - lo], lhsT=lw, rhs=self.h2T[:, c, lo:hi], start=(c == 0), stop=(c == 15)),
                         reads=self.r_h2T[lo // 128:(hi + 127) // 128] + [self.wres[sq]], writes=[self.bres[2 + gi]])
            for gi, (lo, hi) in enumerate(GROUPS):
                S.op("act", "activation", dict(out=self.q2T[:, hh, lo:hi], in_=self.banks[2 + gi][:, 0:hi - lo], func=AF.Copy),
                     reads=[self.bres[2 + gi]], writes=[self.r_q2T[hh]])
        for c in range(16):
            S.op("pe", "matmul", dict(out=self.banks[5][0:4, :], lhsT=self.h2T[:, c, 1024:1028], rhs=self.WR[sq][:, c * 512:(c + 1) * 512],
                                      start=(c == 0), stop=(c == 15)), reads=[self.r_h2T[8], self.wres[sq]], writes=[self.bres[5]])
        S.op("act", "activation", dict(out=self.q2sb[:, :], in_=self.banks[5][0:4, :], func=AF.Copy), reads=[self.bres[5]], writes=[self.r_q2sb])
        self.load_piece(0, self.w_ff1_l[0])
        ones = self.cb("ones")
        iters = [(hh, lo, hi) for hh in range(4) for (lo, hi) in [(0, 384), (384, 768), (768, 1024)]]

        def xS(nb):
            hh, lo, hi = iters[nb]
            n = hi - lo
            p = pT2[nb % 2]; r_p = r_pT2[nb % 2]
            for kt in range(2):
                bk = (nb % 2) * 2 + kt
                S.op("pe", "matmul", dict(out=self.banks[bk][:, 0:n], lhsT=self.memKT[:, hh, kt * 128:(kt + 1) * 128], rhs=self.q2T[:, hh, lo:hi],
                                          start=True, stop=True), reads=[self.r_memKT, self.r_q2T[hh]], writes=[self.bres[bk]])
                S.op("act", "activation", dict(out=p[:, kt, 0:n], in_=self.banks[bk][:, 0:n], func=AF.Exp, scale=scale),
                     reads=[self.bres[bk]], writes=[r_p])

        def xPV(nb):
            hh, lo, hi = iters[nb]
            n = hi - lo
            p = pT2[nb % 2]; r_p = r_pT2[nb % 2]
            ob = 4 + (nb % 2); db = 6 + (nb % 2)
            for kt in range(2):
                S.op("pe", "matmul", dict(out=self.banks[ob][:, 0:n], lhsT=self.memV[:, kt, hh * 128:(hh + 1) * 128], rhs=p[:, kt, 0:n],
                                          start=(kt == 0), stop=(kt == 1)), reads=[self.r_memV, r_p], writes=[self.bres[ob]])
            for kt in range(2):
                S.op("pe", "matmul", dict(out=self.banks[db][:, 0:n], lhsT=ones, rhs=p[:, kt, 0:n], start=(kt == 0), stop=(kt == 1)),
                     reads=[self.r_const, r_p], writes=[self.bres[db]])
            rd = rd2[nb % 2]; r_rd = r_rd2[nb % 2]
            S.op("dve", "reciprocal", dict(out=rd[:, 0:n], in_=self.banks[db][:, 0:n]), reads=[self.bres[db]], writes=[r_rd])
            S.op("dve", "tensor_tensor", dict(out=self.o2T[:, hh, lo:hi], in0=self.banks[ob][:, 0:n], in1=rd[:, 0:n], op=ALU.mult),
                 reads=[self.bres[ob], r_rd], writes=[self.r_o2T[hh]])

        xS(0)
        for nb in range(len(iters)):
            if nb + 1 < len(iters):
                xS(nb + 1)
            xPV(nb)
        GB0 = SBUF_BASE + 6656
        sc, o = self.sample_scratch(GB0, 512, 2, "m", ext=(self.stage[0], self.stage[1], self.r_stage[0], self.r_stage[1]))
        assert o <= GB0 + 8192 + 128, o
        self.realias([self.r_gb], sc["allres"])
        for s_ in range(4):
            sets = [(self.cmem_k[s_, 0:128, :], self.cmem_v[s_, 0:128, :]), (self.cmem_k[s_, 128:256, :], self.cmem_v[s_, 128:256, :])]
            self.sample_attn(s_, 512, 4, sets, self.q2sb, self.r_q2sb, None, self.o2T[:, 0:4, 1024 + s_], self.r_o2T, sc)
        self.xa_tmp = sc["allres"]

    def phase_wxo(self):
        S = self.S
        self.r_gb2 = Res("gb2")
        self.realias(self.xa_tmp, [self.r_gb2])
        self.r_gb = self.r_gb2
        self.load_gb(1)
        self.h3T = self.h2T
        self.r_h3T = [Res("h3T%d" % i) for i in range(9)]
        self.realias(self.r_h2T, self.r_h3T)
        l_x2 = S.lane("x2spill")
        self.r_x2d = [Res("x2d%d" % i) for i in range(9)]
        sm = self.smalls
        tiles = [(t, 128) for t in range(8)] + [(8, 4)]
        n = len(tiles)
        so = 1
        ybuf = self.stage

        def bank_of(ti, cb):
            return 2 + (ti * 4 + cb) % 6

        def stageMM(ti):
            t, rows = tiles[ti]
            for cb in range(4):
                bank = bank_of(ti, cb)
                for c in range(4):
                    S.op("pe", "matmul", dict(out=self.banks[bank][:rows, :], lhsT=self.o2T[:, c, t * 128:t * 128 + rows],
                                              rhs=self.WR[so][:, c * 2048 + cb * 512:c * 2048 + (cb + 1) * 512], start=(c == 0), stop=(c == 3)),
                         reads=[self.r_o2T[c], self.wres[so]], writes=[self.bres[bank]])
                S.op("act", "activation", dict(out=self.junk[:rows, 0:512], in_=self.banks[bank][:rows, :], func=AF.Square,
                                               accum_out=sm[:rows, 16 + t * 4 + cb:16 + t * 4 + cb + 1]),
                     reads=[self.bres[bank]], writes=[self.r_junk, self.r_small[32 + cb]])

        rs_of = {}

        def stageA1a(ti):
            t, rows = tiles[ti]
            rs_of[ti] = self.rstd_chain(rows, sm[:rows, 16 + t * 4:16 + t * 4 + 4], self.r_small[35], 4)

        def stageA1b(ti):
            t, rows = tiles[ti]
            xt = self.x1[:rows, t, :]
            rs, r_s = rs_of[ti]
            for cb in range(4):
                bank = bank_of(ti, cb)
                S.op("dve", "scalar_tensor_tensor", dict(out=ybuf[cb][:rows, :], in0=self.banks[bank][:rows, :], scalar=rs[:rows, 0:1],
                                                       in1=self.gb[:rows, cb * 512:(cb + 1) * 512], op0=ALU.mult, op1=ALU.mult),
                     reads=[self.bres[bank], r_s, self.r_gb], writes=[self.r_stage[cb]])
            for cb in range(4):
                S.op("pool", "tensor_tensor", dict(out=xt[:, cb * 512:(cb + 1) * 512], in0=xt[:, cb * 512:(cb + 1) * 512], in1=ybuf[cb][:rows, :], op=ALU.add),
                     reads=[self.r_stage[cb], self.r_x1[t]], writes=[self.r_x1[t]])
            S.dma("sp", self.x2_d[t * 128:t * 128 + rows, :], xt, reads=[self.r_x1[t]], writes=[self.r_x2d[t]], lane=l_x2)

        def stageA2(ti):
            t, rows = tiles[ti]
            self.prenorm_a2(self.x1[:rows, t, :], self.r_x1[t], self.ehb[ti % 2], self.r_ehb[ti % 2], rows, scale_eng="dve")

        def stageB(ti):
            t, rows = tiles[ti]
            self.prenorm_b(self.ehb[ti % 2], self.r_ehb[ti % 2], rows, 2, self.h3T[:, :, t * 128:t * 128 + rows], self.r_h3T[t], (0, 1))

        stageMM(0)
        stageA1a(0)
        for it in range(n + 2):
            if it < n:
                stageA1b(it)
            if it + 1 < n:
                stageMM(it + 1)
                stageA1a(it + 1)
            if 0 <= it - 1 < n:
                stageA2(it - 1)
            if 0 <= it - 2 < n:
                stageB(it - 2)

    def phase_ffn(self):
        S = self.S
        acc = self.x1
        self.r_acc = [Res("acc%d" % i) for i in range(9)]
        self.realias(self.r_x1 + self.r_x2d, self.r_acc)
        X0 = self.qT_end - 16384
        hid = [self.sb("hid%d" % i, [128, 4, NT], BF16, X0 + i * 8224) for i in range(2)]
        rl = [self.sb("rl%d" % i, [128, 384], F32, X0 + 16448 + i * 1536) for i in range(2)]
        r_hid = [Res("hid0"), Res("hid1")]; r_rl = [Res("rl0"), Res("rl1")]
        self.realias(self.r_q2T + self.r_o2T + [self.r_q2sb] + self.r_xin + self.xa_small, r_hid + r_rl)
        r_slot2 = Res("slot2b")
        self.realias([self.r_junk, self.r_hb1, self.r_hb2, self.r_memKT, self.r_memV, self.wres[2]], [r_slot2])
        self.wres[2] = r_slot2
        seq = []
        for j in range(16):
            seq.append(("f1", j))
            if j >= 1:
                seq.append(("f2", j - 1))
        seq.append(("f2", 15))
        order = [0, 1, 2]
        tiles = [(t, 128) for t in range(8)] + [(8, 4)]
        loaded = {0: 0}
        nload = [1]

        def ensure_loaded(k):
            while nload[0] <= k and nload[0] < len(seq):
                kind, j = seq[nload[0]]
                slot = order[nload[0] % 3]
                self.load_piece(slot, (self.w_ff1_l if kind == "f1" else self.w_ff2_l)[j])
                loaded[nload[0]] = slot
                nload[0] += 1

        nrl = 0
        nfb = 0
        for k, (kind, j) in enumerate(seq):
            ensure_loaded(k + 1)
            slot = loaded[k]
            hb_ = hid[j % 2]; r_hb_ = r_hid[j % 2]
            if kind == "f1":
                for ft in range(4):
                    banks = [(nfb + gi) % 4 for gi in range(3)]
                    nfb += 3
                    for c in range(16):
                        lw = self.WR[slot][:, c * 512 + ft * 128:c * 512 + (ft + 1) * 128]
                        for gi, (lo, hi) in enumerate(GROUPS):
                            S.op("pe", "matmul", dict(out=self.banks[banks[gi]][:, 0:hi - lo], lhsT=lw, rhs=self.h3T[:, c, lo:hi], start=(c == 0), stop=(c == 15)),
                                 reads=self.r_h3T[lo // 128:(hi + 127) // 128] + [self.wres[slot]], writes=[self.bres[banks[gi]]])
                    for gi, (lo, hi) in enumerate(GROUPS):
                        n = hi - lo
                        r_ = rl[nrl % 2]; r_r = r_rl[nrl % 2]
                        S.op("act", "activation", dict(out=r_[:, 0:n], in_=self.banks[banks[gi]][:, 0:n], func=AF.Relu), reads=[self.bres[banks[gi]]], writes=[r_r])
                        S.op("pool", "tensor_tensor", dict(out=hb_[:, ft, lo:hi], in0=r_[:, 0:n], in1=r_[:, 0:n], op=ALU.mult), reads=[r_r], writes=[r_hb_])
                        nrl += 1
            else:
                for t, rows in tiles:
                    for cb in range(4):
                        bank = 4 + cb
                        for cc in range(4):
                            S.op("pe", "matmul", dict(out=self.banks[bank][:rows, :], lhsT=hb_[:, cc, t * 128:t * 128 + rows],
                                                      rhs=self.WR[slot][:, cc * 2048 + cb * 512:cc * 2048 + (cb + 1) * 512], start=(cc == 0), stop=(cc == 3)),
                                 reads=[r_hb_, self.wres[slot]], writes=[self.bres[bank]])
                        a = acc[:rows, t, cb * 512:(cb + 1) * 512]
                        if j == 0:
                            S.op("act", "activation", dict(out=a, in_=self.banks[bank][:rows, :], func=AF.Copy), reads=[self.bres[bank]], writes=[self.r_acc[t]])
                        else:
                            S.op("dve", "tensor_tensor", dict(out=a, in0=a, in1=self.banks[bank][:rows, :], op=ALU.add),
                                 reads=[self.bres[bank], self.r_acc[t]], writes=[self.r_acc[t]])
                    if j == 15:
                        ti = t
                        if ti == 0:
                            self.final_setup()
                        self.final_F1(ti)
                        if ti >= 1:
                            self.final_F2(ti - 1)
                        if ti == 8:
                            self.final_F2(8)
        self.ffn_tmp = r_hid + r_rl

    def final_setup(self):
        S = self.S
        self.r_gb3 = Res("gb3")
        self.realias([self.r_gb], [self.r_gb3])
        self.r_gb = self.r_gb3
        self.load_gb(2)
        H0 = self.R0 + 3 * 16384
        self.fxin = [self.sb("fxin%d" % i, [128, D], F32, H0 + i * 8192) for i in range(2)]
        self.r_fxin = [Res("fxin0"), Res("fxin1")]
        self.l_fxin = [S.lane("fxin0"), S.lane("fxin1")]
        self.fj = self.sb("fjunk", [128, D], BF16, H0 + 16384)
        self.r_fj = Res("fjunk")
        self.realias(self.r_h3T, self.r_fxin + [self.r_fj])
        self.l_out = [S.lane("yout0"), S.lane("yout1")]
        self.ftiles = [(t, 128) for t in range(8)] + [(8, 4)]
        self.rs_of = {}
        self.final_issue(0)
        self.final_issue(1)

    def final_issue(self, ti):
        t, rows = self.ftiles[ti]
        self.S.dma("sp", self.fxin[ti % 2][:rows, :], self.x2_d[t * 128:t * 128 + rows, :], reads=[self.r_x2d[t]], writes=[self.r_fxin[ti % 2]],
                   lane=self.l_fxin[ti % 2])

    def final_F1(self, ti):
        S = self.S
        t, rows = self.ftiles[ti]
        at = self.x1[:rows, t, :]
        col = 16 + t * 4
        sm = self.smalls
        S.op("act", "activation", dict(out=self.fj[:rows, :], in_=at, func=AF.Square, accum_out=sm[:rows, col:col + 1]),
             reads=[self.r_acc[t]], writes=[self.r_fj, self.r_small[36]])
        self.rs_of[ti] = self.rstd_chain(rows, sm[:rows, col:col + 1], self.r_small[36], 1)

    def final_F2(self, ti):
        S = self.S
        t, rows = self.ftiles[ti]
        at = self.x1[:rows, t, :]
        rs, r_s = self.rs_of[ti]
        S.op("dve", "scalar_tensor_tensor", dict(out=at, in0=at, scalar=rs[:rows, 0:1], in1=self.gb[:rows, :], op0=ALU.mult, op1=ALU.mult),
             reads=[self.r_acc[t], r_s, self.r_gb], writes=[self.r_acc[t]])
        S.op("pool", "tensor_tensor", dict(out=at, in0=at, in1=self.fxin[ti % 2][:rows, :], op=ALU.add),
             reads=[self.r_acc[t], self.r_fxin[ti % 2]], writes=[self.r_acc[t]])
        od = self.o_ys if t == 8 else self.o_y[t * 128:(t + 1) * 128, :]
        self.outs.append(S.dma("sp", od, at, reads=[self.r_acc[t]], lane=self.l_out[ti % 2]))
        if ti + 2 < 9:
            self.final_issue(ti + 2)


def _build(upto="all", debug=()):
    b = Builder(upto, debug)
    return b.build()


_NC_CACHE = {}


def kernel(**inputs):
    maps = _prep(inputs)
    upto = "all"
    if upto not in _NC_CACHE:
        _NC_CACHE[upto] = _build(upto)
    nc = _NC_CACHE[upto]
    res = run_bass_kernel_spmd(nc, maps, core_ids=list(range(NCORES)))
    return _assemble(res.results)


def _assemble(results):
    y = np.zeros((4, 2048, D), np.float32)
    ys = np.zeros((32, 1, D), np.float32)
    pool_p = np.zeros((1, 4, 15, 1024), np.float32)
    pool_s = np.zeros((1, 32, 15, 1024), np.float32)
    k_p = np.zeros((1, 4, 2048, 8, 128), np.float32)
    v_p = np.zeros((1, 4, 2048, 8, 128), np.float32)
    k_s = np.zeros((1, 32, 1, 8, 128), np.float32)
    v_s = np.zeros((1, 32, 1, 8, 128), np.float32)
    mk = np.zeros((1, 4, 256, 4, 128), np.float32)
    mv = np.zeros((1, 4, 256, 4, 128), np.float32)
    for core, r in enumerate(results):
        b, half = core // 2, core % 2
        sl = slice(half * 1024, (half + 1) * 1024)
        y[b, sl] = r["o_y"]
        ys[core * 4:(core + 1) * 4, 0] = r["o_ys"]
        if half == 1:
            pool_p[0, b] = r["o_pool"]
            mk[0, b] = r["o_mk"].reshape(256, 4, 128)
            mv[0, b] = r["o_mv"].reshape(256, 4, 128)
        pool_s[0, core * 4:(core + 1) * 4] = r["o_pools"]
        k_p[0, b, sl] = r["o_k"].reshape(1024, 8, 128)
        v_p[0, b, sl] = r["o_v"].reshape(1024, 8, 128)
        k_s[0, core * 4:(core + 1) * 4, 0] = r["o_ks"].reshape(4, 8, 128)
        v_s[0, core * 4:(core + 1) * 4, 0] = r["o_vs"].reshape(4, 8, 128)
    return (y, ys, pool_p, pool_s, k_p, v_p, k_s, v_s, mk, mv)
```

```python
import numpy as np
import concourse.bass as bass
import concourse.mybir as mybir
from concourse.bass_utils import run_bass_kernel_spmd

F32 = mybir.dt.float32
BF16 = mybir.dt.bfloat16
AF = mybir.ActivationFunctionType
ALU = mybir.AluOpType
AX = mybir.AxisListType

D = 2048
NCORES = 8
NT = 1028
EPS = 1e-6
PAST = 8192
SBUF_BASE = 16640
SBUF_END = 229376
TOP_GUARD = 4096
GROUPS = [(0, 384), (384, 768), (768, 1028)]


class Res:
    __slots__ = ("name", "last_write", "reads")

    def __init__(self, name):
        self.name = name
        self.last_write = None
        self.reads = []


class Lane:
    def __init__(self, sem, name):
        self.sem = sem
        self.name = name
        self.count = 0


class Op:
    __slots__ = ("eng", "fn", "deps", "signals", "count", "lane", "lane_val", "is_dma", "idx")

    def __init__(self, eng, fn, is_dma=False, lane=None):
        self.eng = eng
        self.fn = fn
        self.deps = []
        self.signals = False
        self.count = None
        self.lane = lane
        self.lane_val = None
        self.is_dma = is_dma


class Sched:
    ENGS = ("pe", "act", "dve", "pool", "sp")

    def __init__(self, nc):
        self.nc = nc
        self.handles = {"pe": nc.tensor, "act": nc.scalar, "dve": nc.vector,
                        "pool": nc.gpsimd, "sp": nc.sync}
        self.ops = []
        self.sems = {}
        self._ctx = []

    def _new_sem(self, name):
        cm = self.nc.semaphore(name)
        h = cm.__enter__()
        self._ctx.append(cm)
        return h

    def lane(self, name):
        return Lane(self._new_sem("l_" + name), name)

    def op(self, eng, meth, kw=None, reads=(), writes=(), lane=None, is_dma=False, extra=()):
        fn = (lambda h, meth=meth, kw=dict(kw or {}): getattr(h, meth)(**kw))
        o = Op(eng, fn, is_dma=is_dma, lane=lane)
        deps = []
        for r in reads:
            if r.last_write is not None:
                deps.append(r.last_write)
        for w in writes:
            if w.last_write is not None:
                deps.append(w.last_write)
            deps.extend(w.reads)
        deps.extend(extra)
        seen = set()
        for d in deps:
            if d is o or id(d) in seen:
                continue
            seen.add(id(d))
            if (not d.is_dma) and (not is_dma) and d.eng == eng:
                if eng in ("pe", "sp"):
                    continue
            o.deps.append(d)
            d.signals = True
        for r in reads:
            r.reads.append(o)
        for w in writes:
            w.last_write = o
            w.reads = []
        if is_dma:
            lane.count += 16
            o.lane_val = lane.count
        self.ops.append(o)
        return o

    def dma(self, queue, out, in_, reads=(), writes=(), lane=None, extra=()):
        return self.op(queue, "dma_start", dict(out=out, in_=in_), reads=reads,
                       writes=writes, lane=lane, is_dma=True, extra=extra)

    def wait_all(self, eng, ops):
        o = Op(eng, None)
        for d in ops:
            o.deps.append(d)
            d.signals = True
        self.ops.append(o)

    def emit(self):
        for e in self.ENGS:
            self.sems[e] = self._new_sem("s_" + e)
        cnt = {e: 0 for e in self.ENGS}
        for o in self.ops:
            if (not o.is_dma) and o.signals and o.fn is not None:
                cnt[o.eng] += 1
                o.count = cnt[o.eng]
        waited = {e: {} for e in self.ENGS}
        for o in self.ops:
            h = self.handles[o.eng]
            w = waited[o.eng]
            need = {}
            for d in o.deps:
                if d.is_dma:
                    key, sem, val = ("l", id(d.lane)), d.lane.sem, d.lane_val
                else:
                    key, sem, val = ("e", d.eng), self.sems[d.eng], d.count
                if key not in need or need[key][1] < val:
                    need[key] = (sem, val)
            for key, (sem, val) in need.items():
                if w.get(key, 0) >= val:
                    continue
                h.wait_ge(sem, val)
                w[key] = val
            if o.fn is None:
                continue
            inst = o.fn(h)
            if o.is_dma:
                inst.then_inc(o.lane.sem, 16)
            elif o.signals:
                inst.then_inc(self.sems[o.eng], 1)

    def close(self):
        for cm in reversed(self._ctx):
            cm.__exit__(None, None, None)


def _rope_tab(pos):
    half = 16
    inv = np.power(np.float32(500000.0), -np.arange(half, dtype=np.float32) * np.float32(2.0) / np.float32(32.0)).astype(np.float32)
    ang = pos.astype(np.float32)[:, None] * inv[None, :]
    c = np.cos(ang).astype(np.float32)
    s = np.sin(ang).astype(np.float32)
    return np.concatenate([c, c, s, s], axis=1)


CB = {}
_o = 0
for _n, _w in [("ident", 128), ("ones", 128), ("validp", 128), ("valid16", 128), ("mu_ml", 256),
               ("m16", 64), ("ohb", 512), ("bmask", 1024)]:
    CB[_n] = (_o, _w)
    _o += _w
CB_W = _o
CF = {}
_o = 0
for _n, _w in [("gcols", 64), ("pscale", 8), ("invcnt", 64), ("nhalf", 1), ("coefm", 16), ("udiag", 16),
               ("oh4", 4), ("ln3", 1), ("zero", 1)]:
    CF[_n] = (_o, _w)
    _o += _w
CF_W = _o


def _consts(half):
    cb = np.zeros((128, CB_W), np.float32)
    k = np.arange(128)[:, None]
    q = np.arange(128)[None, :]
    o, w = CB["ident"]; cb[:, o:o + w] = (k == q)
    o, w = CB["ones"]; cb[:, o:o + w] = 1.0
    o, w = CB["validp"]; cb[:, o:o + w] = float(half)
    o, w = CB["valid16"]; cb[:, o:o + w] = 1.0; cb[:64, o:o + w] = float(half)
    o, w = CB["mu_ml"]; cb[:, o:o + 128] = (k >= q); cb[:, o + 128:o + 256] = (k <= q)
    o, w = CB["m16"]
    for hf in range(2):
        qq = np.arange(32)[None, :]
        cb[:, o + hf * 32:o + hf * 32 + 32] = (k <= 64 + 32 * hf + qq)
    o, w = CB["ohb"]
    for s in range(4):
        cb[s, o + s * 128:o + (s + 1) * 128] = 1.0
    o, w = CB["bmask"]
    for h in range(8):
        cb[h, o + h * 128:o + (h + 1) * 128] = 1.0
    cf = np.zeros((128, CF_W), np.float32)
    o, w = CF["nhalf"]; cf[:, o] = -0.5
    o, w = CF["ln3"]; cf[:, o] = np.log(3.0)
    o, w = CF["invcnt"]
    for g, win in enumerate((2, 4, 8, 16)):
        for t in range(16):
            cnt = min(win, half * 1024 + t + 1)
            cf[:, o + g * 16 + t] = 1.0 / cnt
    o, w = CF["coefm"]
    for g, win in enumerate((2, 4, 8, 16)):
        for s in range(4):
            for j in range(15):
                if j >= 16 - win:
                    cf[s * 15 + j, o + g * 4 + s] = 1.0 / win
    o, w = CF["udiag"]
    for g, win in enumerate((2, 4, 8, 16)):
        for s in range(4):
            cf[s, o + g * 4 + s] = 1.0 / win - 1.0
    o, w = CF["oh4"]
    for s in range(4):
        cf[s, o + s] = 1.0
    return cb, cf


def _piece_cols(w, ncols=512):
    K, N = w.shape
    c = K // 128
    a = w.reshape(c, 128, N // ncols, ncols).transpose(2, 1, 0, 3)
    return np.ascontiguousarray(a).reshape(N // ncols, 128, c * ncols)


def _piece_rows(w, nrows=512):
    K, N = w.shape
    cc = nrows // 128
    a = w.reshape(K // nrows, cc, 128, N).transpose(0, 2, 1, 3)
    return np.ascontiguousarray(a).reshape(K // nrows, 128, cc * N)


def _prep(inp):
    f = lambda a: np.ascontiguousarray(np.asarray(a, dtype=np.float32))
    shared = {}
    shared["w_in_l"] = _piece_cols(f(inp["w_in"])[0])
    shared["w_out_l"] = _piece_cols(f(inp["w_out"])[0])
    shared["w_mkv_l"] = _piece_cols(f(inp["w_mem_kv"])[0])
    shared["w_xq_l"] = _piece_cols(f(inp["w_xq"])[0])
    shared["w_xo_l"] = _piece_rows(f(inp["w_xo"])[0])
    shared["w_ff1_l"] = _piece_cols(f(inp["w_ff1"])[0])
    shared["w_ff2_l"] = _piece_rows(f(inp["w_ff2"])[0])
    wp = f(inp["w_pool"])[0]
    shared["w_pool_l"] = np.ascontiguousarray(wp.reshape(4, 2, 128, 256).transpose(2, 0, 1, 3)).reshape(128, 2048)
    shared["grows"] = np.stack([f(inp["g_mix_post"])[0], f(inp["g_mem_post"])[0], f(inp["g_ffn_post"])[0]])
    gc = np.concatenate([f(inp[n])[0].reshape(16, 128).T for n in ("g_mix_pre", "g_mem_pre", "g_ffn_pre", "g_mem_kv")], axis=1)
    psc = f(inp["pool_scale"])[0].reshape(8, 128).T
    xp_all = f(inp["x_prompt"])
    xs_all = f(inp["x_sample"])
    maps = []
    for core in range(NCORES):
        b, half = core // 2, core % 2
        m = dict(shared)
        m["xm"] = np.ascontiguousarray(xp_all[b, half * 1024:(half + 1) * 1024])
        m["xp"] = np.ascontiguousarray(xp_all[b, 0:1024]) if half == 1 else np.zeros((1024, D), np.float32)
        m["xs"] = np.ascontiguousarray(xs_all[core * 4:(core + 1) * 4, 0])
        m["memx"] = np.ascontiguousarray(f(inp["mem_prompt"])[b])
        m["cache_k"] = np.ascontiguousarray(f(inp["cache_attn_k"])[0, core * 4:(core + 1) * 4].reshape(4, 2048, 1024))
        m["cache_v"] = np.ascontiguousarray(f(inp["cache_attn_v"])[0, core * 4:(core + 1) * 4].reshape(4, 2048, 1024))
        m["cmem_k"] = np.ascontiguousarray(f(inp["cache_mem_k"])[0, core * 4:(core + 1) * 4].reshape(4, 256, 512))
        m["cmem_v"] = np.ascontiguousarray(f(inp["cache_mem_v"])[0, core * 4:(core + 1) * 4].reshape(4, 256, 512))
        m["spool"] = np.ascontiguousarray(f(inp["state_pool"])[0, core * 4:(core + 1) * 4].reshape(60, 1024))
        cb, cf = _consts(half)
        o, w = CF["gcols"]; cf[:, o:o + w] = gc
        o, w = CF["pscale"]; cf[:, o:o + w] = psc
        m["cbf"] = cb
        m["cf32"] = cf
        pos_m = half * 1024 + np.arange(1024)
        csm = np.zeros((9 * 128, 64), np.float32)
        csm[:1024] = _rope_tab(pos_m)
        csm[1024:1028] = _rope_tab(np.full(4, PAST))
        m["cs_main"] = csm
        m["cs_prev"] = _rope_tab(np.arange(1024))
        maps.append(m)
    return maps


class Builder:
    def __init__(self, upto="all", debug=()):
        self.upto = upto
        self.debug = set(debug)
        self.nc = nc = bass.Bass("TRN2", target_bir_lowering=False)
        self.S = Sched(nc)
        self.outs = []
        di = lambda n, s: nc.dram_tensor(n, list(s), F32, kind="ExternalInput").ap()
        do = lambda n, s: nc.dram_tensor(n, list(s), F32, kind="ExternalOutput").ap()
        self.xm = di("xm", (1024, D)); self.xp = di("xp", (1024, D)); self.xs = di("xs", (4, D))
        self.memx = di("memx", (256, D))
        self.w_in_l = di("w_in_l", (8, 128, 8192)); self.w_out_l = di("w_out_l", (4, 128, 8192))
        self.w_mkv_l = di("w_mkv_l", (2, 128, 8192)); self.w_xq_l = di("w_xq_l", (1, 128, 8192))
        self.w_xo_l = di("w_xo_l", (1, 128, 8192)); self.w_ff1_l = di("w_ff1_l", (16, 128, 8192))
        self.w_ff2_l = di("w_ff2_l", (16, 128, 8192)); self.w_pool_l = di("w_pool_l", (128, 2048))
        self.grows = di("grows", (3, D))
        self.cache_k = di("cache_k", (4, 2048, 1024)); self.cache_v = di("cache_v", (4, 2048, 1024))
        self.cmem_k = di("cmem_k", (4, 256, 512)); self.cmem_v = di("cmem_v", (4, 256, 512))
        self.spool = di("spool", (60, 1024))
        self.cbf = di("cbf", (128, CB_W)); self.cf32 = di("cf32", (128, CF_W))
        self.cs_main = di("cs_main", (9 * 128, 64)); self.cs_prev = di("cs_prev", (1024, 64))
        self.o_y = do("o_y", (1024, D)); self.o_ys = do("o_ys", (4, D))
        self.o_pool = do("o_pool", (15, 1024)); self.o_pools = do("o_pools", (4, 15, 1024))
        self.o_k = do("o_k", (1024, 1024)); self.o_v = do("o_v", (1024, 1024))
        self.o_ks = do("o_ks", (4, 1024)); self.o_vs = do("o_vs", (4, 1024))
        self.o_mk = do("o_mk", (256, 512)); self.o_mv = do("o_mv", (256, 512))
        self.vs_d = nc.dram_tensor("vs_scr", [2048, 1024], BF16).ap()
        self.x2_d = nc.dram_tensor("x2_scr", [9 * 128, D], F32).ap()
        self.banks = [nc.alloc_psum_tensor("bank%d" % i, [128, 512], F32) for i in range(8)]
        self.bres = [Res("bank%d" % i) for i in range(8)]
        self._names = 0

    def sb(self, name, shape, dt, off):
        nbytes = int(np.prod(shape[1:])) * (4 if dt == F32 else 2)
        assert off % 32 == 0, (name, off)
        assert SBUF_BASE <= off and off + nbytes <= SBUF_END, (name, off, nbytes)
        self._names += 1
        t = self.nc.alloc_sbuf_tensor_at("%s_%d" % (name, self._names), list(shape), dt, offset=off)
        return t

    def bank_bf(self, i):
        return self.banks[i][:].bitcast(BF16)

    def build(self):
        S = self.S
        nc = self.nc
        C0 = SBUF_BASE
        cbt = self.sb("cbt", [128, CB_W], BF16, C0)
        cft = self.sb("cft", [128, CF_W], F32, C0 + 4736)
        smalls = self.sb("smalls", [128, 128], F32, C0 + 4736 + 704)
        cs_t = self.sb("cs_t", [128, 2, 64], F32, C0 + 6144)
        gb = self.sb("gb", [128, D], F32, C0 + 6656)
        R0 = C0 + 6656 + 8192 + 128
        R0 = (R0 + 31) // 32 * 32
        self.cbt, self.cft, self.smalls, self.cs_t, self.gb = cbt, cft, smalls, cs_t, gb
        self.r_const = Res("const"); self.r_gb = Res("gb")
        self.cb = lambda n: cbt[:, CB[n][0]:CB[n][0] + CB[n][1]]
        self.cf = lambda n: cft[:, CF[n][0]:CF[n][0] + CF[n][1]]
        self.WR = [self.sb("wr%d" % i, [128, 8192], BF16, R0 + i * 16384) for i in range(4)]
        self.wres = [Res("wr%d" % i) for i in range(4)]
        self.wlane = [S.lane("wr%d" % i) for i in range(4)]
        A0 = R0 + 4 * 16384
        self.A0 = A0
        self.R0 = R0

        l_const = S.lane("const")
        l_const2 = S.lane("const2")
        self.r_constb = Res("constb")
        c1 = S.dma("pool", cbt[:], self.cbf, writes=[self.r_constb], lane=l_const)
        S.dma("sp", cft[:], self.cf32, writes=[self.r_const], lane=l_const2)
        j = S.op("sp", "nop", {}, reads=[self.r_constb], writes=[self.r_const])
        self.phase_kv()
        if self.upto == "kv":
            return self.finish()
        self.phase_qu()
        if self.upto in ("u", "qu"):
            return self.finish()
        self.phase_mkv()
        if self.upto == "mkv":
            return self.finish()
        self.phase_att()
        if self.upto == "att":
            self.dbg("mixed", [128, 16, NT], self.mixed[:], self.r_mixed)
            return self.finish()
        self.phase_satt()
        self.dbg("mixed", [128, 16, NT], self.mixed[:], self.r_mixed)
        if self.upto == "satt":
            return self.finish()
        self.alloc_epi()
        self.realias([self.wres[2]] , [self.r_junk, self.r_hb1, self.r_hb2])
        self.phase_wo()
        self.phase_epi1()
        if self.upto == "wo":
            self.dbg("x1", [128, 9, D], self.x1[:], self.r_x1)
            return self.finish()
        self.phase_xa()
        self.dbg("o2T", [128, 4, NT], self.o2T[:], self.r_o2T)
        if self.upto == "xa":
            return self.finish()
        self.phase_wxo()
        if self.upto == "wxo":
            self.dbg("x2", [128, 9, D], self.x1[:], self.r_x1)
            return self.finish()
        self.phase_ffn()
        return self.finish()

    def finish(self):
        S = self.S
        S.wait_all("sp", self.outs)
        S.emit()
        S.close()
        return self.nc

    def load_piece(self, slot, src):
        return self.S.dma("pool", self.WR[slot][:], src, writes=[self.wres[slot]], lane=self.wlane[slot])

    def prenorm_tile(self, xin, r_xin, hb, r_hb, rows, gidx, dst, r_dst, tb, ss_col):
        self.prenorm_a(xin, r_xin, hb, r_hb, rows, ss_col)
        self.prenorm_b(hb, r_hb, rows, gidx, dst, r_dst, tb)

    def prenorm_a(self, xin, r_xin, hb, r_hb, rows, ss_col):
        S = self.S
        junk = self.junk
        ss = self.smalls[:, ss_col:ss_col + 1]
        rs = self.smalls[:, ss_col + 1:ss_col + 2]
        r_ss = self.r_small[ss_col // 2]
        S.op("act", "activation", dict(out=junk[:rows, :], in_=xin[:rows, :], func=AF.Square, accum_out=ss[:rows, :]),
             reads=[r_xin], writes=[self.r_junk, r_ss])
        S.op("pool", "tensor_scalar", dict(out=rs[:rows, :], in0=ss[:rows, :], scalar1=float(D * EPS), scalar2=None, op0=ALU.add),
             reads=[r_ss], writes=[r_ss])
        nh = self.cf("nhalf")
        S.op("pool", "tensor_tensor", dict(out=rs[:rows, :], in0=rs[:rows, :], in1=nh[:rows, :], op=ALU.pow),
             reads=[r_ss, self.r_const], writes=[r_ss])
        S.op("dve", "tensor_scalar", dict(out=hb[:rows, :], in0=xin[:rows, :], scalar1=rs[:rows, 0:1], scalar2=float(np.sqrt(D)),
                                          op0=ALU.mult, op1=ALU.mult),
             reads=[r_xin, r_ss], writes=[r_hb])

    def prenorm_b(self, hb, r_hb, rows, gidx, dst, r_dst, tb):
        S = self.S
        ident = self.cb("ident")
        gcol = self.cf("gcols")
        for k in range(2):
            bk = self.bank_bf(tb[k])
            for c8 in range(8):
                c = k * 8 + c8
                S.op("pe", "transpose", dict(out=bk[:, c8 * 128:c8 * 128 + rows], in_=hb[:rows, c * 128:(c + 1) * 128],
                                             identity=ident[:rows, :rows]),
                     reads=[r_hb, self.r_const], writes=[self.bres[tb[k]]])
            g_ap = gcol[:, gidx * 16 + k * 8:gidx * 16 + k * 8 + 8]
            S.op("dve", "tensor_tensor", dict(
                out=dst[:, k * 8:(k + 1) * 8, 0:rows],
                in0=bk.rearrange("p (c t) -> p c t", t=128)[:, :, 0:rows],
                in1=g_ap.unsqueeze(2).to_broadcast([128, 8, rows]), op=ALU.mult),
                 reads=[self.bres[tb[k]], self.r_const], writes=[r_dst])

    def setup_common(self):
        S = self.S
        e = SBUF_END - TOP_GUARD
        e -= 32768; self.kT = self.sb("kT", [128, 8, 2048], BF16, e)
        e -= 32896; self.hT = self.sb("hT", [128, 16, NT], BF16, e)
        e -= 512; self.hTp15 = self.sb("hTp15", [128, 16, 16], BF16, e)
        e -= 2048; self.ksb = self.sb("ksb", [4, 1024], BF16, e)
        e -= 2048; self.vsb = self.sb("vsb", [4, 1024], BF16, e)
        e -= 2048; self.qsb = self.sb("qsb", [4, 1024], BF16, e)
        e -= 4096; self.usf = self.sb("usf", [4, 1024], F32, e)
        self.PERS0 = e
        self.r_small = [Res("small%d" % i) for i in range(40)]
        self.r_kT = [Res("kT%d" % i) for i in range(16)]
        self.r_hT = [Res("hT%d" % i) for i in range(9)]
        self.r_hTp15 = Res("hTp15")
        self.r_ksb = Res("ksb"); self.r_vsb = Res("vsb"); self.r_qsb = Res("qsb"); self.r_usf = Res("usf")
        self.r_junk = Res("junk")
        self.r_cs = [Res("cs0"), Res("cs1")]
        self.l_cs = [S.lane("cs0"), S.lane("cs1")]
        self.r_vs = [Res("vs_t%d" % i) for i in range(16)]
        self.l_vs = [S.lane("vs0"), S.lane("vs1")]
        self.l_st = [S.lane("st%d" % i) for i in range(4)]
        self.r_stage = [Res("st%d" % i) for i in range(4)]
        self.r_kb = [Res("kb0"), Res("kb1")]
        self.r_rtmp = [Res("rt0"), Res("rt1")]
        self.nst = 0
        self.npiece = 0

    def alloc_stage(self, o):
        self.stage = [self.sb("stage%d" % i, [128, 512], F32, o + i * 2048) for i in range(4)]; o += 8192
        self.kb = [self.sb("kb%d" % i, [128, 512], BF16, o + i * 1024) for i in range(2)]; o += 2048
        self.rtmp = [self.sb("rtmp%d" % i, [128, 4, 64], F32, o + i * 1024) for i in range(2)]; o += 2048
        return o

    def tm_proj(self, kind, pslot, lhs, r_lhs, rows, cs_sl, tile_kind, t, half_idx, cs_ap=None, r_csx=None):
        S = self.S
        nst = self.nst
        bank = 2 + (nst % 4)
        ps = self.banks[bank]
        for c in range(16):
            S.op("pe", "matmul", dict(out=ps[:rows, :], lhsT=lhs[:, c, 0:rows], rhs=self.WR[pslot][:, c * 512:(c + 1) * 512],
                                      start=(c == 0), stop=(c == 15)),
                 reads=[r_lhs, self.wres[pslot]], writes=[self.bres[bank]])
        self.flush_pending()
        new_pending = None
        st = nst % 4
        stg = self.stage[st]
        r_st = self.r_stage[st]
        S.op("act", "activation", dict(out=stg[:rows, :], in_=ps[:rows, :], func=AF.Copy),
             reads=[self.bres[bank]], writes=[r_st])
        if kind in ("k", "q"):
            rt = self.rtmp[nst % 2]
            r_rt = self.r_rtmp[nst % 2]
            sv = stg[:rows, :].rearrange("p (h d) -> p h d", d=128)
            if cs_ap is None:
                cs_ap = self.cs_t[:rows, cs_sl, :]
                r_csx = self.r_cs[cs_sl]
            cc = cs_ap[:, 0:32].unsqueeze(1).to_broadcast([rows, 4, 32])
            ss_ = cs_ap[:, 32:64].unsqueeze(1).to_broadcast([rows, 4, 32])
            rr = [r_st, r_csx]
            S.op("dve", "tensor_tensor", dict(out=rt[:rows, :, 0:32], in0=sv[:, :, 0:32], in1=cc, op=ALU.mult), reads=rr, writes=[r_rt])
            S.op("dve", "tensor_tensor", dict(out=rt[:rows, :, 32:64], in0=sv[:, :, 0:32], in1=ss_, op=ALU.mult), reads=rr, writes=[r_rt])
            S.op("dve", "tensor_tensor", dict(out=sv[:, :, 0:16], in0=rt[:rows, :, 0:16], in1=rt[:rows, :, 48:64], op=ALU.subtract),
                 reads=[r_rt], writes=[r_st])
            S.op("dve", "tensor_tensor", dict(out=sv[:, :, 16:32], in0=rt[:rows, :, 16:32], in1=rt[:rows, :, 32:48], op=ALU.add),
                 reads=[r_rt], writes=[r_st])
        cbs = slice(half_idx * 512, half_idx * 512 + 512)
        lane = self.l_st[st]
        if kind == "u":
            if tile_kind == "samp":
                S.op("dve", "tensor_copy", dict(out=self.usf[:, cbs], in_=stg[:4, :]), reads=[r_st], writes=[self.r_usf])
                self.outs.append(S.dma("sp", self.o_pools[:, 14, cbs], stg[:4, :], reads=[r_st], lane=lane))
            else:
                self.outs.append(S.dma("sp", self.o_pool[:, cbs], stg[113:128, :], reads=[r_st], lane=lane))
        elif tile_kind == "samp":
            dst_t, r_t, od = {"k": (self.ksb, self.r_ksb, self.o_ks), "v": (self.vsb, self.r_vsb, self.o_vs),
                              "q": (self.qsb, self.r_qsb, None)}[kind]
            S.op("dve", "tensor_copy", dict(out=dst_t[:, cbs], in_=stg[:4, :]), reads=[r_st], writes=[r_t])
            if od is not None:
                self.outs.append(S.dma("sp", od[:, cbs], stg[:4, :], reads=[r_st], lane=lane))
        else:
            kbs = self.kb[nst % 2]
            r_kbs = self.r_kb[nst % 2]
            S.op("dve", "tensor_copy", dict(out=kbs[:, :], in_=stg[:, :]), reads=[r_st], writes=[r_kbs])
            if tile_kind == "main" and kind in ("k", "v"):
                od = (self.o_k if kind == "k" else self.o_v)[t * 128:(t + 1) * 128, cbs]
                self.outs.append(S.dma("sp", od, stg[:, :], reads=[r_st], lane=lane))
            ext_tile = t if tile_kind == "prev" else 8 + t
            if kind in ("k", "q"):
                tb = 6 + (nst % 2)
                tbk = self.bank_bf(tb)
                if kind == "k":
                    dst = self.kT[:, half_idx * 4:(half_idx + 1) * 4, ext_tile * 128:(ext_tile + 1) * 128]
                    r_d = self.r_kT[ext_tile]
                else:
                    dst = self.qT[:, half_idx * 4:(half_idx + 1) * 4, t * 128:(t + 1) * 128]
                    r_d = self.r_qT[t]

                def deferred(tb=tb, tbk=tbk, kbs=kbs, r_kbs=r_kbs, dst=dst, r_d=r_d):
                    for hh in range(4):
                        S.op("pe", "transpose", dict(out=tbk[:, hh * 128:(hh + 1) * 128], in_=kbs[:, hh * 128:(hh + 1) * 128], identity=self.cb("ident")),
                             reads=[r_kbs, self.r_const], writes=[self.bres[tb]])
                    S.op("act", "activation", dict(out=dst, in_=tbk[:, 0:512].rearrange("p (h t) -> p h t", t=128), func=AF.Copy),
                         reads=[self.bres[tb]], writes=[r_d])
                new_pending = deferred
            else:
                vd = self.vs_d[ext_tile * 128:(ext_tile + 1) * 128, cbs]
                S.dma("sp", vd, kbs[:, :], reads=[r_kbs], writes=[self.r_vs[ext_tile]], lane=self.l_vs[nst % 2])
        self.pending = new_pending
        self.nst += 1

    def flush_pending(self):
        if getattr(self, "pending", None) is not None:
            p = self.pending
            self.pending = None
            p()

    def load_cs(self, sl, tile_kind, t, rows):
        if tile_kind == "prev":
            cs_src = self.cs_prev[t * 128:(t + 1) * 128, :]
        elif tile_kind == "main":
            cs_src = self.cs_main[t * 128:(t + 1) * 128, :]
        else:
            cs_src = self.cs_main[1024:1028, :]
        self.S.dma("sp", self.cs_t[:rows, sl, :], cs_src, writes=[self.r_cs[sl]], lane=self.l_cs[sl])

    def phase_kv(self):
        S = self.S
        self.setup_common()
        o = self.A0
        xin = [self.sb("xin%d" % i, [128, D], F32, o + i * 8192) for i in range(2)]; o += 16384
        hb = [self.sb("hb%d" % i, [128, D], BF16, o + i * 4096) for i in range(2)]; o += 8192
        self.junk = self.sb("junk", [128, D], BF16, o); o += 4096
        hTt = [self.sb("hTt%d" % i, [128, 16, 128], BF16, o + i * 4096) for i in range(2)]; o += 8192
        o = self.alloc_stage(o)
        assert o <= self.PERS0, (o, self.PERS0)
        r_xin = [Res("xin0"), Res("xin1")]; r_hb = [Res("hb0"), Res("hb1")]
        r_hTt = [Res("hTt0"), Res("hTt1")]
        l_xin = [S.lane("xin0"), S.lane("xin1")]

        for i, pj in enumerate((4, 5, 6, 7)):
            self.load_piece(i, self.w_in_l[pj])
        self.npiece = 4

        tiles = [("prev", t, 128) for t in range(8)] + [("main", t, 128) for t in range(8)] + [("samp", 8, 4)]

        def dst_of(ti):
            kind, t, rows = tiles[ti]
            if kind == "prev":
                return hTt[ti % 2], r_hTt[ti % 2]
            return self.hT[:, :, t * 128:t * 128 + rows], self.r_hT[t]

        def issue_x(ti):
            kind, t, rows = tiles[ti]
            src = {"prev": self.xp, "main": self.xm}.get(kind)
            src = self.xs if kind == "samp" else src[t * 128:(t + 1) * 128, :]
            S.dma("sp", xin[ti % 2][:rows, :], src, writes=[r_xin[ti % 2]], lane=l_xin[ti % 2])

        def pa(ti):
            kind, t, rows = tiles[ti]
            self.prenorm_a(xin[ti % 2], r_xin[ti % 2], hb[ti % 2], r_hb[ti % 2], rows, (ti % 4) * 2)

        def pb(ti):
            kind, t, rows = tiles[ti]
            dst, r_dst = dst_of(ti)
            self.prenorm_b(hb[ti % 2], r_hb[ti % 2], rows, 0, dst, r_dst, (0, 1))
            if kind == "prev" and t == 7:
                S.op("pool", "tensor_copy", dict(out=self.hTp15[:, :, 0:15], in_=dst[:, :, 113:128]),
                     reads=[r_dst], writes=[self.r_hTp15])

        n = len(tiles)
        issue_x(0); issue_x(1)
        self.load_cs(0, *[tiles[0][0], tiles[0][1], tiles[0][2]])
        self.load_cs(1, *[tiles[1][0], tiles[1][1], tiles[1][2]])
        pa(0); pb(0)
        for ti, (kind, t, rows) in enumerate(tiles):
            sl = ti % 2
            dst, r_dst = dst_of(ti)
            if ti + 1 < n:
                pa(ti + 1)
            if ti + 2 < n:
                issue_x(ti + 2)
            self.tm_proj("k", 0, dst, r_dst, rows, sl, kind, t, 0)
            if ti + 1 < n:
                pb(ti + 1)
            self.tm_proj("k", 1, dst, r_dst, rows, sl, kind, t, 1)
            if ti + 2 < n:
                k2, t2, rows2 = tiles[ti + 2]
                self.load_cs(sl, k2, t2, rows2)
            self.tm_proj("v", 2, dst, r_dst, rows, sl, kind, t, 0)
            self.tm_proj("v", 3, dst, r_dst, rows, sl, kind, t, 1)
        self.flush_pending()
        self.kv_tmp_res = r_xin + r_hb + r_hTt + [self.r_junk]

    def next_piece(self, src):
        slot = self.npiece % 3
        self.npiece += 1
        self.load_piece(slot, src)
        return slot

    def realias(self, old, new):
        bar = []
        for r in old:
            if r.last_write is not None:
                bar.append(r.last_write)
            bar.extend(r.reads)
        if not bar:
            return
        j = self.S.op("sp", "nop", {}, extra=bar)
        for r in new:
            r.last_write = j
            r.reads = []

    def dbg(self, name, shape, src, reads):
        if name not in self.debug:
            return
        t = self.nc.dram_tensor("dbg_" + name, list(shape), src.dtype, kind="ExternalOutput").ap()
        self.outs.append(self.S.dma("sp", t, src, reads=reads, lane=self.S.lane("dbg_" + name)))

    def phase_qu(self):
        S = self.S
        R0 = self.R0
        o = R0 + 3 * 16384
        self.mixed = self.sb("mixed", [128, 16, NT], BF16, o); o += 32896
        self.r_mixed = [Res("mixed%d" % i) for i in range(16)]
        o = (o + 31) // 32 * 32
        o = self.alloc_stage(o)
        X0 = o
        uT = [self.sb("uT0", [128, 1044], F32, X0)] * 2
        sa = self.sb("sa", [128, 1044], F32, X0 + 4192)
        sbb = self.sb("sbb", [128, 1044], F32, X0 + 8384)
        pooled = [self.sb("pooled%d" % i, [128, 2, NT], BF16, X0 + 12576 + i * 4128) for i in range(2)]
        GB0 = SBUF_BASE + 6656
        wpool = self.sb("wpool", [128, 2048], BF16, GB0)
        spool_t = self.sb("spool_t", [60, 1024], F32, GB0 + 4096)
        t16 = self.sb("t16", [128, 16], F32, X0 + 20832)
        xend = X0 + 20896
        assert xend <= self.PERS0, (xend, self.PERS0)
        self.qT = self.sb("qT", [128, 8, 1024], BF16, X0)
        self.qT_end = X0 + 16384
        self.r_qT = [Res("qT%d" % i) for i in range(8)]
        r_uT = [Res("uT0")] * 2
        r_sa = Res("sa"); r_sb = Res("sb")
        r_pooled = [Res("pooled0"), Res("pooled1")]
        r_wpool = Res("wpool"); r_spool = Res("spool"); r_t16 = Res("t16")
        self.gb_alias = [r_wpool, r_spool]
        old = list(self.kv_tmp_res) + self.r_stage + self.r_kb + self.r_rtmp + [self.wres[3]]
        self.r_stage = [Res("st%d" % i) for i in range(4)]
        self.r_kb = [Res("kb0"), Res("kb1")]
        self.r_rtmp = [Res("rt0"), Res("rt1")]
        newres = self.r_mixed + self.r_stage + self.r_kb + self.r_rtmp + [r_uT[0], r_sa, r_sb] + r_pooled + [r_t16]
        self.realias(old, newres)
        l_wp = S.lane("wpool"); l_sp = S.lane("spool"); l_po = S.lane("pools_out")
        S.dma("pool", wpool[:], self.w_pool_l, writes=[r_wpool], lane=l_wp)
        S.dma("sp", spool_t[:], self.spool, writes=[r_spool], lane=l_sp)
        for s_ in range(4):
            self.outs.append(S.dma("sp", self.o_pools[s_, 0:14, :], spool_t[s_ * 15 + 1:s_ * 15 + 15, :], reads=[r_spool], lane=l_po))

        groups4 = GROUPS + [None]
        gct = 0
        pending_map = []
        for piece in range(2):
            slot = (0, 1)[piece]
            self.load_piece(slot, self.w_in_l[piece])
            self.tm_proj("u", slot, self.hT[:, :, 7 * 128:8 * 128], self.r_hT[7], 128, 0, "main", 7, piece)
            self.tm_proj("u", slot, self.hT[:, :, 1024:1028], self.r_hT[8], 4, 0, "samp", 8, piece)
            for ct in range(4):
                g = gct // 2
                cc = gct % 2
                win = 2 << g
                u = uT[gct % 2]
                r_u = r_uT[gct % 2]
                bset = (2, 3, 4) if gct % 2 == 0 else (5, 6, 7)
                for c in range(16):
                    lw = self.WR[slot][:, c * 512 + ct * 128:c * 512 + (ct + 1) * 128]
                    for gi, grp in enumerate(groups4):
                        if grp is None:
                            S.op("pe", "matmul", dict(out=self.banks[bset[2]][:, 300:315], lhsT=lw, rhs=self.hTp15[:, c, 0:15], start=False, stop=(c == 15),
                                                      skip_group_check=True),
                                 reads=[self.r_hTp15, self.wres[slot]], writes=[self.bres[bset[2]]])
                        else:
                            n = grp[1] - grp[0]
                            rr = self.r_hT[grp[0] // 128:(grp[1] + 127) // 128]
                            S.op("pe", "matmul", dict(out=self.banks[bset[gi]][:, 0:n], lhsT=lw, rhs=self.hT[:, c, grp[0]:grp[1]], start=(c == 0), stop=(c == 15),
                                                      skip_group_check=True),
                                 reads=rr + [self.wres[slot]], writes=[self.bres[bset[gi]]])
                for gi, grp in enumerate(groups4):
                    if grp is None:
                        S.op("act", "activation", dict(out=u[:, 0:15], in_=self.banks[bset[2]][:, 300:315], func=AF.Copy),
                             reads=[self.bres[bset[2]]], writes=[r_u])
                    else:
                        n = grp[1] - grp[0]
                        S.op("act", "activation", dict(out=u[:, 15 + grp[0]:15 + grp[1]], in_=self.banks[bset[gi]][:, 0:n], func=AF.Copy),
                             reads=[self.bres[bset[gi]]], writes=[r_u])
                while pending_map:
                    pending_map.pop(0)()
                cur, r_cur = u, r_u
                bufs = [(sa, r_sa), (sbb, r_sb)]
                sh = 1
                k = 0
                while sh < win:
                    nxt, r_nxt = bufs[k % 2]
                    lo = 2 * sh - 1
                    S.op("dve", "tensor_tensor", dict(out=nxt[:, lo:1039], in0=cur[:, lo:1039], in1=cur[:, lo - sh:1039 - sh], op=ALU.add),
                         reads=[r_cur], writes=[r_nxt])
                    cur, r_cur = nxt, r_nxt
                    sh *= 2
                    k += 1
                pl = pooled[g % 2]
                r_pl = r_pooled[g % 2]
                S.op("dve", "scalar_tensor_tensor", dict(out=pl[:, cc, 16:1024], in0=cur[:, 31:1039], scalar=float(1.0 / win), in1=u[:, 31:1039],
                                                       op0=ALU.mult, op1=ALU.subtract),
                     reads=[r_cur, r_u], writes=[r_pl])
                ic = self.cf("invcnt")[:, g * 16:(g + 1) * 16]
                S.op("dve", "tensor_tensor", dict(out=t16[:, :], in0=cur[:, 15:31], in1=ic, op=ALU.mult), reads=[r_cur, self.r_const], writes=[r_t16])
                S.op("dve", "tensor_tensor", dict(out=pl[:, cc, 0:16], in0=t16[:, :], in1=u[:, 15:31], op=ALU.subtract), reads=[r_t16, r_u], writes=[r_pl])
                cs_ = slice(gct * 128, (gct + 1) * 128)
                S.op("pe", "matmul", dict(out=self.banks[1][:, 400:404], lhsT=spool_t[0:60, cs_], rhs=self.cf("coefm")[0:60, g * 4:(g + 1) * 4],
                                          start=True, stop=False, skip_group_check=True), reads=[r_spool, self.r_const], writes=[self.bres[1]])
                S.op("pe", "matmul", dict(out=self.banks[1][:, 400:404], lhsT=self.usf[0:4, cs_], rhs=self.cf("udiag")[0:4, g * 4:(g + 1) * 4],
                                          start=False, stop=True, skip_group_check=True), reads=[self.r_usf, self.r_const], writes=[self.bres[1]])
                S.op("act", "activation", dict(out=pl[:, cc, 1024:1028], in_=self.banks[1][:, 400:404], func=AF.Copy),
                     reads=[self.bres[1]], writes=[r_pl])
                if cc == 1:
                  def group_map(g=g, pl=pl, r_pl=r_pl):
                    for et in range(2):
                        for gi, grp in enumerate(GROUPS):
                            n = grp[1] - grp[0]
                            bk = 2 + gi if False else (0 + (gi + et * 3) % 2)
                            for c2 in range(2):
                                off = g * 512 + c2 * 256 + et * 128
                                S.op("pe", "matmul", dict(out=self.banks[bk][:, 0:n], lhsT=wpool[:, off:off + 128], rhs=pl[:, c2, grp[0]:grp[1]],
                                                          start=(c2 == 0), stop=(c2 == 1)),
                                     reads=[r_wpool, r_pl], writes=[self.bres[bk]])
                            mt = 2 * g + et
                            S.op("dve", "tensor_scalar", dict(out=self.mixed[:, mt, grp[0]:grp[1]], in0=self.banks[bk][:, 0:n],
                                                              scalar1=self.cf("pscale")[:, mt:mt + 1], scalar2=None, op0=ALU.mult),
                                 reads=[self.bres[bk], self.r_const], writes=[self.r_mixed[mt]])
                  pending_map.append(group_map)
                gct += 1
        while pending_map:
            pending_map.pop(0)()
        if self.upto == "u":
            return
        self.realias([r_uT[0], r_sa, r_sb] + r_pooled + [r_t16], self.r_qT)
        tiles = [("main", t, 128) for t in range(8)] + [("samp", 8, 4)]
        csq = self.sb("csq", [128, 9, 64], F32, self.PERS0)
        r_csq = Res("csq")
        self.r_csq = r_csq
        self.realias([self.r_usf], [r_csq])
        S.dma("sp", csq[:, :, :], self.cs_main.rearrange("(t p) c -> p t c", p=128), writes=[r_csq], lane=S.lane("csq"))
        for piece in range(2):
            slot = (2, 0)[piece]
            self.load_piece(slot, self.w_in_l[2 + piece])
            for ti, (kind, t, rows) in enumerate(tiles):
                lhs = self.hT[:, :, t * 128:t * 128 + rows]
                self.tm_proj("q", slot, lhs, self.r_hT[t], rows, 0, kind, t, piece, cs_ap=csq[:rows, t, :], r_csx=r_csq)
        self.flush_pending()
        self.qu_tmp = r_uT + [r_sa, r_sb] + r_pooled + [r_t16]


    def phase_mkv(self):
        S = self.S
        hT0 = SBUF_END - TOP_GUARD - 32768 - 32896
        o = hT0
        mxin = self.sb("mxin", [128, D], F32, o); o += 8192
        mhb = self.sb("mhb", [128, D], BF16, o); o += 4096
        self.junk = self.sb("junk2", [128, D], BF16, o); o += 4096
        hmT = self.sb("hmT", [128, 16, 256], BF16, o); o += 8192
        mxin2 = self.sb("mxin2", [128, D], F32, o); o += 8192
        assert o <= hT0 + 32896 + 512
        g0 = self.R0 + 16384 + 24576
        self.memKT = self.sb("memKT", [128, 4, 256], BF16, g0)
        self.memV = self.sb("memV", [128, 2, 512], BF16, g0 + 2048)
        r_mxin = Res("mxin"); r_mxin2 = Res("mxin2"); r_mhb = Res("mhb"); self.r_junk = Res("junk2"); r_hmT = [Res("hmT0"), Res("hmT1")]
        self.r_memKT = Res("memKT"); self.r_memV = Res("memV")
        self.realias(self.r_hT + [self.r_hTp15], [r_mxin, r_mxin2, r_mhb, self.r_junk] + r_hmT)
        self.realias([self.wres[2]], [self.r_memKT, self.r_memV])
        l_mx = S.lane("mxin")
        sK = 1; sV = 0
        self.load_piece(sK, self.w_mkv_l[0])
        self.load_piece(sV, self.w_mkv_l[1])
        mx = [(mxin, r_mxin, l_mx), (mxin2, r_mxin2, S.lane("mxin2"))]
        for t in range(2):
            S.dma("sp", mx[t][0][:, :], self.memx[t * 128:(t + 1) * 128, :], writes=[mx[t][1]], lane=mx[t][2])
        for t in range(2):
            self.prenorm_tile(mx[t][0], mx[t][1], mhb, r_mhb, 128, 3, hmT[:, :, t * 128:(t + 1) * 128], r_hmT[t], (0, 1), 8 + 2 * t)
        for t in range(2):
            for kind, slot in (("mk", sK), ("mv", sV)):
                nst = self.nst
                bank = 2 + (nst % 4)
                ps = self.banks[bank]
                for c in range(16):
                    S.op("pe", "matmul", dict(out=ps[:, :], lhsT=hmT[:, c, t * 128:(t + 1) * 128], rhs=self.WR[slot][:, c * 512:(c + 1) * 512],
                                              start=(c == 0), stop=(c == 15)), reads=[r_hmT[t], self.wres[slot]], writes=[self.bres[bank]])
                st = nst % 4
                stg = self.stage[st]
                S.op("act", "activation", dict(out=stg[:, :], in_=ps[:, :], func=AF.Copy), reads=[self.bres[bank]], writes=[self.r_stage[st]])
                od = (self.o_mk if kind == "mk" else self.o_mv)[t * 128:(t + 1) * 128, :]
                self.outs.append(S.dma("sp", od, stg[:, :], reads=[self.r_stage[st]], lane=self.l_st[st]))
                if kind == "mv":
                    S.op("dve", "tensor_copy", dict(out=self.memV[:, t, :], in_=stg[:, :]), reads=[self.r_stage[st]], writes=[self.r_memV])
                self.nst += 1
        for hh in range(4):
            bank = 2 + (hh % 4)
            for c in range(16):
                S.op("pe", "matmul", dict(out=self.banks[bank][:, 0:256], lhsT=self.WR[sK][:, c * 512 + hh * 128:c * 512 + (hh + 1) * 128],
                                          rhs=hmT[:, c, 0:256], start=(c == 0), stop=(c == 15)),
                     reads=r_hmT + [self.wres[sK]], writes=[self.bres[bank]])
            S.op("act", "activation", dict(out=self.memKT[:, hh, :], in_=self.banks[bank][:, 0:256], func=AF.Copy),
                 reads=[self.bres[bank]], writes=[self.r_memKT])
        self.mkv_tmp = [r_mxin, r_mxin2, r_mhb, self.r_junk] + r_hmT

    def phase_att(self):
        S = self.S
        hT0 = SBUF_END - TOP_GUARD - 32768 - 32896
        vA = self.sb("vbufA", [128, 48, 256], BF16, hT0)
        pT = [self.sb("pT%d" % i, [128, 512], BF16, hT0 + 24576 + i * 1024) for i in range(3)]
        rden = [self.sb("rden%d" % i, [128, 512], F32, hT0 + 24576 + 3072 + i * 2048) for i in range(2)]
        assert hT0 + 24576 + 3072 + 4096 <= hT0 + 32896 + 512
        vB = self.sb("vbufB", [128, 48, 256], BF16, self.R0 + 16384)
        vbuf = [vA, vB]
        r_vA = Res("vbufA"); r_pT = [Res("pT%d" % i) for i in range(3)]; r_rden = [Res("rden0"), Res("rden1")]
        self.realias(self.mkv_tmp, [r_vA] + r_pT + r_rden)
        r_vAp = [Res("vA_nat"), Res("vA_d4"), Res("vA_d16")]
        r_vBp = [Res("vB_nat"), Res("vB_d4"), Res("vB_d16")]
        self.realias([r_vA], r_vAp)
        self.realias([self.wres[1], self.wres[2]], r_vBp)
        r_vbuf = [r_vAp, r_vBp]
        l_v = [[S.lane("vbufA%d" % i) for i in range(3)], [S.lane("vbufB%d" % i) for i in range(3)]]
        scale = float(128 ** -0.5)
        ones = self.cb("ones"); validp = self.cb("validp"); valid16 = self.cb("valid16")
        mu_ml = self.cb("mu_ml"); m16 = self.cb("m16")
        mask4 = mu_ml.unsqueeze(1).to_broadcast([128, 2, 256])
        all_vs = self.r_vs

        def load_v(hp):
            b = hp % 2
            cols = slice(hp * 256, (hp + 1) * 256)
            vsrc = self.vs_d[:, cols]
            wr = r_vbuf[b]
            S.dma("sp", vbuf[b][:, 0:16, :], vsrc.rearrange("(b p) c -> p b c", p=128), reads=all_vs, writes=[wr[0]], lane=l_v[b][0])
            S.dma("sp", vbuf[b][:, 16:32, :].rearrange("p (r b) c -> p r b c", r=4),
                  vsrc.rearrange("(b l r) c -> l r b c", l=128, r=4), reads=all_vs, writes=[wr[1]], lane=l_v[b][1])
            S.dma("sp", vbuf[b][:, 32:48, :], vsrc.rearrange("(l r) c -> l r c", r=16), reads=all_vs, writes=[wr[2]], lane=l_v[b][2])

        load_v(0)
        load_v(1)
        self.load_piece(0, self.w_out_l[0])
        batches = []
        for hp in range(4):
            for hh in range(2):
                for hf in range(2):
                    unit = (hp * 2 + hh) * 2 + hf
                    h = hp * 2 + hh
                    hc = slice(hh * 128, (hh + 1) * 128)
                    vb = vbuf[hp % 2]; r_vb = r_vbuf[hp % 2]
                    ob = 4 + (unit % 2); db = 6 + (unit % 2)
                    q0 = 512 * hf
                    ubatches = []
                    for ip in range(2):
                        smm = []; pvs = []
                        for ii in range(2):
                            i = 4 * hf + 2 * ip + ii
                            for j, kb in enumerate((7 + i, 8 + i)):
                                col = (ii * 2 + j) * 128
                                smm.append((slice(col, col + 128), self.kT[:, h, kb * 128:(kb + 1) * 128], self.qT[:, h, i * 128:(i + 1) * 128],
                                            [self.r_kT[kb], self.r_qT[i]]))
                                pvs.append((vb[:, kb, hc], validp if kb < 8 else ones, slice(col, col + 128),
                                            slice((i - 4 * hf) * 128, (i - 4 * hf + 1) * 128)))
                        ubatches.append((smm, pvs, mask4, 256, 0))
                    qb = 2 + hf
                    for rp in range(2):
                        smm = []; pvs = []
                        for ri in range(2):
                            r4 = rp * 2 + ri
                            for j, blk in enumerate((qb - 1, qb)):
                                col = (ri * 2 + j) * 128
                                smm.append((slice(col, col + 128), self.kT[:, h, 512 * blk + r4:512 * blk + 512:4], self.qT[:, h, q0 + r4:q0 + 512:4],
                                            self.r_kT[4 * blk:4 * blk + 4] + self.r_qT[4 * hf:4 * hf + 4]))
                                pvs.append((vb[:, 16 + r4 * 4 + blk, hc], validp if blk < 2 else ones, slice(col, col + 128), slice(r4, 512, 4)))
                        ubatches.append((smm, pvs, mask4, 256, 1))
                    smm = []; pvs = []
                    for r16 in range(16):
                        smm.append((slice(r16 * 32, (r16 + 1) * 32), self.kT[:, h, r16:2048:16], self.qT[:, h, q0 + r16:q0 + 512:16],
                                    self.r_kT + self.r_qT[4 * hf:4 * hf + 4]))
                        pvs.append((vb[:, 32 + r16, hc], valid16, slice(r16 * 32, (r16 + 1) * 32), slice(r16, 512, 16)))
                    m16b = m16[:, hf * 32:(hf + 1) * 32].unsqueeze(1).to_broadcast([128, 16, 32])
                    ubatches.append((smm, pvs, m16b, 32, 2))
                    for bi, ub in enumerate(ubatches):
                        batches.append(dict(unit=unit, h=h, q0=q0, ob=ob, db=db, r_vb=[r_vb[ub[4]]], first=(bi == 0), last=(bi == len(ubatches) - 1),
                                            hp=hp, smm=ub[0], pvs=ub[1], mask=ub[2], shape3=ub[3]))

        def emit_S(bi):
            B = batches[bi]
            sb_ = bi % 4
            for (cols, lhsT, rhs, rr) in B["smm"]:
                S.op("pe", "matmul", dict(out=self.banks[sb_][:, cols], lhsT=lhsT, rhs=rhs, start=True, stop=True, skip_group_check=True),
                     reads=rr, writes=[self.bres[sb_]])
            p = pT[bi % 3]; r_p = r_pT[bi % 3]
            S.op("act", "activation", dict(out=p[:, :], in_=self.banks[sb_][:, :], func=AF.Exp, scale=scale), reads=[self.bres[sb_]], writes=[r_p])
            pv_ = p[:, :].rearrange("p (a b) -> p a b", b=B["shape3"])
            S.op("pool", "tensor_tensor", dict(out=pv_, in0=pv_, in1=B["mask"], op=ALU.mult), reads=[r_p, self.r_const], writes=[r_p])

        def emit_PV(bi):
            B = batches[bi]
            p = pT[bi % 3]; r_p = r_pT[bi % 3]
            ob, db = B["ob"], B["db"]
            O = self.banks[ob]; DEN = self.banks[db]
            for k, (vl, dl, pcols, ocols) in enumerate(B["pvs"]):
                st = B["first"] and k == 0
                S.op("pe", "matmul", dict(out=O[:, ocols], lhsT=vl, rhs=p[:, pcols], start=st, stop=False, skip_group_check=True),
                     reads=[r_p] + B["r_vb"], writes=[self.bres[ob]])
                S.op("pe", "matmul", dict(out=DEN[:, ocols], lhsT=dl, rhs=p[:, pcols], start=st, stop=False, skip_group_check=True),
                     reads=[r_p, self.r_const], writes=[self.bres[db]])
            if B["last"]:
                unit = B["unit"]
                rd = rden[unit % 2]; r_rd = r_rden[unit % 2]
                S.op("dve", "reciprocal", dict(out=rd[:, :], in_=DEN[:, :]), reads=[self.bres[db]], writes=[r_rd])
                S.op("dve", "tensor_tensor", dict(out=self.mixed[:, 8 + B["h"], B["q0"]:B["q0"] + 512], in0=O[:, :], in1=rd[:, :], op=ALU.mult),
                     reads=[self.bres[ob], r_rd], writes=[self.r_mixed[8 + B["h"]]])
                if unit % 4 == 3 and B["hp"] + 2 < 4:
                    load_v(B["hp"] + 2)

        nbt = len(batches)
        emit_S(0)
        for bi in range(nbt):
            if bi + 1 < nbt:
                emit_S(bi + 1)
            emit_PV(bi)
        self.realias(r_vBp, [self.wres[1], self.wres[2]])
        self.att_tmp = [r_vA] + r_vAp + r_pT + r_rden


    def sample_attn(self, s_, W, nh, sets, qrow, r_qrow, selfkv, dest, r_dest, sc):
        self.sample_p1(s_, W, nh, sets, qrow, r_qrow, selfkv, sc, load_v=True)
        self.sample_p2(s_, W, nh, len(sets), selfkv, dest, r_dest, sc)

    def sample_p1(self, s_, W, nh, sets, qrow, r_qrow, selfkv, sc, load_v=True, qb=(0, 1)):
        S = self.S
        scale = float(128 ** -0.5)
        Kc, Vc, r_K, r_V, l_K, l_V = sc["Kc"], sc["Vc"], sc["r_K"], sc["r_V"], sc["l_K"], sc["l_V"]
        qbc, tmp, scr, e = sc["qbc"], sc["tmp"], sc["scr"], sc["e"]
        r_qbc, r_tmp, r_scr, r_e = sc["r_qbc"], sc["r_tmp"], sc["r_scr"], sc["r_e"]
        ns = len(sets)
        for i, (ks, vs) in enumerate(sets):
            S.dma("pool", Kc[i][:, 0:W], ks, writes=[r_K[i]], lane=l_K[i])
            if load_v:
                S.dma("pool", Vc[i][:, 0:W], vs, writes=[r_V[i]], lane=l_V[i])
        nhalf = W // 512
        ohb = self.cb("ohb")
        for hf in range(nhalf):
            S.op("pe", "matmul", dict(out=self.banks[qb[hf]][:, :], lhsT=ohb[0:4, s_ * 128:(s_ + 1) * 128], rhs=qrow[0:4, hf * 512:(hf + 1) * 512],
                                      start=True, stop=True), reads=[r_qrow, self.r_const], writes=[self.bres[qb[hf]]])
            S.op("act", "activation", dict(out=qbc[:, hf * 512:(hf + 1) * 512], in_=self.banks[qb[hf]][:, :], func=AF.Copy),
                 reads=[self.bres[qb[hf]]], writes=[r_qbc])
        for i in range(ns):
            S.op("dve", "tensor_tensor", dict(out=tmp[:, 0:W], in0=Kc[i][:, 0:W], in1=qbc[:, 0:W], op=ALU.mult), reads=[r_K[i], r_qbc], writes=[r_tmp])
            S.op("dve", "tensor_reduce", dict(out=scr[:, i * nh:(i + 1) * nh], in_=tmp[:, 0:W].rearrange("p (h d) -> p h d", d=128),
                                              axis=AX.X, op=ALU.add), reads=[r_tmp], writes=[r_scr])
        S.op("act", "activation", dict(out=e[:, 0:ns * nh], in_=scr[:, 0:ns * nh], func=AF.Exp, scale=scale), reads=[r_scr], writes=[r_e])
        if selfkv is not None:
            e4f, r_e4f, e4s, r_e4s, vrow, r_vrow = selfkv
            S.op("dve", "tensor_scalar", dict(out=e4s[0:4, 0:nh], in0=e4f[0:4, 0:nh], scalar1=self.cf("oh4")[0:4, s_:s_ + 1], scalar2=None, op0=ALU.mult),
                 reads=[r_e4f, self.r_const], writes=[r_e4s])

    def sample_p2(self, s_, W, nh, ns, selfkv, dest, r_dest, sc):
        S = self.S
        Vc, r_V = sc["Vc"], sc["r_V"]
        e, po, od, odn, rd = sc["e"], sc["po"], sc["od"], sc["odn"], sc["rd"]
        r_e, r_po, r_od = sc["r_e"], sc["r_po"], sc["r_od"]
        nhalf = W // 512
        ones = self.cb("ones")
        if selfkv is not None:
            e4f, r_e4f, e4s, r_e4s, vrow, r_vrow = selfkv
        for hf in range(nhalf):
            bk = 2 + hf
            for i in range(ns):
                S.op("pe", "matmul", dict(out=self.banks[bk][0:nh, :], lhsT=e[:, i * nh:(i + 1) * nh], rhs=Vc[i][:, hf * 512:(hf + 1) * 512],
                                          start=(i == 0), stop=(i == ns - 1 and selfkv is None)),
                     reads=[r_e, r_V[i]], writes=[self.bres[bk]])
            if selfkv is not None:
                S.op("pe", "matmul", dict(out=self.banks[bk][0:nh, :], lhsT=e4s[0:4, 0:nh], rhs=vrow[0:4, hf * 512:(hf + 1) * 512], start=False, stop=True),
                     reads=[r_e4s, r_vrow], writes=[self.bres[bk]])
        for i in range(ns):
            S.op("pe", "matmul", dict(out=self.banks[4][0:nh, 0:1], lhsT=e[:, i * nh:(i + 1) * nh], rhs=ones[:, 0:1],
                                      start=(i == 0), stop=(i == ns - 1 and selfkv is None)), reads=[r_e, self.r_const], writes=[self.bres[4]])
        if selfkv is not None:
            S.op("pe", "matmul", dict(out=self.banks[4][0:nh, 0:1], lhsT=e4s[0:4, 0:nh], rhs=ones[0:4, 0:1], start=False, stop=True),
                 reads=[r_e4s, self.r_const], writes=[self.bres[4]])
        bmask = self.cb("bmask")
        for hf in range(nhalf):
            S.op("dve", "tensor_tensor", dict(out=po[0:nh, hf * 512:(hf + 1) * 512], in0=self.banks[2 + hf][0:nh, :],
                                              in1=bmask[0:nh, hf * 512:(hf + 1) * 512], op=ALU.mult),
                 reads=[self.bres[2 + hf], self.r_const], writes=[r_po])
        S.op("dve", "tensor_reduce", dict(out=od[0:nh, 0:128], in_=po[0:nh, 0:W].rearrange("p (h d) -> p d h", d=128), axis=AX.X, op=ALU.add),
             reads=[r_po], writes=[r_od])
        S.op("dve", "reciprocal", dict(out=rd[0:nh, 0:1], in_=self.banks[4][0:nh, 0:1]), reads=[self.bres[4]], writes=[r_od])
        S.op("dve", "tensor_scalar", dict(out=odn[0:nh, 0:128], in0=od[0:nh, 0:128], scalar1=rd[0:nh, 0:1], scalar2=None, op0=ALU.mult),
             reads=[r_od], writes=[r_od])
        tb = self.bank_bf(5)
        S.op("pe", "transpose", dict(out=tb[:, 0:nh], in_=odn[0:nh, 0:128], identity=self.cb("ident")[0:nh, 0:nh]),
             reads=[r_od, self.r_const], writes=[self.bres[5]])
        S.op("act", "activation", dict(out=dest, in_=tb[:, 0:nh], func=AF.Copy), reads=[self.bres[5]], writes=r_dest)

    def sample_scratch(self, o, W, ns, tag, ext=None):
        S = self.S
        sc = {}
        sc["Kc"] = [self.sb("Kc%s%d" % (tag, i), [128, W], BF16, o + i * 2 * W) for i in range(ns)]; o += ns * 2 * W
        sc["Vc"] = [self.sb("Vc%s%d" % (tag, i), [128, W], BF16, o + i * 2 * W) for i in range(ns)]; o += ns * 2 * W
        sc["qbc"] = self.sb("qbc" + tag, [128, W], BF16, o); o += 2 * W
        if ext is None:
            sc["tmp"] = self.sb("tmp" + tag, [128, W], F32, o); o += 4 * W
            sc["po"] = self.sb("po" + tag, [8, W], F32, o); o += 4 * W
        else:
            sc["tmp"], sc["po"] = ext[0], ext[1]
        sc["scr"] = self.sb("scr" + tag, [128, 32], F32, o); o += 128
        sc["e"] = self.sb("e" + tag, [128, 32], BF16, o); o += 64
        sc["od"] = self.sb("od" + tag, [8, 128], F32, o); o += 512
        sc["odn"] = self.sb("odn" + tag, [8, 128], BF16, o); o += 256
        sc["rd"] = self.sb("rd" + tag, [8, 8], F32, o); o += 32
        sc["r_K"] = [Res("Kc%d" % i) for i in range(ns)]; sc["r_V"] = [Res("Vc%d" % i) for i in range(ns)]
        sc["l_K"] = [S.lane("Kc%s%d" % (tag, i)) for i in range(ns)]; sc["l_V"] = [S.lane("Vc%s%d" % (tag, i)) for i in range(ns)]
        for n in ("qbc", "tmp", "scr", "e", "po", "od"):
            sc["r_" + n] = Res(n + tag)
        if ext is not None:
            sc["r_tmp"], sc["r_po"] = ext[2], ext[3]
        sc["allres"] = sc["r_K"] + sc["r_V"] + [sc["r_" + n] for n in ("qbc", "scr", "e", "od")] + ([sc["r_tmp"], sc["r_po"]] if ext is None else [])
        return sc, o

    def phase_satt(self):
        S = self.S
        kT0 = SBUF_END - TOP_GUARD - 32768
        sc0, o = self.sample_scratch(kT0, 1024, 3, "d")
        sc1 = dict(sc0)
        sc1["qbc"] = self.sb("qbcd1", [128, 1024], BF16, o); o += 2048
        sc1["scr"] = self.sb("scrd1", [128, 32], F32, o); o += 128
        sc1["e"] = self.sb("ed1", [128, 32], BF16, o); o += 64
        sc1["od"] = self.sb("odd1", [8, 128], F32, o); o += 512
        sc1["odn"] = self.sb("odnd1", [8, 128], BF16, o); o += 256
        sc1["rd"] = self.sb("rdd1", [8, 8], F32, o); o += 32
        for nme in ("qbc", "scr", "e", "od"):
            sc1["r_" + nme] = Res(nme + "d1")
        e4s1 = self.sb("e4s1", [4, 8], BF16, o); o += 32
        tmp4 = self.sb("tmp4", [4, 1024], F32, o); o += 4096
        ssf = self.sb("ssf", [4, 8], F32, o); o += 32
        e4f = self.sb("e4f", [4, 8], F32, o); o += 32
        e4s0 = self.sb("e4s", [4, 8], BF16, o); o += 32
        assert o <= SBUF_END - TOP_GUARD, o
        r_t4 = Res("tmp4"); r_e4f = Res("e4f"); r_e4s = [Res("e4s0"), Res("e4s1")]
        extra = [sc1["r_" + nme] for nme in ("qbc", "scr", "e", "od")]
        self.realias(self.r_kT, sc0["allres"] + extra + [r_t4, r_e4f] + r_e4s)
        scale = float(128 ** -0.5)
        S.op("dve", "tensor_tensor", dict(out=tmp4[:, :], in0=self.qsb[:, :], in1=self.ksb[:, :], op=ALU.mult),
             reads=[self.r_qsb, self.r_ksb], writes=[r_t4])
        S.op("dve", "tensor_reduce", dict(out=ssf[:, :], in_=tmp4[:, :].rearrange("p (h d) -> p h d", d=128), axis=AX.X, op=ALU.add),
             reads=[r_t4], writes=[r_t4])
        S.op("act", "activation", dict(out=e4f[:, :], in_=ssf[:, :], func=AF.Exp, scale=scale, bias=self.cf("ln3")[0:4, :]),
             reads=[r_t4, self.r_const], writes=[r_e4f])
        scs = [sc0, sc1]
        e4ss = [e4s0, e4s1]

        def sets_of(s_):
            return [(self.cache_k[s_, 1920:2048, :], self.cache_v[s_, 1920:2048, :]),
                    (self.cache_k[s_, 1536:2048:4, :], self.cache_v[s_, 1536:2048:4, :]),
                    (self.cache_k[s_, 0:2048:16, :], self.cache_v[s_, 0:2048:16, :])]

        def selfkv(s_):
            return (e4f, r_e4f, e4ss[s_ % 2], r_e4s[s_ % 2], self.vsb, self.r_vsb)

        def load_vsets(s_):
            sc = scs[s_ % 2]
            for i, (ks, vs) in enumerate(sets_of(s_)):
                S.dma("pool", sc["Vc"][i][:, 0:1024], vs, writes=[sc["r_V"][i]], lane=sc["l_V"][i])

        def P1(s_):
            self.sample_p1(s_, 1024, 8, sets_of(s_), self.qsb, self.r_qsb, selfkv(s_), scs[s_ % 2], load_v=False,
                           qb=(0, 1) if s_ % 2 == 0 else (6, 7))

        def P2(s_):
            self.sample_p2(s_, 1024, 8, 3, selfkv(s_), self.mixed[:, 8:16, 1024 + s_], self.r_mixed[8:16], scs[s_ % 2])

        load_vsets(0)
        P1(0)
        for s_ in range(4):
            if s_ + 1 < 4:
                P1(s_ + 1)
            P2(s_)
            if s_ + 1 < 4:
                load_vsets(s_ + 1)
        self.satt_tmp = sc0["allres"] + extra + [r_t4, r_e4f] + r_e4s

    def rstd_chain(self, rows, src_cols, r_src, nred):
        S = self.S
        n = self.nepi
        self.nepi += 1
        sm = self.smalls
        base = 64 + (n % 16) * 2
        ssc = sm[:, base:base + 1]
        rs = sm[:, base + 1:base + 2]
        r_s = self.r_small[16 + n % 16]
        if nred > 1:
            S.op("dve", "tensor_reduce", dict(out=ssc[:rows, :], in_=src_cols, axis=AX.X, op=ALU.add), reads=[r_src], writes=[r_s])
            src, rr = ssc[:rows, :], [r_s]
        else:
            src, rr = src_cols, [r_src]
        S.op("pool", "tensor_scalar", dict(out=rs[:rows, :], in0=src, scalar1=float(1.0 / D), scalar2=float(EPS), op0=ALU.mult, op1=ALU.add), reads=rr, writes=[r_s])
        S.op("pool", "tensor_tensor", dict(out=rs[:rows, :], in0=rs[:rows, :], in1=self.cf("nhalf")[:rows, :], op=ALU.pow), reads=[r_s, self.r_const], writes=[r_s])
        return rs, r_s

    def prenorm_a2(self, xt, r_x, hb, r_hb, rows, scale_eng="act"):
        S = self.S
        n = self.nepi
        col = 100 + (n % 8)
        ss = self.smalls[:, col:col + 1]
        r_ss = self.r_small[8 + n % 8]
        S.op("act", "activation", dict(out=self.junk[:rows, :], in_=xt, func=AF.Square, accum_out=ss[:rows, :]), reads=[r_x], writes=[self.r_junk, r_ss])
        rs, r_s = self.rstd_chain(rows, ss[:rows, :], r_ss, 1)
        if scale_eng == "act":
            S.op("act", "activation", dict(out=hb[:rows, :], in_=xt, func=AF.Copy, scale=rs[:rows, 0:1]), reads=[r_x, r_s], writes=[r_hb])
        else:
            S.op("dve", "tensor_scalar", dict(out=hb[:rows, :], in0=xt, scalar1=rs[:rows, 0:1], scalar2=None, op0=ALU.mult), reads=[r_x, r_s], writes=[r_hb])

    def load_gb(self, idx):
        self.S.dma("sp", self.gb[:, :], self.grows[idx:idx + 1, :].partition_broadcast(128), writes=[self.r_gb], lane=self.l_gb)

    def phase_wo(self):
        S = self.S
        X1_0 = SBUF_END - TOP_GUARD - 73728
        self.x1 = self.sb("x1", [128, 9, D], F32, X1_0)
        self.r_x1 = [Res("x1_%d" % i) for i in range(9)]
        dead = self.att_tmp + self.satt_tmp + self.r_kT + [self.r_ksb, self.r_vsb, self.r_qsb, self.r_usf] + self.r_qT
        self.realias(dead, self.r_x1)
        self.nepi = 0
        self.l_gb = S.lane("gb")
        self.realias(self.gb_alias, [self.r_gb])
        self.load_gb(0)
        sm = self.smalls
        tiles = [(t, 128) for t in range(8)] + [(8, 4)]
        slots = [0, 1, 0, 1]
        self.wo_last_mm = {}
        for cb in range(4):
            slot = slots[cb]
            if cb > 0:
                self.load_piece(slot, self.w_out_l[cb])
            for t, rows in tiles:
                nst = self.nst
                bank = 2 + (nst % 4)
                ps = self.banks[bank]
                for c in range(16):
                    mm_last = S.op("pe", "matmul", dict(out=ps[:rows, :], lhsT=self.mixed[:, c, t * 128:t * 128 + rows], rhs=self.WR[slot][:, c * 512:(c + 1) * 512],
                                                        start=(c == 0), stop=(c == 15)), reads=[self.r_mixed[c], self.wres[slot]], writes=[self.bres[bank]])
                ydst = self.x1[:rows, t, cb * 512:(cb + 1) * 512]
                S.op("act", "activation", dict(out=ydst, in_=ps[:rows, :], func=AF.Copy), reads=[self.bres[bank]], writes=[self.r_x1[t]])
                S.op("act", "activation", dict(out=self.junkw[:rows, 0:512], in_=ps[:rows, :], func=AF.Square, accum_out=sm[:rows, 16 + t * 4 + cb:16 + t * 4 + cb + 1]),
                     reads=[self.bres[bank]], writes=[self.r_x1[t], self.r_junkw])
                self.nst += 1
                if cb == 3:
                    self.wo_last_mm[t] = mm_last
                    if t == 0:
                        self.epi1_setup()
                    else:
                        self.epi1_step(t - 1)

    def alloc_epi(self):
        S = self.S
        X0 = self.qT_end - 16384
        self.xin = [self.sb("exin%d" % i, [128, D], F32, X0 + i * 8192) for i in range(2)]
        self.r_xin = [Res("exin0"), Res("exin1")]
        self.l_xin = [S.lane("exin0"), S.lane("exin1")]
        self.realias(self.r_qT, self.r_xin)
        s2 = self.R0 + 2 * 16384
        self.junk = self.sb("ejunk", [128, D], BF16, s2)
        self.hb1 = self.sb("ehb", [128, D], BF16, self.R0 + 16384 + 24576 + 4096)
        self.hb2 = self.sb("ehb2", [128, D], BF16, self.R0 + 16384 + 24576 - 4096)
        self.ehb = [self.hb1, self.hb2]
        self.r_junk = Res("ejunk"); self.r_hb1 = Res("ehb"); self.r_hb2 = Res("ehb2")
        self.r_ehb = [self.r_hb1, self.r_hb2]
        self.junkw = self.junk
        self.r_junkw = self.r_junk

    def epi1_setup(self):
        S = self.S
        tiles = [(t, 128) for t in range(8)] + [(8, 4)]
        self.e1_tiles = tiles
        sm = self.smalls
        n = len(tiles)

        def issue(ti):
            t, rows = tiles[ti]
            src = self.xs if t == 8 else self.xm[t * 128:(t + 1) * 128, :]
            S.dma("sp", self.xin[ti % 2][:rows, :], src, writes=[self.r_xin[ti % 2]], lane=self.l_xin[ti % 2])
        issue(0); issue(1)
        self.h2T = self.sb("h2T", [128, 16, NT], BF16, self.R0 + 3 * 16384)
        self.r_h2T = [Res("h2T%d" % i) for i in range(9)]
        self.load_piece(0, self.w_xq_l[0])
        rs_of = {}

        def stageA1a(ti):
            t, rows = tiles[ti]
            rs_of[ti] = self.rstd_chain(rows, sm[:rows, 16 + t * 4:16 + t * 4 + 4], self.r_x1[t], 4)

        def stageA1b(ti):
            t, rows = tiles[ti]
            xt = self.x1[:rows, t, :]
            rs, r_s = rs_of[ti]
            S.op("dve", "scalar_tensor_tensor", dict(out=xt, in0=xt, scalar=rs[:rows, 0:1], in1=self.gb[:rows, :], op0=ALU.mult, op1=ALU.mult),
                 reads=[self.r_x1[t], r_s, self.r_gb], writes=[self.r_x1[t]])
            xi = self.xin[ti % 2]
            S.op("dve", "tensor_tensor", dict(out=xt[:, 0:512], in0=xt[:, 0:512], in1=xi[:rows, 0:512], op=ALU.add),
                 reads=[self.r_x1[t], self.r_xin[ti % 2]], writes=[self.r_x1[t]])
            S.op("pool", "tensor_tensor", dict(out=xt[:, 512:D], in0=xt[:, 512:D], in1=xi[:rows, 512:D], op=ALU.add),
                 reads=[self.r_x1[t], self.r_xin[ti % 2]], writes=[self.r_x1[t]])

        def stageA2(ti):
            t, rows = tiles[ti]
            self.prenorm_a2(self.x1[:rows, t, :], self.r_x1[t], self.ehb[ti % 2], self.r_ehb[ti % 2], rows)

        def stageB(ti):
            t, rows = tiles[ti]
            self.r_h2T[t].last_write = self.wo_last_mm[t]
            self.prenorm_b(self.ehb[ti % 2], self.r_ehb[ti % 2], rows, 1, self.h2T[:, :, t * 128:t * 128 + rows], self.r_h2T[t], (0, 1))

        def step(it):
            if it + 1 < n:
                stageA1a(it + 1)
            if it < n:
                stageA1b(it)
            if it + 2 < n:
                issue(it + 2)
            if 0 <= it - 1 < n:
                stageA2(it - 1)
            if 0 <= it - 2 < n:
                stageB(it - 2)
        self._e1_step = step
        stageA1a(0)

    def epi1_step(self, it):
        self._e1_step(it)

    def phase_epi1(self):
        n = len(self.e1_tiles)
        for it in range(n - 1, n + 2):
            self._e1_step(it)
        self.load_piece(1, self.w_xo_l[0])

    def phase_xa(self):
        S = self.S
        scale = float(128 ** -0.5)
        X0 = self.qT_end - 16384
        self.q2T = self.sb("q2T", [128, 4, NT], BF16, X0)
        self.o2T = self.sb("o2T", [128, 4, NT], BF16, X0 + 8224)
        o = X0 + 16448
        pT2 = [self.sb("pT2_%d" % i, [128, 2, 384], BF16, o + i * 1536) for i in range(2)]; o += 3072
        rd2 = [self.sb("rd2_%d" % i, [128, 384], F32, o + i * 1536) for i in range(2)]; o += 3072
        assert o <= self.PERS0
        self.q2sb = self.sb("q2sb", [4, 512], BF16, self.PERS0)
        self.r_q2T = [Res("q2T%d" % i) for i in range(4)]
        self.r_o2T = [Res("o2T%d" % i) for i in range(4)]
        r_pT2 = [Res("pT2_0"), Res("pT2_1")]; r_rd2 = [Res("rd2_0"), Res("rd2_1")]
        self.r_q2sb = Res("q2sb")
        self.realias(self.r_xin + [self.r_csq], self.r_q2T + self.r_o2T + r_pT2 + r_rd2 + [self.r_q2sb])
        self.xa_small = r_pT2 + r_rd2
        sq = 0
        for hh in range(4):
            for c in range(16):
                lw = self.WR[sq][:, c * 512 + hh * 128:c * 512 + (hh + 1) * 128]
                for gi, (lo, hi) in enumerate(GROUPS):
                    S.op("pe", "matmul", dict(out=self.banks[2 + gi][:, 0:hi - lo], lhsT=lw, rhs=self.h2T[:, c, lo:hi], start=(c == 0), stop=(c == 15)),
                         reads=self.r_h2T[lo // 128:(hi + 127) // 128] + [self.wres[sq]], writes=[self.bres[2 + gi]])
            for gi, (lo, hi) in enumerate(GROUPS):
                S.op("act", "activation", dict(out=self.q2T[:, hh, lo:hi], in_=self.banks[2 + gi][:, 0:hi - lo], func=AF.Copy),
                     reads=[self.bres[2 + gi]], writes=[self.r_q2T[hh]])
        for c in range(16):
            S.op("pe", "matmul", dict(out=self.banks[5][0:4, :], lhsT=self.h2T[:, c, 1024:1028], rhs=self.WR[sq][:, c * 512:(c + 1) * 512],
                                      start=(c == 0), stop=(c == 15)), reads=[self.r_h2T[8], self.wres[sq]], writes=[self.bres[5]])
        S.op("act", "activation", dict(out=self.q2sb[:, :], in_=self.banks[5][0:4, :], func=AF.Copy), reads=[self.bres[5]], writes=[self.r_q2sb])
        self.load_piece(0, self.w_ff1_l[0])
        ones = self.cb("ones")
        iters = [(hh, lo, hi) for hh in range(4) for (lo, hi) in [(0, 384), (384, 768), (768, 1024)]]

        def xS(nb):
            hh, lo, hi = iters[nb]
            n = hi - lo
            p = pT2[nb % 2]; r_p = r_pT2[nb % 2]
            for kt in range(2):
                bk = (nb % 2) * 2 + kt
                S.op("pe", "matmul", dict(out=self.banks[bk][:, 0:n], lhsT=self.memKT[:, hh, kt * 128:(kt + 1) * 128], rhs=self.q2T[:, hh, lo:hi],
                                          start=True, stop=True), reads=[self.r_memKT, self.r_q2T[hh]], writes=[self.bres[bk]])
                S.op("act", "activation", dict(out=p[:, kt, 0:n], in_=self.banks[bk][:, 0:n], func=AF.Exp, scale=scale),
                     reads=[self.bres[bk]], writes=[r_p])

        def xPV(nb):
            hh, lo, hi = iters[nb]
            n = hi - lo
            p = pT2[nb % 2]; r_p = r_pT2[nb % 2]
            ob = 4 + (nb % 2); db = 6 + (nb % 2)
            for kt in range(2):
                S.op("pe", "matmul", dict(out=self.banks[ob][:, 0:n], lhsT=self.memV[:, kt, hh * 128:(hh + 1) * 128], rhs=p[:, kt, 0:n],
                                          start=(kt == 0), stop=(kt == 1)), reads=[self.r_memV, r_p], writes=[self.bres[ob]])
            for kt in range(2):
                S.op("pe", "matmul", dict(out=self.banks[db][:, 0:n], lhsT=ones, rhs=p[:, kt, 0:n], start=(kt == 0), stop=(kt == 1)),
                     reads=[self.r_const, r_p], writes=[self.bres[db]])
            rd = rd2[nb % 2]; r_rd = r_rd2[nb % 2]
            S.op("dve", "reciprocal", dict(out=rd[:, 0:n], in_=self.banks[db][:, 0:n]), reads=[self.bres[db]], writes=[r_rd])
            S.op("dve", "tensor_tensor", dict(out=self.o2T[:, hh, lo:hi], in0=self.banks[ob][:, 0:n], in1=rd[:, 0:n], op=ALU.mult),
                 reads=[self.bres[ob], r_rd], writes=[self.r_o2T[hh]])

        xS(0)
        for nb in range(len(iters)):
            if nb + 1 < len(iters):
                xS(nb + 1)
            xPV(nb)
        GB0 = SBUF_BASE + 6656
        sc, o = self.sample_scratch(GB0, 512, 2, "m", ext=(self.stage[0], self.stage[1], self.r_stage[0], self.r_stage[1]))
        assert o <= GB0 + 8192 + 128, o
        self.realias([self.r_gb], sc["allres"])
        for s_ in range(4):
            sets = [(self.cmem_k[s_, 0:128, :], self.cmem_v[s_, 0:128, :]), (self.cmem_k[s_, 128:256, :], self.cmem_v[s_, 128:256, :])]
            self.sample_attn(s_, 512, 4, sets, self.q2sb, self.r_q2sb, None, self.o2T[:, 0:4, 1024 + s_], self.r_o2T, sc)
        self.xa_tmp = sc["allres"]

    def phase_wxo(self):
        S = self.S
        self.r_gb2 = Res("gb2")
        self.realias(self.xa_tmp, [self.r_gb2])
        self.r_gb = self.r_gb2
        self.load_gb(1)
        self.h3T = self.h2T
        self.r_h3T = [Res("h3T%d" % i) for i in range(9)]
        self.realias(self.r_h2T, self.r_h3T)
        l_x2 = S.lane("x2spill")
        self.r_x2d = [Res("x2d%d" % i) for i in range(9)]
        sm = self.smalls
        tiles = [(t, 128) for t in range(8)] + [(8, 4)]
        n = len(tiles)
        so = 1
        ybuf = self.stage

        def bank_of(ti, cb):
            return 2 + (ti * 4 + cb) % 6

        def stageMM(ti):
            t, rows = tiles[ti]
            for cb in range(4):
                bank = bank_of(ti, cb)
                for c in range(4):
                    S.op("pe", "matmul", dict(out=self.banks[bank][:rows, :], lhsT=self.o2T[:, c, t * 128:t * 128 + rows],
                                              rhs=self.WR[so][:, c * 2048 + cb * 512:c * 2048 + (cb + 1) * 512], start=(c == 0), stop=(c == 3)),
                         reads=[self.r_o2T[c], self.wres[so]], writes=[self.bres[bank]])
                S.op("act", "activation", dict(out=self.junk[:rows, 0:512], in_=self.banks[bank][:rows, :], func=AF.Square,
                                               accum_out=sm[:rows, 16 + t * 4 + cb:16 + t * 4 + cb + 1]),
                     reads=[self.bres[bank]], writes=[self.r_junk, self.r_small[32 + cb]])

        rs_of = {}

        def stageA1a(ti):
            t, rows = tiles[ti]
            rs_of[ti] = self.rstd_chain(rows, sm[:rows, 16 + t * 4:16 + t * 4 + 4], self.r_small[35], 4)

        def stageA1b(ti):
            t, rows = tiles[ti]
            xt = self.x1[:rows, t, :]
            rs, r_s = rs_of[ti]
            for cb in range(4):
                bank = bank_of(ti, cb)
                S.op("dve", "scalar_tensor_tensor", dict(out=ybuf[cb][:rows, :], in0=self.banks[bank][:rows, :], scalar=rs[:rows, 0:1],
                                                       in1=self.gb[:rows, cb * 512:(cb + 1) * 512], op0=ALU.mult, op1=ALU.mult),
                     reads=[self.bres[bank], r_s, self.r_gb], writes=[self.r_stage[cb]])
            for cb in range(4):
                S.op("pool", "tensor_tensor", dict(out=xt[:, cb * 512:(cb + 1) * 512], in0=xt[:, cb * 512:(cb + 1) * 512], in1=ybuf[cb][:rows, :], op=ALU.add),
                     reads=[self.r_stage[cb], self.r_x1[t]], writes=[self.r_x1[t]])
            S.dma("sp", self.x2_d[t * 128:t * 128 + rows, :], xt, reads=[self.r_x1[t]], writes=[self.r_x2d[t]], lane=l_x2)

        def stageA2(ti):
            t, rows = tiles[ti]
            self.prenorm_a2(self.x1[:rows, t, :], self.r_x1[t], self.ehb[ti % 2], self.r_ehb[ti % 2], rows, scale_eng="dve")

        def stageB(ti):
            t, rows = tiles[ti]
            self.prenorm_b(self.ehb[ti % 2], self.r_ehb[ti % 2], rows, 2, self.h3T[:, :, t * 128:t * 128 + rows], self.r_h3T[t], (0, 1))

        stageMM(0)
        stageA1a(0)
        for it in range(n + 2):
            if it < n:
                stageA1b(it)
            if it + 1 < n:
                stageMM(it + 1)
                stageA1a(it + 1)
            if 0 <= it - 1 < n:
                stageA2(it - 1)
            if 0 <= it - 2 < n:
                stageB(it - 2)

    def phase_ffn(self):
        S = self.S
        acc = self.x1
        self.r_acc = [Res("acc%d" % i) for i in range(9)]
        self.realias(self.r_x1 + self.r_x2d, self.r_acc)
        X0 = self.qT_end - 16384
        hid = [self.sb("hid%d" % i, [128, 4, NT], BF16, X0 + i * 8224) for i in range(2)]
        rl = [self.sb("rl%d" % i, [128, 384], F32, X0 + 16448 + i * 1536) for i in range(2)]
        r_hid = [Res("hid0"), Res("hid1")]; r_rl = [Res("rl0"), Res("rl1")]
        self.realias(self.r_q2T + self.r_o2T + [self.r_q2sb] + self.r_xin + self.xa_small, r_hid + r_rl)
        r_slot2 = Res("slot2b")
        self.realias([self.r_junk, self.r_hb1, self.r_hb2, self.r_memKT, self.r_memV, self.wres[2]], [r_slot2])
        self.wres[2] = r_slot2
        seq = []
        for j in range(16):
            seq.append(("f1", j))
            if j >= 1:
                seq.append(("f2", j - 1))
        seq.append(("f2", 15))
        order = [0, 1, 2]
        tiles = [(t, 128) for t in range(8)] + [(8, 4)]
        loaded = {0: 0}
        nload = [1]

        def ensure_loaded(k):
            while nload[0] <= k and nload[0] < len(seq):
                kind, j = seq[nload[0]]
                slot = order[nload[0] % 3]
                self.load_piece(slot, (self.w_ff1_l if kind == "f1" else self.w_ff2_l)[j])
                loaded[nload[0]] = slot
                nload[0] += 1

        nrl = 0
        nfb = 0
        for k, (kind, j) in enumerate(seq):
            ensure_loaded(k + 1)
            slot = loaded[k]
            hb_ = hid[j % 2]; r_hb_ = r_hid[j % 2]
            if kind == "f1":
                for ft in range(4):
                    banks = [(nfb + gi) % 4 for gi in range(3)]
                    nfb += 3
                    for c in range(16):
                        lw = self.WR[slot][:, c * 512 + ft * 128:c * 512 + (ft + 1) * 128]
                        for gi, (lo, hi) in enumerate(GROUPS):
                            S.op("pe", "matmul", dict(out=self.banks[banks[gi]][:, 0:hi - lo], lhsT=lw, rhs=self.h3T[:, c, lo:hi], start=(c == 0), stop=(c == 15)),
                                 reads=self.r_h3T[lo // 128:(hi + 127) // 128] + [self.wres[slot]], writes=[self.bres[banks[gi]]])
                    for gi, (lo, hi) in enumerate(GROUPS):
                        n = hi - lo
                        r_ = rl[nrl % 2]; r_r = r_rl[nrl % 2]
                        S.op("act", "activation", dict(out=r_[:, 0:n], in_=self.banks[banks[gi]][:, 0:n], func=AF.Relu), reads=[self.bres[banks[gi]]], writes=[r_r])
                        S.op("pool", "tensor_tensor", dict(out=hb_[:, ft, lo:hi], in0=r_[:, 0:n], in1=r_[:, 0:n], op=ALU.mult), reads=[r_r], writes=[r_hb_])
                        nrl += 1
            else:
                for t, rows in tiles:
                    for cb in range(4):
                        bank = 4 + cb
                        for cc in range(4):
                            S.op("pe", "matmul", dict(out=self.banks[bank][:rows, :], lhsT=hb_[:, cc, t * 128:t * 128 + rows],
                                                      rhs=self.WR[slot][:, cc * 2048 + cb * 512:cc * 2048 + (cb + 1) * 512], start=(cc == 0), stop=(cc == 3)),
                                 reads=[r_hb_, self.wres[slot]], writes=[self.bres[bank]])
                        a = acc[:rows, t, cb * 512:(cb + 1) * 512]
                        if j == 0:
                            S.op("act", "activation", dict(out=a, in_=self.banks[bank][:rows, :], func=AF.Copy), reads=[self.bres[bank]], writes=[self.r_acc[t]])
                        else:
                            S.op("dve", "tensor_tensor", dict(out=a, in0=a, in1=self.banks[bank][:rows, :], op=ALU.add),
                                 reads=[self.bres[bank], self.r_acc[t]], writes=[self.r_acc[t]])
                    if j == 15:
                        ti = t
                        if ti == 0:
                            self.final_setup()
                        self.final_F1(ti)
                        if ti >= 1:
                            self.final_F2(ti - 1)
                        if ti == 8:
                            self.final_F2(8)
        self.ffn_tmp = r_hid + r_rl

    def final_setup(self):
        S = self.S
        self.r_gb3 = Res("gb3")
        self.realias([self.r_gb], [self.r_gb3])
        self.r_gb = self.r_gb3
        self.load_gb(2)
        H0 = self.R0 + 3 * 16384
        self.fxin = [self.sb("fxin%d" % i, [128, D], F32, H0 + i * 8192) for i in range(2)]
        self.r_fxin = [Res("fxin0"), Res("fxin1")]
        self.l_fxin = [S.lane("fxin0"), S.lane("fxin1")]
        self.fj = self.sb("fjunk", [128, D], BF16, H0 + 16384)
        self.r_fj = Res("fjunk")
        self.realias(self.r_h3T, self.r_fxin + [self.r_fj])
        self.l_out = [S.lane("yout0"), S.lane("yout1")]
        self.ftiles = [(t, 128) for t in range(8)] + [(8, 4)]
        self.rs_of = {}
        self.final_issue(0)
        self.final_issue(1)

    def final_issue(self, ti):
        t, rows = self.ftiles[ti]
        self.S.dma("sp", self.fxin[ti % 2][:rows, :], self.x2_d[t * 128:t * 128 + rows, :], reads=[self.r_x2d[t]], writes=[self.r_fxin[ti % 2]],
                   lane=self.l_fxin[ti % 2])

    def final_F1(self, ti):
        S = self.S
        t, rows = self.ftiles[ti]
        at = self.x1[:rows, t, :]
        col = 16 + t * 4
        sm = self.smalls
        S.op("act", "activation", dict(out=self.fj[:rows, :], in_=at, func=AF.Square, accum_out=sm[:rows, col:col + 1]),
             reads=[self.r_acc[t]], writes=[self.r_fj, self.r_small[36]])
        self.rs_of[ti] = self.rstd_chain(rows, sm[:rows, col:col + 1], self.r_small[36], 1)

    def final_F2(self, ti):
        S = self.S
        t, rows = self.ftiles[ti]
        at = self.x1[:rows, t, :]
        rs, r_s = self.rs_of[ti]
        S.op("dve", "scalar_tensor_tensor", dict(out=at, in0=at, scalar=rs[:rows, 0:1], in1=self.gb[:rows, :], op0=ALU.mult, op1=ALU.mult),
             reads=[self.r_acc[t], r_s, self.r_gb], writes=[self.r_acc[t]])
        S.op("pool", "tensor_tensor", dict(out=at, in0=at, in1=self.fxin[ti % 2][:rows, :], op=ALU.add),
             reads=[self.r_acc[t], self.r_fxin[ti % 2]], writes=[self.r_acc[t]])
        od = self.o_ys if t == 8 else self.o_y[t * 128:(t + 1) * 128, :]
        self.outs.append(S.dma("sp", od, at, reads=[self.r_acc[t]], lane=self.l_out[ti % 2]))
        if ti + 2 < 9:
            self.final_issue(ti + 2)


def _build(upto="all", debug=()):
    b = Builder(upto, debug)
    return b.build()


_NC_CACHE = {}


def kernel(**inputs):
    maps = _prep(inputs)
    upto = "all"
    if upto not in _NC_CACHE:
        _NC_CACHE[upto] = _build(upto)
    nc = _NC_CACHE[upto]
    res = run_bass_kernel_spmd(nc, maps, core_ids=list(range(NCORES)))
    return _assemble(res.results)


def _assemble(results):
    y = np.zeros((4, 2048, D), np.float32)
    ys = np.zeros((32, 1, D), np.float32)
    pool_p = np.zeros((1, 4, 15, 1024), np.float32)
    pool_s = np.zeros((1, 32, 15, 1024), np.float32)
    k_p = np.zeros((1, 4, 2048, 8, 128), np.float32)
    v_p = np.zeros((1, 4, 2048, 8, 128), np.float32)
    k_s = np.zeros((1, 32, 1, 8, 128), np.float32)
    v_s = np.zeros((1, 32, 1, 8, 128), np.float32)
    mk = np.zeros((1, 4, 256, 4, 128), np.float32)
    mv = np.zeros((1, 4, 256, 4, 128), np.float32)
    for core, r in enumerate(results):
        b, half = core // 2, core % 2
        sl = slice(half * 1024, (half + 1) * 1024)
        y[b, sl] = r["o_y"]
        ys[core * 4:(core + 1) * 4, 0] = r["o_ys"]
        if half == 1:
            pool_p[0, b] = r["o_pool"]
            mk[0, b] = r["o_mk"].reshape(256, 4, 128)
            mv[0, b] = r["o_mv"].reshape(256, 4, 128)
        pool_s[0, core * 4:(core + 1) * 4] = r["o_pools"]
        k_p[0, b, sl] = r["o_k"].reshape(1024, 8, 128)
        v_p[0, b, sl] = r["o_v"].reshape(1024, 8, 128)
        k_s[0, core * 4:(core + 1) * 4, 0] = r["o_ks"].reshape(4, 8, 128)
        v_s[0, core * 4:(core + 1) * 4, 0] = r["o_vs"].reshape(4, 8, 128)
    return (y, ys, pool_p, pool_s, k_p, v_p, k_s, v_s, mk, mv)
```
